# Optimizing a Trainium2 kernel written in Bass

```python
import jax, jax.numpy as jnp
from jax import lax
import numpy as np

D_MODEL = 1024
BATCH = 16
SEQ = 256
DEPTH = 2
DEC_BATCH = 4
DEC_SEQ = 1024
PAST_LEN = 512

F32 = jnp.float32
GRID_W = 64
N_MIXERS = 4
W_GRP = D_MODEL // N_MIXERS
N_HEADS = 4
HEAD_V = W_GRP // N_HEADS
HGRN_DK = HEAD_V
GLA_DK = HEAD_V // 2
GLA_RANK = 16
GLA_NORMALIZER = 16.0
RG_BLOCKS = N_HEADS
RG_BLOCK = W_GRP // RG_BLOCKS
RG_CONV = 4
RG_C = 8.0
SCONV_W = 3
D_FF = 4 * D_MODEL
CHUNK = 32
EPS = 1e-6
F_FLOOR = 1e-20
SPLIT_SIZES = (
    N_HEADS * HGRN_DK, N_HEADS * HEAD_V, N_HEADS * HGRN_DK, N_HEADS * HGRN_DK, W_GRP,
    N_HEADS * GLA_DK, N_HEADS * GLA_DK, W_GRP, W_GRP, GLA_RANK, GLA_RANK,
    W_GRP, W_GRP,
    W_GRP, W_GRP, W_GRP,
)
PROJ_W = sum(SPLIT_SIZES)

kernel_name = 'hybrid_parallel_heads_diffusion_step'


def rmsnorm(x, g):
    xf = x.astype(F32)
    y = xf * lax.rsqrt(jnp.mean(xf * xf, axis=-1, keepdims=True) + EPS)
    return (y * g.astype(F32)).astype(x.dtype)


def to_heads(t):
    b, l, _ = t.shape
    return t.reshape(b, l, N_HEADS, -1).transpose(0, 2, 1, 3)


def head_norm_gate(o, g, gain):
    b, h, l, dv = o.shape
    o = o.transpose(0, 2, 1, 3)
    o = o * lax.rsqrt(jnp.mean(o * o, axis=-1, keepdims=True) + EPS)
    o = o.reshape(b, l, h * dv) * gain.astype(F32)
    return (o * jax.nn.silu(g.astype(F32))).astype(g.dtype)


def chunk_gla(q, k, v, log_a, s0):
    b, h, l, dk = q.shape
    dv = v.shape[-1]
    n = l // CHUNK
    q = q.reshape(b, h, n, CHUNK, dk)
    k = k.reshape(b, h, n, CHUNK, dk)
    v = v.reshape(b, h, n, CHUNK, dv)
    cum = jnp.cumsum(log_a.reshape(b, h, n, CHUNK, dk), axis=3)
    last = cum[:, :, :, -1:, :]
    causal = jnp.tril(jnp.ones((CHUNK, CHUNK), dtype=bool))[:, :, None]
    diff = cum[:, :, :, :, None, :] - cum[:, :, :, None, :, :]
    decay = jnp.where(causal, jnp.exp(jnp.minimum(diff, 0.0)), 0.0)
    scores = jnp.einsum('bhntd,bhnsd,bhntsd->bhnts', q, k, decay)
    o_intra = jnp.einsum('bhnts,bhnse->bhnte', scores, v)
    ds = jnp.einsum('bhncd,bhnce->bhnde', k * jnp.exp(last - cum), v)
    a_chunk = jnp.exp(last[:, :, :, 0, :])

    def step(s, inp):
        a_n, ds_n = inp
        return a_n[..., None] * s + ds_n, s

    s_final, s_in = lax.scan(step, s0, (jnp.moveaxis(a_chunk, 2, 0), jnp.moveaxis(ds, 2, 0)))
    s_in = jnp.moveaxis(s_in, 0, 2)
    o_inter = jnp.einsum('bhntd,bhnde->bhnte', q * jnp.exp(cum), s_in)
    return (o_intra + o_inter).reshape(b, h, l, dv), s_final


def bidirectional_gla(q, v, ks, log_as, s0):
    outs, finals = [], []
    for d in range(2):
        qd, kd, vd, ad = q, ks[d], v, log_as[d]
        if d == 1:
            qd, kd, vd, ad = (jnp.flip(t, axis=2) for t in (qd, kd, vd, ad))
        o, sf = chunk_gla(qd, kd, vd, ad, s0[:, d].astype(F32))
        outs.append(jnp.flip(o, axis=2) if d == 1 else o)
        finals.append(sf)
    return outs[0] + outs[1], jnp.stack(finals, axis=1)


def hgrn2_mixer(q, i, f_fwd, f_bwd, g, lb, norm_g, s0):
    qh = to_heads(q.astype(F32))
    vh = to_heads(i.astype(F32))
    lb = lb.astype(F32)
    ks, las = [], []
    for d, fl in enumerate((f_fwd, f_bwd)):
        f = lb[d] + (1.0 - lb[d]) * jax.nn.sigmoid(fl.astype(F32))
        log_f = jnp.log(jnp.maximum(f, F_FLOOR))
        ks.append(to_heads(1.0 - f))
        las.append(to_heads(log_f))
    o, sf = bidirectional_gla(qh, vh, ks, las, s0)
    return head_norm_gate(o, g, norm_g), sf


def gla_mixer(q, k, v, g, a_fwd, a_bwd, wa2, ba2, norm_g, s0):
    qh = to_heads(q.astype(F32)) * (GLA_DK ** -0.5)
    kh = to_heads(k.astype(F32))
    vh = to_heads(v.astype(F32))
    las = [to_heads(jax.nn.log_sigmoid((a @ wa2[d] + ba2[d]).astype(F32)) / GLA_NORMALIZER)
           for d, a in enumerate((a_fwd, a_bwd))]
    o, sf = bidirectional_gla(qh, vh, [kh, kh], las, s0)
    return head_norm_gate(o, g, norm_g), sf


def causal_conv(u, w, bias):
    l = u.shape[1]
    kw = w.shape[0]
    up = jnp.pad(u, ((0, 0), (kw - 1, 0), (0, 0)))
    out = bias
    for j in range(kw):
        out = out + up[:, j:j + l] * w[j]
    return out


def block_diag(x, w, bias):
    b, l, _ = x.shape
    y = jnp.einsum('blnc,ncd->blnd', x.reshape(b, l, RG_BLOCKS, RG_BLOCK), w.astype(F32))
    return y.reshape(b, l, W_GRP) + bias.astype(F32)


def rglru_scan(x, r, ig, lam, h0):
    log_a = -RG_C * r * jax.nn.softplus(-lam)
    a = jnp.exp(log_a)
    bx = jnp.sqrt(-jnp.expm1(2.0 * log_a)) * (ig * x)
    bx = bx.at[:, 0].add(a[:, 0] * h0)

    def combine(e1, e2):
        a1, b1 = e1
        a2, b2 = e2
        return a1 * a2, a2 * b1 + b2

    _, h = lax.associative_scan(combine, (a, bx), axis=1)
    return h, h[:, -1]


def rglru_mixer(u, gate, conv_w, conv_b, w_r, b_r, w_i, b_i, lam, h0):
    u = u.astype(F32)
    outs, finals = [], []
    for d in range(2):
        ud = jnp.flip(u, axis=1) if d == 1 else u
        xc = causal_conv(ud, conv_w[d].astype(F32), conv_b[d].astype(F32))
        r = jax.nn.sigmoid(block_diag(xc, w_r[d], b_r[d]))
        ig = jax.nn.sigmoid(block_diag(xc, w_i[d], b_i[d]))
        h, hf = rglru_scan(xc, r, ig, lam[d].astype(F32), h0[:, d].astype(F32))
        outs.append(jnp.flip(h, axis=1) if d == 1 else h)
        finals.append(hf)
    y = (outs[0] + outs[1]) * jax.nn.gelu(gate.astype(F32))
    return y.astype(gate.dtype), jnp.stack(finals, axis=1)


def short_conv_mixer(bg, cg, v, w, rows):
    u = cg * v
    b, l, ch = u.shape
    u = u.reshape(b, rows, l // rows, ch)
    pad = SCONV_W // 2
    up = jnp.pad(u, ((0, 0), (0, 0), (pad, pad), (0, 0)))
    seg = l // rows
    y = up[:, :, 0:seg] * w[0]
    for j in range(1, SCONV_W):
        y = y + up[:, :, j:j + seg] * w[j]
    return bg * y.reshape(b, l, ch)


def trunk_layer(x, mod, rows, s_hgrn, s_gla, s_rg, lb, norm1_g, norm2_g, w_in, w_out,
                hgrn_norm_g, gla_wa2, gla_ba2, gla_norm_g, rg_conv_w, rg_conv_b,
                rg_w_r, rg_b_r, rg_w_i, rg_b_i, rg_lambda, sconv_w, mlp_w1, mlp_w2):
    shift1, scale1, gate1, shift2, scale2, gate2 = jnp.split(mod[:, None, :], 6, axis=-1)
    h = rmsnorm(x, norm1_g) * (1 + scale1) + shift1
    proj = h @ w_in
    points = np.cumsum(SPLIT_SIZES)[:-1].tolist()
    (a_q, a_i, a_ff, a_fb, a_g, b_q, b_k, b_v, b_g, b_af, b_ab,
     c_x, c_g, d_b, d_c, d_v) = jnp.split(proj, points, axis=-1)
    o_a, sf_a = hgrn2_mixer(a_q, a_i, a_ff, a_fb, a_g, lb, hgrn_norm_g, s_hgrn)
    o_b, sf_b = gla_mixer(b_q, b_k, b_v, b_g, b_af, b_ab, gla_wa2, gla_ba2, gla_norm_g, s_gla)
    o_c, sf_c = rglru_mixer(c_x, c_g, rg_conv_w, rg_conv_b, rg_w_r, rg_b_r, rg_w_i, rg_b_i,
                            rg_lambda, s_rg)
    o_d = short_conv_mixer(d_b, d_c, d_v, sconv_w, rows)
    mix = jnp.concatenate([o_a, o_b, o_c, o_d], axis=-1) @ w_out
    x = x + gate1 * mix
    h2 = rmsnorm(x, norm2_g) * (1 + scale2) + shift2
    ff = jnp.square(jax.nn.relu(h2 @ mlp_w1)) @ mlp_w2
    x = x + gate2 * ff
    return x, sf_a, sf_b, sf_c


def setup_inputs(seed: int = 0) -> dict:
    key = jax.random.key(seed)
    ks = jax.random.split(key, 32)
    D = D_MODEL

    def nrm(k, shape, s):
        return jax.random.normal(k, shape, F32) * s

    def gain(k, shape):
        return 1.0 + 0.1 * jax.random.normal(k, shape, F32)

    u = jax.random.uniform(ks[29], (DEPTH, 2, W_GRP), F32, 0.9, 0.999)
    p_a = u ** (1.0 / RG_C)
    rg_lambda = jnp.log(p_a) - jnp.log1p(-p_a)
    return {
        'x_prompt': nrm(ks[0], (BATCH, SEQ, D), 1.0),
        'x_sample': nrm(ks[1], (DEC_BATCH, DEC_SEQ, D), 1.0),
        'state_hgrn': nrm(ks[2], (DEC_BATCH, DEPTH, 2, N_HEADS, HGRN_DK, HEAD_V), 0.5),
        'state_gla': nrm(ks[3], (DEC_BATCH, DEPTH, 2, N_HEADS, GLA_DK, HEAD_V), 0.5),
        'state_rglru': nrm(ks[4], (DEC_BATCH, DEPTH, 2, W_GRP), 0.5),
        'c': nrm(ks[5], (DEC_BATCH, D), 1.0),
        'c_ctx': nrm(ks[6], (D,), 1.0),
        'norm1_g': gain(ks[7], (DEPTH, D)),
        'norm2_g': gain(ks[8], (DEPTH, D)),
        'ada_w': nrm(ks[9], (DEPTH, D, 6 * D), 0.3 * D ** -0.5),
        'ada_b': nrm(ks[10], (DEPTH, 6 * D), 0.02),
        'w_in': nrm(ks[11], (DEPTH, D, PROJ_W), D ** -0.5),
        'w_out': nrm(ks[12], (DEPTH, D, D), D ** -0.5),
        'hgrn_lb_logits': nrm(ks[13], (DEPTH, 2, W_GRP), 1.0),
        'hgrn_norm_g': gain(ks[14], (DEPTH, W_GRP)),
        'gla_wa2': nrm(ks[15], (DEPTH, 2, GLA_RANK, N_HEADS * GLA_DK), GLA_RANK ** -0.5),
        'gla_ba2': nrm(ks[16], (DEPTH, 2, N_HEADS * GLA_DK), 0.1),
        'gla_norm_g': gain(ks[17], (DEPTH, W_GRP)),
        'rg_conv_w': nrm(ks[18], (DEPTH, 2, RG_CONV, W_GRP), RG_CONV ** -0.5),
        'rg_conv_b': nrm(ks[19], (DEPTH, 2, W_GRP), 0.02),
        'rg_w_r': nrm(ks[20], (DEPTH, 2, RG_BLOCKS, RG_BLOCK, RG_BLOCK), RG_BLOCK ** -0.5),
        'rg_b_r': nrm(ks[21], (DEPTH, 2, W_GRP), 0.02),
        'rg_w_i': nrm(ks[22], (DEPTH, 2, RG_BLOCKS, RG_BLOCK, RG_BLOCK), RG_BLOCK ** -0.5),
        'rg_b_i': nrm(ks[23], (DEPTH, 2, W_GRP), 0.02),
        'rg_lambda': rg_lambda,
        'sconv_w': nrm(ks[24], (DEPTH, SCONV_W, W_GRP), SCONV_W ** -0.5),
        'mlp_w1': nrm(ks[25], (DEPTH, D, D_FF), D ** -0.5),
        'mlp_w2': nrm(ks[26], (DEPTH, D_FF, D), D_FF ** -0.5),
        'final_norm_g': gain(ks[27], (D,)),
    }


def reference(x_prompt, x_sample, state_hgrn, state_gla, state_rglru, c, c_ctx,
              norm1_g, norm2_g, ada_w, ada_b, w_in, w_out, hgrn_lb_logits, hgrn_norm_g,
              gla_wa2, gla_ba2, gla_norm_g, rg_conv_w, rg_conv_b, rg_w_r, rg_b_r,
              rg_w_i, rg_b_i, rg_lambda, sconv_w, mlp_w1, mlp_w2, final_norm_g):
    sm = jax.nn.softmax(hgrn_lb_logits.astype(F32), axis=0)
    lower_bounds = jnp.cumsum(sm, axis=0) - sm[0]
    b_ctx = x_prompt.shape[0]
    rows = x_sample.shape[1] // GRID_W
    zero_h = jnp.zeros((b_ctx, 2, N_HEADS, HGRN_DK, HEAD_V), F32)
    zero_g = jnp.zeros((b_ctx, 2, N_HEADS, GLA_DK, HEAD_V), F32)
    zero_r = jnp.zeros((b_ctx, 2, W_GRP), F32)
    xp, xs = x_prompt, x_sample
    new_h, new_g, new_r = [], [], []
    for l in range(DEPTH):
        lw = (norm1_g[l], norm2_g[l], w_in[l], w_out[l], hgrn_norm_g[l], gla_wa2[l], gla_ba2[l],
              gla_norm_g[l], rg_conv_w[l], rg_conv_b[l], rg_w_r[l], rg_b_r[l], rg_w_i[l],
              rg_b_i[l], rg_lambda[l], sconv_w[l], mlp_w1[l], mlp_w2[l])
        mod_ctx = (jax.nn.silu(c_ctx) @ ada_w[l] + ada_b[l])[None, :]
        mod_lat = jax.nn.silu(c) @ ada_w[l] + ada_b[l]
        xp, sh, sg, sr = trunk_layer(xp, mod_ctx, 1, zero_h, zero_g, zero_r, lower_bounds[l], *lw)
        xs, _, _, _ = trunk_layer(xs, mod_lat, rows, state_hgrn[:, l], state_gla[:, l],
                                  state_rglru[:, l], lower_bounds[l], *lw)
        new_h.append(sh)
        new_g.append(sg)
        new_r.append(sr)
    y_prompt = rmsnorm(xp, final_norm_g)
    y_sample = rmsnorm(xs, final_norm_g)
    new_state_hgrn = jnp.stack(new_h, axis=1).astype(x_prompt.dtype)
    new_state_gla = jnp.stack(new_g, axis=1).astype(x_prompt.dtype)
    new_state_rglru = jnp.stack(new_r, axis=1).astype(x_prompt.dtype)
    return (y_prompt, y_sample, new_state_hgrn, new_state_gla, new_state_rglru)
```

```python
import numpy as np
from contextlib import ExitStack
import concourse.bass as bass
import concourse.mybir as mybir
from concourse.bass_utils import run_bass_kernel_spmd

F32 = mybir.dt.float32
F32R = mybir.dt.float32r
AF = mybir.ActivationFunctionType
ALU = mybir.AluOpType
ENGS = ("tensor", "vector", "scalar", "gpsimd", "sync")

D = 1024; L = 2; T = 1024; NB = 8; NCH = 32
ADA_CNT = [2, 2, 2, 2, 1, 1, 1, 1]
ADA_G0 = [0, 2, 4, 6, 8, 9, 10, 11]
PROJ_W = 3360


class V:
    def __init__(self, ap, name, p0, p1, f0, f1):
        self.ap, self.name, self.p0, self.p1, self.f0, self.f1 = ap, name, p0, p1, f0, f1

    def reg(self):
        return (self.name, self.p0, self.p1, self.f0, self.f1)

    def w(self, ap):
        return V(ap, self.name, self.p0, self.p1, self.f0, self.f1)

    def r(self, pattern, **kw):
        return self.w(self.ap.rearrange(pattern, **kw))

    def bc(self, dt=F32R):
        return self.w(self.ap.bitcast(dt))


class TL:
    def __init__(self, handle, name, P, F, base=0):
        self.h, self.name, self.P, self.F, self.base = handle, name, P, F, base

    def v(self, p0=0, p1=None, f0=0, f1=None):
        p1 = self.P if p1 is None else p1
        f1 = self.F if f1 is None else f1
        return V(self.h[p0:p1, self.base + f0:self.base + f1], self.name, p0, p1, self.base + f0, self.base + f1)

    def sub(self, f0, F):
        return TL(self.h, self.name, self.P, F, self.base + f0)


def _ov(a, b):
    return a[0] == b[0] and a[1] < b[2] and b[1] < a[2] and a[3] < b[4] and b[3] < a[4]


def _ct(a, b):
    return a[0] == b[0] and a[1] <= b[1] and b[2] <= a[2] and a[3] <= b[3] and b[4] <= a[4]


class Prog:
    def __init__(self, nc, es, n_dma_sems=12):
        self.nc = nc
        self.ops = {e: [] for e in ENGS}
        self.cnt = {e: 0 for e in ENGS}
        self.sems = {e: es.enter_context(nc.semaphore("s_" + e)) for e in ENGS}
        self.dma_sems = [es.enter_context(nc.semaphore("d%d" % i)) for i in range(n_dma_sems)]
        self.es = es
        self.nfixed = n_dma_sems
        self.dma_uses = [0] * n_dma_sems
        self.dma_next = 0
        self.waited = {e: {} for e in ENGS}
        self.recs = {}

    def _deps(self, eng, reads, writes):
        deps = []
        for v in reads:
            rg = v.reg()
            for (r2, kind, sk, val, e2) in self.recs.get(rg[0], ()):
                if kind == "w" and _ov(rg, r2):
                    if e2 == eng and eng == "tensor":
                        continue
                    deps.append((sk, val))
        for v in writes:
            rg = v.reg()
            if eng == "tensor":
                rg = (rg[0], rg[1], rg[2], 0, 1 << 30)
            for (r2, kind, sk, val, e2) in self.recs.get(rg[0], ()):
                if _ov(rg, r2):
                    if e2 == eng and eng == "tensor":
                        continue
                    deps.append((sk, val))
        return deps

    def _record(self, eng, reads, writes, sk, val):
        for v in writes:
            rg = v.reg()
            lst = self.recs.setdefault(rg[0], [])
            lst[:] = [x for x in lst if not _ct(rg, x[0])]
            lst.append((rg, "w", sk, val, eng))
        for v in reads:
            rg = v.reg()
            lst = self.recs.setdefault(rg[0], [])
            lst[:] = [x for x in lst if not (x[1] == "r" and x[2] == sk and _ct(rg, x[0]))]
            lst.append((rg, "r", sk, val, eng))

    def _sem(self, sk):
        return self.sems[sk] if isinstance(sk, str) else self.dma_sems[sk]

    def _emit_waits(self, eng, deps):
        best = {}
        for sk, val in deps:
            if val > best.get(sk, 0):
                best[sk] = val
        for sk, val in best.items():
            if val > self.waited[eng].get(sk, 0):
                self.waited[eng][sk] = val
                sem = self._sem(sk)
                self.ops[eng].append(lambda E, sem=sem, val=val: E.wait_ge(sem, val))

    def op(self, eng, fn, reads=(), writes=()):
        self._emit_waits(eng, self._deps(eng, reads, writes))
        self.cnt[eng] += 1
        sem = self.sems[eng]
        self.ops[eng].append(lambda E, fn=fn, sem=sem: fn(E).then_inc(sem, 1))
        self._record(eng, reads, writes, eng, self.cnt[eng])

    def dma(self, eng, out_ap, in_ap, reads=(), writes=()):
        if eng == "gpsimd":
            deps = self._deps(eng, reads, writes)
            self._emit_waits(eng, deps)
            sem = self.es.enter_context(self.nc.semaphore("w%d" % len(self.dma_sems)))
            self.dma_sems.append(sem)
            self.dma_uses.append(1)
            slot = len(self.dma_sems) - 1
            self.ops[eng].append(lambda E, o=out_ap, i=in_ap, sem=sem: E.dma_start(out=o, in_=i).then_inc(sem, 16))
            self._record("dma", reads, writes, slot, 16)
            return
        slot = self.dma_next
        self.dma_next = (self.dma_next + 1) % self.nfixed
        deps = self._deps(eng, reads, writes)
        if self.dma_uses[slot] > 0:
            deps.append((slot, 16 * self.dma_uses[slot]))
        self._emit_waits(eng, deps)
        self.dma_uses[slot] += 1
        sem = self.dma_sems[slot]
        self.ops[eng].append(lambda E, o=out_ap, i=in_ap, sem=sem: E.dma_start(out=o, in_=i).then_inc(sem, 16))
        self._record("dma", reads, writes, slot, 16 * self.dma_uses[slot])

    def finish(self, eng="sync"):
        self._emit_waits(eng, [(s, 16 * u) for s, u in enumerate(self.dma_uses) if u > 0])

    def emit(self, block):
        for e in ENGS:
            ops = self.ops[e]
            if not ops:
                continue

            def body(E, ops=ops):
                for o in ops:
                    o(E)

            getattr(block, e)(body)


class Packer:
    def __init__(self):
        self.cols = []
        self.off = {}
        self.n = 0

    def add(self, name, arr):
        arr = np.asarray(arr, np.float32)
        assert arr.shape[0] == 128
        self.off[name] = self.n
        self.cols.append(arr)
        self.n += arr.shape[1]

    def get(self):
        return np.ascontiguousarray(np.concatenate(self.cols, axis=1))


def fm(vec):
    return np.asarray(vec, np.float32).reshape(-1, 128).T


def kslot(mat):
    out = np.zeros((128, 8, 512), np.float32)
    out[:, :, :mat.shape[1]] = mat.reshape(8, 128, mat.shape[1]).transpose(1, 0, 2)
    return out


def build_stream(inp):
    slots = []
    def ada_slots(l, g0, n):
        if l >= L:
            return
        aw = inp["ada_w"][l]
        for g in range(g0, g0 + n):
            slots.append(kslot(aw[:, 512 * g:512 * g + 512]).reshape(128, 4096))

    for l in range(L):
        if l == 0:
            ada_slots(0, 0, 4)
        wi = inp["w_in"][l]; wo = inp["w_out"][l]

        def chunks(cols_list):
            s = np.zeros((128, 8, 512), np.float32)
            for i, m in enumerate(cols_list):
                if m is not None:
                    s[:, :, i * 128:(i + 1) * 128] = m.reshape(8, 128, 128).transpose(1, 0, 2)
            return s

        def wcols(c0, n=128):
            m = np.zeros((1024, 128), np.float32)
            m[:, :n] = wi[:, c0:c0 + n]
            return m

        def woslot(r0):
            s = np.zeros((128, 4096), np.float32)
            s[:, :2048] = wo[r0:r0 + 256].reshape(2, 128, 1024).transpose(1, 0, 2).reshape(128, 2048)
            return s

        dB, dC, dV = 2592, 2848, 3104
        slots.append(chunks([wcols(dB), wcols(dC), wcols(dV), wcols(dB + 128)]).reshape(128, 4096))
        s2 = chunks([wcols(dC + 128), wcols(dV + 128), None, None])
        for d in range(2):
            for cc in range(2):
                for wh, w in enumerate((inp["rg_w_r"][l], inp["rg_w_i"][l])):
                    i = (d * 2 + cc) * 2 + wh
                    bd = np.zeros((128, 128), np.float32)
                    for nn in range(2):
                        bd[nn * 64:(nn + 1) * 64, nn * 64:(nn + 1) * 64] = w[d, 2 * cc + nn]
                    s2[:, i, 256:384] = bd
        slots.append(s2.reshape(128, 4096))
        if l == 0:
            ada_slots(0, 4, 2)
        slots.append(woslot(768))
        cX, cG = 2080, 2336
        slots.append(chunks([wcols(cX), wcols(cX + 128), wcols(cG), wcols(cG + 128)]).reshape(128, 4096))
        slots.append(woslot(512))
        def padqk(c0, ch):
            m = np.zeros((1024, 128), np.float32)
            for hh in range(2):
                h = 2 * ch + hh
                m[:, hh * 64:hh * 64 + 32] = wi[:, c0 + h * 32:c0 + h * 32 + 32]
            return m
        sv = chunks([wcols(1536), wcols(1664), None, None])
        for d in range(2):
            for ch in range(2):
                m = np.zeros((128, 128), np.float32)
                for hh in range(2):
                    h = 2 * ch + hh
                    m[d * 16:(d + 1) * 16, hh * 64:hh * 64 + 32] = inp["gla_wa2"][l, d][:, h * 32:(h + 1) * 32]
                sv[:, d * 2 + ch, 256:384] = m
        slots.append(sv.reshape(128, 4096))
        for ch in range(2):
            slots.append(chunks([padqk(1280, ch), padqk(1408, ch), wcols(1792 + ch * 128), wcols(2048, 32)]).reshape(128, 4096))
            if l == 0:
                ada_slots(0, 6 + 2 * ch, 2)
        slots.append(woslot(256))
        slots.append(chunks([wcols(256), wcols(384), None, None]).reshape(128, 4096))
        for ch in range(2):
            slots.append(chunks([wcols(0 + ch * 128), wcols(512 + ch * 128), wcols(768 + ch * 128),
                                 wcols(1024 + ch * 128)]).reshape(128, 4096))
            if l == 0 and ch == 0:
                ada_slots(0, 10, 2)
        slots.append(woslot(0))
        w1 = inp["mlp_w1"][l]; w2 = inp["mlp_w2"][l]
        for g in range(8):
            slots.append(kslot(w1[:, g * 512:(g + 1) * 512]).reshape(128, 4096))
            slots.append(np.ascontiguousarray(
                w2[g * 512:(g + 1) * 512].reshape(4, 128, 1024).transpose(1, 0, 2)).reshape(128, 4096))
            ada_slots(l + 1, ADA_G0[g], ADA_CNT[g])
    return np.ascontiguousarray(np.stack(slots, 0))


def build_consts():
    pk = Packer()
    s = np.arange(128)[:, None]; t = np.arange(128)[None, :]
    same = (s // 32) == (t // 32)
    pk.add("mf", (same & (s <= t)).astype(np.float32))
    pk.add("mb", (same & (s >= t)).astype(np.float32))
    pk.add("mf128", (s <= t).astype(np.float32))
    pk.add("mb128", (s >= t).astype(np.float32))
    pk.add("ident", np.eye(128, dtype=np.float32))
    o64 = np.zeros((128, 128), np.float32); o64[:64, :64] = 1 / 64.; o64[64:, 64:] = 1 / 64.
    pk.add("o64", o64)
    pk.add("bm", ((np.arange(128)[:, None] // 32) == np.arange(4)[None, :]).astype(np.float32))
    mFB = np.ones((128, 1025), np.float32); mFB[:, 0::32] = 0
    pk.add("mFB", mFB)
    pk.add("one", np.ones((128, 1), np.float32))
    return pk


def build_params(inp, core):
    pk = Packer()
    samp = core < 4
    crow = inp["c"][core] if samp else inp["c_ctx"]
    pk.add("cvec", fm(crow))
    fl = np.zeros((128, 2), np.float32); fl[:, 0] = 1.0 if samp else 0.0; fl[:, 1] = 0.0 if samp else 1.0
    pk.add("flags", fl)
    pk.add("fng", fm(inp["final_norm_g"]))
    for l in range(L):
        pk.add("adab%d" % l, fm(inp["ada_b"][l]))
        pk.add("n1g%d" % l, fm(inp["norm1_g"][l]))
        pk.add("n2g%d" % l, fm(inp["norm2_g"][l]))
        pk.add("lbz%d" % l, fm(inp["hgrn_lb_logits"][l].reshape(-1)))
        pk.add("hgain%d" % l, fm(inp["hgrn_norm_g"][l]))
        pk.add("ggain%d" % l, fm(inp["gla_norm_g"][l]))
        b = np.zeros((128, 4), np.float32)
        for d in range(2):
            for ch in range(2):
                for hh in range(2):
                    h = 2 * ch + hh
                    b[hh * 64:hh * 64 + 32, d * 2 + ch] = inp["gla_ba2"][l, d, h * 32:(h + 1) * 32]
        pk.add("ba2%d" % l, b)
        cw = np.zeros((128, 16), np.float32)
        for d in range(2):
            for cc in range(2):
                for j in range(4):
                    cw[:, (d * 2 + cc) * 4 + j] = inp["rg_conv_w"][l, d, j, cc * 128:(cc + 1) * 128]
        pk.add("rgcw%d" % l, cw)
        pk.add("rgcb%d" % l, fm(inp["rg_conv_b"][l].reshape(-1)))
        pk.add("rgbr%d" % l, fm(inp["rg_b_r"][l].reshape(-1)))
        pk.add("rgbi%d" % l, fm(inp["rg_b_i"][l].reshape(-1)))
        pk.add("rglam%d" % l, fm(inp["rg_lambda"][l].reshape(-1)))
        sw = np.zeros((128, 6), np.float32)
        for cc in range(2):
            for j in range(3):
                sw[:, cc * 3 + j] = inp["sconv_w"][l, j, cc * 128:(cc + 1) * 128]
        pk.add("scw%d" % l, sw)
        if samp:
            pk.add("str%d" % l, fm(inp["state_rglru"][core, l].reshape(-1)))
        else:
            pk.add("str%d" % l, np.zeros((128, 4), np.float32))
    return pk


def build_st0(inp, core):
    st = np.zeros((128, 16, 64), np.float32)
    if core < 4:
        for m, key in enumerate(("state_hgrn", "state_gla")):
            s = inp[key][core]
            dk = s.shape[3]
            for l in range(L):
                for d in range(2):
                    for ch in range(2):
                        i = ((m * 2 + l) * 2 + d) * 2 + ch
                        for hh in range(2):
                            st[hh * 64:hh * 64 + dk, i, :] = s[l, d, 2 * ch + hh]
    return st.reshape(128, 1024)


def build_program(po, co, n_slots):
    nc = bass.Bass("TRN2", target_bir_lowering=False)
    xT = nc.dram_tensor("xT", [D, T], F32, kind="ExternalInput").ap()
    wst = nc.dram_tensor("wst", [n_slots, 128, 4096], F32, kind="ExternalInput").ap()
    prm = nc.dram_tensor("prm", [128, po.n], F32, kind="ExternalInput").ap()
    cst = nc.dram_tensor("cst", [128, co.n], F32, kind="ExternalInput").ap()
    st0 = nc.dram_tensor("st0", [128, 1024], F32, kind="ExternalInput").ap()
    onesd = nc.dram_tensor("onesd", [128, 128], F32, kind="ExternalInput").ap()
    yT = nc.dram_tensor("yT", [D, T], F32, kind="ExternalOutput").ap()
    osA = nc.dram_tensor("osA", [32, 128, 64], F32, kind="ExternalOutput").ap()
    osB = nc.dram_tensor("osB", [32, 128, 64], F32, kind="ExternalOutput").ap()
    osr = nc.dram_tensor("osr", [128, 32], F32, kind="ExternalOutput").ap()

    with ExitStack() as es:
        def sb(name, P, Fn, dt=F32):
            return TL(es.enter_context(nc.sbuf_tensor(name, [P, Fn], dt)), name, P, Fn)

        def psum(name):
            return TL(es.enter_context(nc.psum_tensor(name, [128, 512], F32)), name, 128, 512)

        p = Prog(nc, es)
        X = sb("X", 128, 8192); H = sb("H", 128, 8192)
        RING = sb("RING", 128, 8192, F32R)
        W = sb("W", 128, 12288)
        FR = sb("FR", 128, 4096, F32R)
        OM = FR.sub(0, 2048); SREP = FR.sub(2048, 1024); HID = FR.sub(0, 4096)
        PRM = sb("PRM", 128, po.n); CST = sb("CST", 128, co.n)
        ONES = sb("ONES", 128, 128, F32R)
        MOD = sb("MOD", 128, 48); GS = sb("GS", 128, 16)
        SM = sb("SM", 128, 64)
        OSR = sb("OSR", 128, 32)
        MODN = sb("MODN", 128, 96); ACH = sb("ACH", 128, 64)
        WA2T = sb("WA2T", 128, 512)
        PS = [psum("P%d" % i) for i in range(8)]
        PJ = [PS[0], PS[1]]; PMS = PS[2]; PSC = PS[3]; PAC = PS[4]; PDS = PS[5]; PTR = PS[6]; PX = PS[7]

        def wt(i):
            return W.sub(i * 1024, 1024)
        tQ, tKF, tLAF, tG, tO, tCUM, tKH, tTMP = [wt(i) for i in range(8)]
        DS = W.sub(8192, 2048); SRING = W.sub(10240, 1024); AUXC = W.sub(11264, 1024)
        KHT = sb("KHT", 128, 256, F32R); VBD = sb("VBD", 128, 1024, F32R)
        tQT = sb("QT", 128, 1024, F32R); tKT = sb("KT", 128, 1024, F32R); PT = sb("PTt", 128, 512, F32R)
        VPAD = sb("VPAD", 128, 4096, F32R); SRR = sb("SRR", 128, 1024, F32R)
        SQT = W.sub(8192, 1024)

        def prm_v(name, c0=0, n=1):
            o = po.off[name] + c0
            return PRM.v(f0=o, f1=o + n)

        def cst_v(name, c0=0, n=1, p0=0, p1=128):
            o = co.off[name] + c0
            return CST.v(p0=p0, p1=p1, f0=o, f1=o + n)

        flag_s = prm_v("flags", 0); flag_p = prm_v("flags", 1)

        def act(out, in_, func, bias=None, scale=None, extra_reads=()):
            kw = {}
            rd = [in_] + list(extra_reads)
            if bias is not None:
                if isinstance(bias, V):
                    kw["bias"] = bias.ap; rd.append(bias)
                else:
                    kw["bias"] = bias
            if scale is not None:
                if isinstance(scale, V):
                    kw["scale"] = scale.ap; rd.append(scale)
                else:
                    kw["scale"] = scale
            p.op("scalar", lambda E: E.activation(out=out.ap, in_=in_.ap, func=func, **kw), reads=rd, writes=[out])

        def tt(out, a, b, op, eng="vector"):
            p.op(eng, lambda E: E.tensor_tensor(out=out.ap, in0=a.ap, in1=b.ap, op=op), reads=[a, b], writes=[out])

        def ts(out, a, s1, op0, s2=None, op1=None, eng="vector"):
            rd = [a]
            a1 = s1.ap if isinstance(s1, V) else s1
            if isinstance(s1, V): rd.append(s1)
            kw = {"scalar2": None}
            if op1 is not None:
                kw["scalar2"] = s2.ap if isinstance(s2, V) else s2
                kw["op1"] = op1
                if isinstance(s2, V): rd.append(s2)
            p.op(eng, lambda E: E.tensor_scalar(out=out.ap, in0=a.ap, scalar1=a1, op0=op0, **kw), reads=rd, writes=[out])

        def stt(out, a, s, b, op0, op1):
            rd = [a, b]
            sa = s.ap if isinstance(s, V) else s
            if isinstance(s, V): rd.append(s)
            p.op("vector", lambda E: E.scalar_tensor_tensor(out=out.ap, in0=a.ap, scalar=sa, in1=b.ap, op0=op0, op1=op1),
                 reads=rd, writes=[out])

        def cp(out, in_, eng="vector"):
            if eng == "scalar":
                p.op("scalar", lambda E: E.copy(out.ap, in_.ap), reads=[in_], writes=[out])
            else:
                p.op(eng, lambda E: E.tensor_copy(out.ap, in_.ap), reads=[in_], writes=[out])

        def mm(out, lhsT, rhs, start=True, stop=True):
            p.op("tensor", lambda E: E.matmul(out.ap, lhsT.ap, rhs.ap, start=start, stop=stop),
                 reads=[lhsT, rhs], writes=[out])

        def memset(v, val, eng="gpsimd"):
            p.op(eng, lambda E: E.memset(v.ap, val), writes=[v])

        state = {"cur": -1, "issued": 0, "pj": 0, "rb": 0, "rb2": 0}

        def issue(i):
            if i < n_slots and state["issued"] <= i:
                rv = RING.v(f0=(i % 2) * 4096, f1=(i % 2) * 4096 + 4096)
                p.dma("gpsimd", rv.ap, wst[i], writes=[rv])
                state["issued"] = i + 1

        def next_slot():
            state["cur"] += 1
            i = state["cur"]
            issue(i)
            issue(i + 1)
            return RING.sub((i % 2) * 4096, 4096)

        def sl_k(sl, k, c0, n):
            return sl.v(f0=k * 512 + c0, f1=k * 512 + c0 + n)

        def Hk(k, t0, n):
            return H.v(f0=k * 1024 + t0, f1=k * 1024 + t0 + n).bc()

        def proj_chunk(sl, ci, evac):
            for half in range(2):
                P = PJ[state["pj"]]; state["pj"] ^= 1
                for k in range(8):
                    mm(P.v(), sl_k(sl, k, ci * 128, 128), Hk(k, half * 512, 512), start=(k == 0), stop=(k == 7))
                evac(P.v(), half)

        def proj_vtok(sl, ci0):
            for b in range(NB):
                P = PJ[state["pj"]]; state["pj"] ^= 1
                for k in range(8):
                    mm(P.v(f1=256), Hk(k, b * 128, 128), sl_k(sl, k, ci0 * 128, 256), start=(k == 0), stop=(k == 7))
                src = P.v(f1=256).r("p (a h e) -> p a h e", a=2, h=2)
                dst = VPAD.v(f0=b * 512, f1=b * 512 + 512)
                base = dst.ap
                d4 = bass.AP(tensor=base.tensor, offset=base.offset, ap=[base.ap[0], [256, 2], [192, 2], [1, 64]])
                p.op("scalar", lambda E, s=src, d4=d4: E.copy(d4, s.ap), reads=[src], writes=[dst])

        for c in range(8):
            p.dma("sync", X.v(f0=c * 1024, f1=(c + 1) * 1024).ap, xT[c * 128:(c + 1) * 128, :],
                  writes=[X.v(f0=c * 1024, f1=(c + 1) * 1024)])
        p.dma("sync", PRM.v().ap, prm, writes=[PRM.v()])
        p.dma("sync", CST.v().ap, cst, writes=[CST.v()])
        p.dma("gpsimd", ONES.v().ap, onesd, writes=[ONES.v()])
        memset(SRING.v(), 0.0)
        zsrc = cst_v("mFB", 0, 1024)
        for i in range(4):
            ts(VPAD.v(f0=i * 1024, f1=i * 1024 + 1024), zsrc, 0.0, ALU.mult)
        ts(SRR.v(), zsrc, 0.0, ALU.mult)
        memset(OSR.v(), 0.0)

        def make_srep(dst=None):
            dst = SREP if dst is None else dst
            act(SM.v(f0=0, f1=8), prm_v("cvec", 0, 8), AF.Silu)
            srep3 = dst.v().r("p (k m) -> p k m", k=8)
            sin3 = SM.v(f0=0, f1=8)
            sin_ap = sin3.ap.rearrange("p (k o) -> p k o", o=1).broadcast_to([128, 8, 128])
            p.op("vector", lambda E: E.tensor_copy(srep3.ap, sin_ap), reads=[sin3], writes=[dst.v()])

        def rmsnorm_to_H(gs_v_fn, shift_fn):
            rms_stats()
            rms_apply(gs_v_fn, shift_fn)

        def rms_stats():
            sb_ = [PMS, PX]
            for half in range(2):
                for c in range(8):
                    act(H.v(f0=c * 1024 + half * 512, f1=c * 1024 + half * 512 + 512).bc(),
                        X.v(f0=c * 1024 + half * 512, f1=c * 1024 + half * 512 + 512), AF.Square)
                for k in range(8):
                    mm(sb_[half].v(), ONES.v(), Hk(k, half * 512, 512), start=(k == 0), stop=(k == 7))
                act(tTMP.v(f0=half * 512, f1=half * 512 + 512), sb_[half].v(), AF.Ln, bias=prm_eps, scale=1.0 / D)
                act(tTMP.v(f0=half * 512, f1=half * 512 + 512), tTMP.v(f0=half * 512, f1=half * 512 + 512), AF.Exp, scale=-0.5)

        def rms_apply(gs_v_fn, shift_fn):
            for c in range(8):
                xv = X.v(f0=c * 1024, f1=(c + 1) * 1024); hv = H.v(f0=c * 1024, f1=(c + 1) * 1024)
                if shift_fn is None:
                    ot = (tQ if c % 2 == 0 else tKF).v()
                    stt(ot, xv, gs_v_fn(c), tTMP.v(), ALU.mult, ALU.mult)
                    p.dma("sync", yT[c * 128:(c + 1) * 128, :], ot.ap, reads=[ot])
                else:
                    tmp = (tCUM if c % 2 == 0 else tKH).v()
                    stt(tmp, xv, gs_v_fn(c), tTMP.v(), ALU.mult, ALU.mult)
                    act(hv.bc(), tmp, AF.Identity, bias=shift_fn(c))

        memset(SM.v(f0=8, f1=9), 1e-6, eng="vector")
        prm_eps = SM.v(f0=8, f1=9)

        def hv2(t, h):
            return t.v(f0=h * 512, f1=h * 512 + 512)

        def gla_prep_part1(d, Kt, LAt, CS=32):
            NCK = T // CS
            NH = NCK // 2
            th = []
            if CS == 32:
                for h in range(2):
                    m = cst_v("mFB", (0 if d == 0 else 1) + h * 512, 512)
                    cv = hv2(tCUM, h); lv = hv2(LAt, h)
                    if d == 0:
                        th.append(lambda cv=cv, lv=lv, m=m: p.op("vector", lambda E: E.tensor_tensor_scan(
                            out=cv.ap, data0=m.ap, data1=lv.ap, initial=0.0, op0=ALU.mult, op1=ALU.add), reads=[m, lv], writes=[cv]))
                    else:
                        th.append(lambda cv=cv, lv=lv, m=m: p.op("vector", lambda E: E.tensor_tensor_scan(
                            out=cv.ap[:, ::-1], data0=m.ap[:, ::-1], data1=lv.ap[:, ::-1], initial=0.0,
                            op0=ALU.mult, op1=ALU.add), reads=[m, lv], writes=[cv]))
            else:
                onesv = cst_v("mf128", 127, 1)
                ones_b = onesv.ap.broadcast_to([128, CS])

                def scans(n0, n1):
                    for n in range(n0, n1):
                        cv = tCUM.v(f0=n * CS, f1=(n + 1) * CS); lv = LAt.v(f0=n * CS, f1=(n + 1) * CS)
                        if d == 0:
                            p.op("vector", lambda E, cv=cv, lv=lv: E.tensor_tensor_scan(out=cv.ap, data0=ones_b, data1=lv.ap, initial=0.0,
                                                                                    op0=ALU.mult, op1=ALU.add), reads=[onesv, lv], writes=[cv])
                        else:
                            p.op("vector", lambda E, cv=cv, lv=lv: E.tensor_tensor_scan(out=cv.ap[:, ::-1], data0=ones_b, data1=lv.ap[:, ::-1],
                                                                                    initial=0.0, op0=ALU.mult, op1=ALU.add),
                                 reads=[onesv, lv], writes=[cv])
                th.append(lambda: scans(0, NH))
                th.append(lambda: scans(NH, NCK))
            li = CS - 1 if d == 0 else 0
            for h in range(2):
                cvh = hv2(tCUM, h); tmh = hv2(tTMP, h)
                c3 = cvh.r("p (n j) -> p n j", j=CS)
                lastb = c3.ap[:, :, li:li + 1].broadcast_to([128, NH, CS])
                t3 = tmh.r("p (n j) -> p n j", j=CS)
                th.append(lambda c3=c3, lastb=lastb, t3=t3, cvh=cvh, tmh=tmh: p.op(
                    "vector", lambda E: E.tensor_tensor(out=t3.ap, in0=lastb, in1=c3.ap, op=ALU.subtract), reads=[cvh], writes=[tmh]))
            for h in range(2):
                th.append(lambda h=h: act(hv2(tTMP, h), hv2(tTMP, h), AF.Exp))
            late = [lambda h=h: tt(hv2(tKH, h), hv2(Kt, h), hv2(tTMP, h), ALU.mult) for h in range(2)]
            cumv = tCUM.v()
            c3a = cumv.r("p (n j) -> p n j", j=CS)
            last = c3a.w(c3a.ap[:, :, li:li + 1])
            ach = ACH.v(f0=d * 32, f1=d * 32 + NCK)
            th.append(lambda: p.op("scalar", lambda E: E.activation(out=ach.ap, in_=last.ap.rearrange("p n o -> p (n o)"), func=AF.Exp),
                                   reads=[cumv], writes=[ach]))
            return th, late, ach

        def gla_prep_part2(Kt):
            for h in range(2):
                act(hv2(tTMP, h), hv2(tCUM, h), AF.Exp)
            for h in range(2):
                tt(hv2(tQT, h), hv2(tQ, h), hv2(tTMP, h), ALU.mult)
            for h in range(2):
                act(hv2(tTMP, h), hv2(tCUM, h), AF.Exp, scale=-1.0)
            for h in range(2):
                tt(hv2(tKT, h), hv2(Kt, h), hv2(tTMP, h), ALU.mult)

        def gla_core(mix_id, l, ch, prep_hooks, gain_v, os_dram, CS=32):
            ident = cst_v("ident", 0, 128)
            for d in range(2):
                si = ((mix_id * 2 + l) * 2 + d) * 2 + ch
                dv = SRING.v(f0=0, f1=64)
                p.dma("sync", dv.ap, st0[:, si * 64:si * 64 + 64], writes=[dv])
                CPB = 128 // CS; SEGC = 256 // CS
                if d == 0:
                    th0, late0, ach = gla_prep_part1(0, tKF, tLAF, CS)
                    for f in list(prep_hooks[0]) + th0 + late0:
                        f()
                    th1, late1, ach1 = gla_prep_part1(1, tKF, tLAF, CS)
                    pending = list(prep_hooks[1]) + th1
                else:
                    for f in pending + late1:
                        f()
                    pending = []
                    ach = ach1
                blocks = list(range(NB)) if d == 0 else list(range(NB - 1, -1, -1))
                gla_prep_part2(tKF)
                def ds_front(b):
                    ptr = PTR.v(f0=(b % 2) * 128, f1=(b % 2) * 128 + 128)
                    p.op("tensor", lambda E, b=b, ptr=ptr: E.transpose(ptr.ap, tKH.v(f0=b * 128, f1=b * 128 + 128).ap, ident.ap),
                         reads=[tKH.v(f0=b * 128, f1=b * 128 + 128), ident], writes=[ptr])
                    cp(KHT.v(f0=(b % 2) * 128, f1=(b % 2) * 128 + 128), ptr, eng="scalar")
                    if CS == 128:
                        return
                    vsrc = VPAD.v(f0=b * 512 + ch * 256, f1=b * 512 + ch * 256 + 256).bc(F32)
                    base = vsrc.ap
                    vin = bass.AP(tensor=base.tensor, offset=base.offset, ap=[base.ap[0], [192, 2], [0, 4], [1, 64]])
                    bmv = cst_v("bm", 0, 4)
                    bmin = bass.AP(tensor=bmv.ap.tensor, offset=bmv.ap.offset, ap=[bmv.ap.ap[0], [0, 2], [1, 4], [0, 64]])
                    vb = VBD.v(f0=(b % 2) * 512, f1=(b % 2) * 512 + 512)
                    vout = vb.r("p (h n e) -> p h n e", h=2, n=4)
                    p.op("vector", lambda E, vout=vout, vin=vin, bmin=bmin: E.tensor_tensor(out=vout.ap, in0=vin, in1=bmin, op=ALU.mult),
                         reads=[vsrc, bmv], writes=[vb])

                def ds_back(b):
                    vb = VBD.v(f0=(b % 2) * 512, f1=(b % 2) * 512 + 512)
                    pds = PDS if b % 2 == 0 else PMS
                    if CS == 128:
                        mm(pds.v(f1=256), KHT.v(f0=(b % 2) * 128, f1=(b % 2) * 128 + 128),
                           VPAD.v(f0=b * 512 + ch * 256, f1=b * 512 + ch * 256 + 256))
                        cp(DS.v(p0=0, p1=64, f0=b * 64, f1=b * 64 + 64), pds.v(p0=0, p1=64, f0=0, f1=64), eng="scalar")
                        cp(DS.v(p0=64, p1=128, f0=b * 64, f1=b * 64 + 64), pds.v(p0=64, p1=128, f0=192, f1=256), eng="scalar")
                        return
                    mm(pds.v(), KHT.v(f0=(b % 2) * 128, f1=(b % 2) * 128 + 128), vb)
                    cp(DS.v(p0=0, p1=64, f0=b * 256, f1=b * 256 + 256), pds.v(p0=0, p1=64, f0=0, f1=256), eng="scalar")
                    cp(DS.v(p0=64, p1=128, f0=b * 256, f1=b * 256 + 256), pds.v(p0=64, p1=128, f0=256, f1=512), eng="scalar")

                ds_front(blocks[0])
                def cast_state(sl_):
                    for hh in range(2):
                        cp(SRR.v(p0=hh * 64, p1=hh * 64 + 64, f0=sl_ * 128 + hh * 64, f1=sl_ * 128 + hh * 64 + 64),
                           SRING.v(p0=hh * 64, p1=hh * 64 + 64, f0=sl_ * 64, f1=sl_ * 64 + 64), eng="scalar")
                cast_state(0)
                mask = cst_v("mf" if CS == 32 else "mf128", 0, 128) if d == 0 else cst_v("mb" if CS == 32 else "mb128", 0, 128)
                SCB = [(PSC, PX), (PSC, PX)]
                ACB = [PAC, PJ[0], PJ[1]]
                st_ = {"slot": 0}

                def stage_scores(i):
                    b = blocks[i]
                    for hh in range(2):
                        mm(SCB[i % 2][hh].v(f1=128),
                           tKT.v(p0=hh * 64, p1=hh * 64 + 64, f0=b * 128, f1=b * 128 + 128),
                           tQT.v(p0=hh * 64, p1=hh * 64 + 64, f0=b * 128, f1=b * 128 + 128))
                    for hh in range(2):
                        tt(PT.v(f0=(i % 2) * 256 + hh * 128, f1=(i % 2) * 256 + hh * 128 + 128), SCB[i % 2][hh].v(f1=128), mask, ALU.mult)

                def stage_main(i):
                    b = blocks[i]
                    accb = ACB[i % 3]
                    acc = accb.v(f1=128)
                    for hh in range(2):
                        h = 2 * ch + hh
                        mm(acc, VPAD.v(f0=b * 512 + h * 128, f1=b * 512 + h * 128 + 128),
                           PT.v(f0=(i % 2) * 256 + hh * 128, f1=(i % 2) * 256 + hh * 128 + 128), start=(hh == 0), stop=False)
                    chunks = list(range(CPB * b, CPB * b + CPB)) if d == 0 else list(range(CPB * b + CPB - 1, CPB * b - 1, -1))
                    for ci, n in enumerate(chunks):
                        slot = st_["slot"]
                        mm(accb.v(f0=(n % CPB) * CS, f1=(n % CPB) * CS + CS),
                           SRR.v(f0=slot * 128, f1=slot * 128 + 128), tQT.v(f0=n * CS, f1=n * CS + CS),
                           start=False, stop=(ci == CPB - 1))
                        nslot = (slot + 1) % 8
                        stt(SRING.v(f0=nslot * 64, f1=nslot * 64 + 64), SRING.v(f0=slot * 64, f1=slot * 64 + 64),
                            V(ach.ap[:, n:n + 1], ach.name, 0, 128, ach.f0 + n, ach.f0 + n + 1),
                            DS.v(f0=n * 64, f1=n * 64 + 64), ALU.mult, ALU.add)
                        slot = nslot
                        st_["slot"] = slot
                        bnd = (n % SEGC == SEGC - 1) if d == 0 else (n % SEGC == 0)
                        if bnd:
                            seg = n // SEGC
                            oi = ((l * 2 + d) * 4 + seg) * 2 + ch
                            sv2 = SRING.v(f0=slot * 64, f1=slot * 64 + 64)
                            p.dma("sync", os_dram[oi], sv2.ap, reads=[sv2])
                            ts(sv2, sv2, flag_s, ALU.mult)
                        cast_state(slot)

                def stage_evac(i):
                    b = blocks[i]
                    acc = ACB[i % 3].v(f1=128)
                    ov = tO.v(f0=b * 128, f1=b * 128 + 128)
                    if d == 0:
                        cp(ov, acc, eng="scalar")
                    else:
                        tt(ov, ov, acc, ALU.add)

                for step in range(NB + 2):
                    if step < NB:
                        stage_scores(step)
                        if step + 1 < NB:
                            ds_front(blocks[step + 1])
                        ds_back(blocks[step])
                    if 1 <= step <= NB:
                        stage_main(step - 1)
                    if step >= 2:
                        stage_evac(step - 2)
                    if d == 0 and pending:
                        for _ in range(2 if len(pending) > NB + 1 - step else 1):
                            if pending:
                                pending.pop(0)()
            o64 = cst_v("o64", 0, 128)
            hb = [PMS, PX]
            for h in range(2):
                act(hv2(tTMP, h), hv2(tO, h), AF.Square)
            for h in range(2):
                mm(hb[h].v(), o64, hv2(tTMP, h))
                act(hv2(tCUM, h), hb[h].v(), AF.Ln, bias=prm_eps)
            for h in range(2):
                act(hv2(tCUM, h), hv2(tCUM, h), AF.Exp, scale=-0.5)
            for h in range(2):
                tt(hv2(tO, h), hv2(tO, h), hv2(tCUM, h), ALU.mult)
            for h in range(2):
                stt(OM.v(f0=ch * 1024 + h * 512, f1=ch * 1024 + h * 512 + 512).bc(), hv2(tO, h), gain_v, hv2(tG, h), ALU.mult, ALU.mult)

        def wout_partial(l):
            sl = next_slot()
            for fo in range(8):
                for half in range(2):
                    P = [PX, PSC, PAC, PDS][state["rb2"] % 4]; state["rb2"] += 1
                    for kk in range(2):
                        mm(P.v(), sl.v(f0=kk * 1024 + fo * 128, f1=kk * 1024 + fo * 128 + 128),
                           OM.v(f0=kk * 1024 + half * 512, f1=kk * 1024 + half * 512 + 512).bc(), start=(kk == 0), stop=(kk == 1))
                    xv = X.v(f0=fo * 1024 + half * 512, f1=fo * 1024 + half * 512 + 512)
                    stt(xv, P.v(), MOD.v(f0=16 + fo, f1=17 + fo), xv, ALU.mult, ALU.add)

        one11 = cst_v("one", 0, 1, p0=0, p1=1)

        def ada_slot(g, lay, stg=None, srep=None, trb=None, tro=256):
            stg = DS if stg is None else stg
            srep = SREP if srep is None else srep
            trb = PTR if trb is None else trb
            sl = next_slot()
            for k in range(8):
                mm(PMS.v(), srep.v(f0=k * 128, f1=k * 128 + 128), sl_k(sl, k, 0, 512), start=(k == 0), stop=(k == 7))
            cp(stg.v(p0=0, p1=1, f0=0, f1=512), PMS.v(p0=0, p1=1), eng="scalar")
            for j in range(4):
                mm(trb.v(f0=tro + 4 * g + j, f1=tro + 1 + 4 * g + j), stg.v(p0=0, p1=1, f0=j * 128, f1=j * 128 + 128), one11)
            o = (lay % 2) * 48
            cp(MODN.v(f0=o + 4 * g, f1=o + 4 * g + 4), trb.v(f0=tro + 4 * g, f1=tro + 4 + 4 * g))

        def mod_commit(lay, c0, c1):
            o = (lay % 2) * 48
            tt(MOD.v(f0=c0, f1=c1), MODN.v(f0=o + c0, f1=o + c1), prm_v("adab%d" % lay, c0, c1 - c0), ALU.add)

        for l in range(L):
            if l == 0:
                make_srep()
                rms_stats()
                for g in range(4):
                    ada_slot(g, 0)
                mod_commit(0, 0, 16)
            else:
                mod_commit(l, 0, 48)
                stt(GS.v(f0=8, f1=16), MOD.v(f0=32, f1=40), 1.0, prm_v("n2g%d" % l, 0, 8), ALU.add, ALU.mult)
            stt(GS.v(f0=0, f1=8), MOD.v(f0=8, f1=16), 1.0, prm_v("n1g%d" % l, 0, 8), ALU.add, ALU.mult)
            LB = SM.v(f0=48, f1=52); OML = SM.v(f0=52, f1=56)
            if l == 0:
                memset(LB, 0.0, eng="vector")
            else:
                tt(LB, prm_v("lbz1", 0, 4), prm_v("lbz0", 0, 4), ALU.subtract)
                act(LB, LB, AF.Sigmoid)
            ts(OML, LB, -1.0, ALU.mult, 1.0, ALU.add)
            C8 = SM.v(f0=56, f1=60)
            act(C8, prm_v("rglam%d" % l, 0, 4), AF.Exp, scale=-1.0)
            act(C8, C8, AF.Ln, bias=1.0)
            ts(C8, C8, -8.0, ALU.mult)
            NBA = SM.v(f0=60, f1=64)
            ts(NBA, prm_v("ba2%d" % l, 0, 4), -1.0, ALU.mult)

            if l > 0:
                rms_stats()
            rms_apply(lambda c: GS.v(f0=c, f1=c + 1), lambda c: MOD.v(f0=c, f1=c + 1))

            slD1 = next_slot()
            TB, TC, TV, U, Y = tQ, tKF, tLAF, tCUM, DS.sub(0, 1024)
            scw = lambda cc, j: prm_v("scw%d" % l, cc * 3 + j)
            WP = SM.v(f0=9, f1=11)

            def sconv(cc):
                ts(WP.w(WP.ap[:, 0:1]), scw(cc, 0), flag_p, ALU.mult)
                ts(WP.w(WP.ap[:, 1:2]), scw(cc, 2), flag_p, ALU.mult)
                tt(U.v(), TC.v(), TV.v(), ALU.mult)
                ts(Y.v(), U.v(), scw(cc, 1), ALU.mult)
                y3 = Y.v().r("p (s j) -> p s j", j=64); u3 = U.v().r("p (s j) -> p s j", j=64)
                w0v = scw(cc, 0); w2v = scw(cc, 2)
                p.op("vector", lambda E: E.scalar_tensor_tensor(out=y3.ap[:, :, 1:64], in0=u3.ap[:, :, 0:63], scalar=w0v.ap,
                                                                in1=y3.ap[:, :, 1:64], op0=ALU.mult, op1=ALU.add),
                     reads=[U.v(), Y.v(), w0v], writes=[Y.v()])
                p.op("vector", lambda E: E.scalar_tensor_tensor(out=y3.ap[:, :, 0:63], in0=u3.ap[:, :, 1:64], scalar=w2v.ap,
                                                                in1=y3.ap[:, :, 0:63], op0=ALU.mult, op1=ALU.add),
                     reads=[U.v(), Y.v(), w2v], writes=[Y.v()])
                y4 = Y.v().r("p (a b j) -> p a b j", a=4, b=4); u4 = U.v().r("p (a b j) -> p a b j", a=4, b=4)
                p.op("vector", lambda E: E.scalar_tensor_tensor(out=y4.ap[:, :, 1:4, 0:1], in0=u4.ap[:, :, 0:3, 63:64], scalar=WP.ap[:, 0:1],
                                                                in1=y4.ap[:, :, 1:4, 0:1], op0=ALU.mult, op1=ALU.add),
                     reads=[U.v(), Y.v(), WP], writes=[Y.v()])
                p.op("vector", lambda E: E.scalar_tensor_tensor(out=y4.ap[:, :, 0:3, 63:64], in0=u4.ap[:, :, 1:4, 0:1], scalar=WP.ap[:, 1:2],
                                                                in1=y4.ap[:, :, 0:3, 63:64], op0=ALU.mult, op1=ALU.add),
                     reads=[U.v(), Y.v(), WP], writes=[Y.v()])
                tt(OM.v(f0=cc * 1024, f1=cc * 1024 + 1024).bc(), TB.v(), Y.v(), ALU.mult)

            def ev_copy(tile):
                return lambda pv, half: cp(tile.v(f0=half * 512, f1=half * 512 + 512), pv, eng="scalar")

            proj_chunk(slD1, 0, ev_copy(TB)); proj_chunk(slD1, 1, ev_copy(TC)); proj_chunk(slD1, 2, ev_copy(TV))
            sconv(0)
            proj_chunk(slD1, 3, ev_copy(TB))
            slD2 = next_slot()
            for i in range(8):
                cp(AUXC.v(f0=i * 128, f1=i * 128 + 128), sl_k(slD2, i, 256, 128).bc(F32))
            proj_chunk(slD2, 0, ev_copy(TC)); proj_chunk(slD2, 1, ev_copy(TV))
            sconv(1)
            if l == 0:
                for g in (4, 5):
                    ada_slot(g, 0, tKH)
                mod_commit(0, 16, 24)
            wout_partial(l)

            slC = next_slot()
            Ux = [tQ, tKF]; GG = [tLAF, tG]
            for cc in range(2):
                proj_chunk(slC, cc, ev_copy(Ux[cc]))
            for cc in range(2):
                proj_chunk(slC, 2 + cc, lambda pv, half, cc=cc: act(GG[cc].v(f0=half * 512, f1=half * 512 + 512), pv, AF.Gelu_apprx_tanh))
            XC, R, IG, HS = tCUM, DS.sub(0, 1024), DS.sub(1024, 1024), tKH
            GB = [PMS, PX, PSC, PAC]

            def hv_(t, h):
                return t.v(f0=h * 512, f1=h * 512 + 512)

            for cc in range(2):
                u = Ux[cc]
                for d in range(2):
                    di = d * 2 + cc
                    cw = [prm_v("rgcw%d" % l, di * 4 + j) for j in range(4)]
                    cb = prm_v("rgcb%d" % l, di)
                    c8 = SM.v(f0=56 + di, f1=57 + di)
                    WS = SM.v(f0=11, f1=15)
                    for j in range(4):
                        ts(WS.w(WS.ap[:, j:j + 1]), cw[j], flag_s, ALU.mult)
                    HSd = tO if d == 0 else HS
                    horder = [0, 1] if d == 0 else [1, 0]
                    x3 = XC.v().r("p (a j) -> p a j", j=256); u3 = u.v().r("p (a j) -> p a j", j=256)

                    def conv(h):
                        xh = hv_(XC, h)
                        ts(xh, hv_(u, h), cw[3], ALU.mult, cb, ALU.add)
                        a0, a1 = 2 * h, 2 * h + 2
                        for s in range(1, 4):
                            if d == 0:
                                o1, i1 = x3.ap[:, a0:a1, s:256], u3.ap[:, a0:a1, 0:256 - s]
                                c0 = max(a0, 1)
                                o2, i2 = x3.ap[:, c0:a1, 0:s], u3.ap[:, c0 - 1:a1 - 1, 256 - s:256]
                            else:
                                o1, i1 = x3.ap[:, a0:a1, 0:256 - s], u3.ap[:, a0:a1, s:256]
                                c1 = min(a1, 3)
                                o2, i2 = x3.ap[:, a0:c1, 256 - s:256], u3.ap[:, a0 + 1:c1 + 1, 0:s]
                            cws = cw[3 - s]
                            p.op("vector", lambda E, o1=o1, i1=i1, cws=cws: E.scalar_tensor_tensor(out=o1, in0=i1, scalar=cws.ap, in1=o1,
                                                                                      op0=ALU.mult, op1=ALU.add),
                                 reads=[u.v(), xh, cws], writes=[xh])
                            p.op("vector", lambda E, o2=o2, i2=i2, s=s: E.scalar_tensor_tensor(out=o2, in0=i2, scalar=WS.ap[:, 3 - s:4 - s], in1=o2,
                                                                                      op0=ALU.mult, op1=ALU.add),
                                 reads=[u.v(), xh, WS], writes=[xh])

                    def gates(h):
                        for wh, (dst, bname) in enumerate(((R, "rgbr%d" % l), (IG, "rgbi%d" % l))):
                            wm = AUXC.v(f0=(di * 2 + wh) * 128, f1=(di * 2 + wh) * 128 + 128)
                            gb = GB[h * 2 + wh]
                            mm(gb.v(), wm, hv_(XC, h))
                            act(hv_(dst, h), gb.v(), AF.Sigmoid, bias=prm_v(bname, di))

                    def scans(h):
                        segs = [2 * h, 2 * h + 1] if d == 0 else [2 * h + 1, 2 * h]
                        for sg in segs:
                            first = (sg == 0) if d == 0 else (sg == 3)
                            if first:
                                init = prm_v("str%d" % l, di)
                            else:
                                prev = HSd.v(f0=sg * 256 - 1, f1=sg * 256) if d == 0 else HSd.v(f0=sg * 256 + 256, f1=sg * 256 + 257)
                                init = ACH.v(f0=sg, f1=sg + 1)
                                ts(init, prev, flag_s, ALU.mult)
                            hv = HSd.v(f0=sg * 256, f1=sg * 256 + 256); av = R.v(f0=sg * 256, f1=sg * 256 + 256); bv = IG.v(f0=sg * 256, f1=sg * 256 + 256)
                            if d == 0:
                                p.op("vector", lambda E, hv=hv, av=av, bv=bv, init=init: E.tensor_tensor_scan(
                                    out=hv.ap, data0=av.ap, data1=bv.ap, initial=init.ap, op0=ALU.mult, op1=ALU.add),
                                    reads=[av, bv, init], writes=[hv])
                            else:
                                p.op("vector", lambda E, hv=hv, av=av, bv=bv, init=init: E.tensor_tensor_scan(
                                    out=hv.ap[:, ::-1], data0=av.ap[:, ::-1], data1=bv.ap[:, ::-1], initial=init.ap, op0=ALU.mult, op1=ALU.add),
                                    reads=[av, bv, init], writes=[hv])

                    for h in horder:
                        conv(h)
                    for h in horder:
                        gates(h)
                    for h in horder:
                        act(hv_(R, h), hv_(R, h), AF.Exp, scale=c8)
                    for h in horder:
                        tt(hv_(tTMP, h), hv_(R, h), hv_(R, h), ALU.mult)
                    for h in horder:
                        act(hv_(tTMP, h), hv_(tTMP, h), AF.Ln, bias=1.0, scale=-1.0)
                    for h in horder:
                        act(hv_(tTMP, h), hv_(tTMP, h), AF.Exp, scale=0.5)
                    for h in horder:
                        tt(hv_(IG, h), hv_(IG, h), hv_(tTMP, h), ALU.mult)
                    for h in horder:
                        tt(hv_(IG, h), hv_(IG, h), hv_(XC, h), ALU.mult)
                    for h in horder:
                        scans(h)
                        if d == 1:
                            tt(hv_(tO, h), hv_(tO, h), hv_(HS, h), ALU.add)
                    oc = ((l * 2 + d) * 2 + cc) * 4
                    h4 = HSd.v().r("p (a j) -> p a j", j=256)
                    fin = h4.w(h4.ap[:, :, 255:256] if d == 0 else h4.ap[:, :, 0:1])
                    o4 = OSR.v(f0=oc, f1=oc + 4)
                    p.op("vector", lambda E, o4=o4, fin=fin: E.tensor_copy(o4.ap.rearrange("p (a o) -> p a o", o=1), fin.ap),
                         reads=[HSd.v()], writes=[o4])
                tt(OM.v(f0=cc * 1024, f1=cc * 1024 + 1024).bc(), tO.v(), GG[cc].v(), ALU.mult)
            wout_partial(l)

            slBv = next_slot()
            proj_vtok(slBv, 0)
            for i in range(4):
                cp(WA2T.v(f0=i * 128, f1=i * 128 + 128), sl_k(slBv, i, 256, 128).bc(F32))
            sc = 32 ** -0.5
            for pair in range(2):
                slB = next_slot()
                proj_chunk(slB, 0, lambda pv, half: act(tQ.v(f0=half * 512, f1=half * 512 + 512), pv, AF.Identity, scale=sc))
                proj_chunk(slB, 1, ev_copy(tKF))
                proj_chunk(slB, 2, lambda pv, half: act(tG.v(f0=half * 512, f1=half * 512 + 512), pv, AF.Silu))
                proj_chunk(slB, 3, ev_copy(AUXC))

                def gla_hook(d, pair=pair):
                    i = d * 2 + pair
                    th = []
                    for half in range(2):
                        def f(half=half, i=i):
                            mm(PMS.v(), WA2T.v(f0=i * 128, f1=i * 128 + 128), AUXC.v(f0=half * 512, f1=half * 512 + 512))
                            act(tLAF.v(f0=half * 512, f1=half * 512 + 512), PMS.v(), AF.Exp, bias=SM.v(f0=60 + i, f1=61 + i), scale=-1.0)
                        th.append(f)
                    for h in range(2):
                        th.append(lambda h=h: act(hv2(tLAF, h), hv2(tLAF, h), AF.Ln, bias=1.0))
                    for h in range(2):
                        th.append(lambda h=h: ts(hv2(tLAF, h), hv2(tLAF, h), -1.0 / 16.0, ALU.mult))
                    return th
                gla_core(1, l, pair, [gla_hook(0), gla_hook(1)], prm_v("ggain%d" % l, pair), osB, CS=128)
                if l == 0:
                    for g in (6 + 2 * pair, 7 + 2 * pair):
                        ada_slot(g, 0)
            wout_partial(l)

            slAv = next_slot()
            proj_vtok(slAv, 0)
            for pair in range(2):
                slA = next_slot()
                proj_chunk(slA, 0, ev_copy(tQ))
                proj_chunk(slA, 1, lambda pv, half: act(tLAF.v(f0=half * 512, f1=half * 512 + 512), pv, AF.Sigmoid))
                proj_chunk(slA, 2, lambda pv, half: act(AUXC.v(f0=half * 512, f1=half * 512 + 512), pv, AF.Sigmoid))
                proj_chunk(slA, 3, lambda pv, half: act(tG.v(f0=half * 512, f1=half * 512 + 512), pv, AF.Silu))

                def hg_hook(d, pair=pair):
                    i = d * 2 + pair
                    src = tLAF if d == 0 else AUXC
                    th = []
                    for h in range(2):
                        th.append(lambda h=h: ts(hv2(tLAF, h), hv2(src, h), SM.v(f0=52 + i, f1=53 + i), ALU.mult, SM.v(f0=48 + i, f1=49 + i), ALU.add))
                    for h in range(2):
                        th.append(lambda h=h: ts(hv2(tKF, h), hv2(tLAF, h), -1.0, ALU.mult, 1.0, ALU.add))
                    for h in range(2):
                        th.append(lambda h=h: ts(hv2(tLAF, h), hv2(tLAF, h), 1e-20, ALU.max))
                    for h in range(2):
                        th.append(lambda h=h: act(hv2(tLAF, h), hv2(tLAF, h), AF.Ln))
                    return th
                gla_core(0, l, pair, [hg_hook(0), hg_hook(1)], prm_v("hgain%d" % l, pair), osA)
                if l == 0 and pair == 0:
                    for g in (10, 11):
                        ada_slot(g, 0)
                    mod_commit(0, 24, 48)
                    stt(GS.v(f0=8, f1=16), MOD.v(f0=32, f1=40), 1.0, prm_v("n2g%d" % l, 0, 8), ALU.add, ALU.mult)
            wout_partial(l)

            rmsnorm_to_H(lambda c: GS.v(f0=8 + c, f1=9 + c), lambda c: MOD.v(f0=24 + c, f1=25 + c))
            W1B = [PJ[0], PJ[1], PTR, PJ[0]]
            W2B = [PX, PSC, PAC, PX]
            if l + 1 < L:
                make_srep(tQT)
            for g in range(8):
                sl = next_slot()
                for jj in range(4):
                    for half in range(2):
                        P = W1B[state["rb"] % 3]; state["rb"] += 1
                        for k in range(8):
                            mm(P.v(), sl_k(sl, k, jj * 128, 128), Hk(k, half * 512, 512), start=(k == 0), stop=(k == 7))
                        hv = HID.v(f0=jj * 1024 + half * 512, f1=jj * 1024 + half * 512 + 512)
                        sq = SQT.v(f0=(state["rb"] % 2) * 512, f1=(state["rb"] % 2) * 512 + 512)
                        act(sq, P.v(), AF.Square)
                        stt(hv, P.v(), 0.0, sq, ALU.is_gt, ALU.mult)
                sl = next_slot()
                for fo in range(8):
                    for half in range(2):
                        P = W2B[state["rb2"] % 3]; state["rb2"] += 1
                        for j in range(4):
                            mm(P.v(), sl.v(f0=j * 1024 + fo * 128, f1=j * 1024 + fo * 128 + 128),
                               HID.v(f0=j * 1024 + half * 512, f1=j * 1024 + half * 512 + 512), start=(j == 0), stop=(j == 3))
                        xv = X.v(f0=fo * 1024 + half * 512, f1=fo * 1024 + half * 512 + 512)
                        stt(xv, P.v(), MOD.v(f0=40 + fo, f1=41 + fo), xv, ALU.mult, ALU.add)
                if l + 1 < L:
                    for gg in range(ADA_G0[g], ADA_G0[g] + ADA_CNT[g]):
                        ada_slot(gg, l + 1, stg=tCUM, srep=tQT, trb=PDS, tro=0)

        rmsnorm_to_H(lambda c: prm_v("fng", c), None)
        p.dma("sync", osr, OSR.v().ap, reads=[OSR.v()])
        p.finish("sync")
        with nc.Block() as block:
            p.emit(block)
    return nc


def kernel(**inp):
    inp = {k: np.asarray(v) for k, v in inp.items()}
    wst = build_stream(inp)
    co = build_consts()
    cst = co.get()
    pos = [build_params(inp, c) for c in range(8)]
    nc = build_program(pos[0], co, wst.shape[0])
    ones = np.ones((128, 128), np.float32)
    in_maps = []
    for c in range(8):
        if c < 4:
            x = inp["x_sample"][c]
        else:
            x = inp["x_prompt"][4 * (c - 4):4 * (c - 4) + 4].reshape(1024, 1024)
        in_maps.append({"xT": np.ascontiguousarray(x.T), "wst": wst, "prm": pos[c].get(), "cst": cst,
                        "st0": build_st0(inp, c), "onesd": ones})
    res = run_bass_kernel_spmd(nc, in_maps, core_ids=list(range(8)))
    R = res.results
    y_sample = np.stack([R[c]["yT"].T for c in range(4)], 0).astype(np.float32)
    y_prompt = np.concatenate([R[c]["yT"].T.reshape(4, 256, 1024) for c in range(4, 8)], 0).astype(np.float32)
    nh = np.zeros((16, L, 2, 4, 64, 64), np.float32)
    ng = np.zeros((16, L, 2, 4, 32, 64), np.float32)
    nr = np.zeros((16, L, 2, 256), np.float32)
    for c in range(4, 8):
        oa, ob, orr = R[c]["osA"], R[c]["osB"], R[c]["osr"]
        for seg in range(4):
            bi = 4 * (c - 4) + seg
            for l in range(L):
                for d in range(2):
                    for ch in range(2):
                        oi = ((l * 2 + d) * 4 + seg) * 2 + ch
                        for hh in range(2):
                            nh[bi, l, d, 2 * ch + hh] = oa[oi, hh * 64:hh * 64 + 64, :]
                            ng[bi, l, d, 2 * ch + hh] = ob[oi, hh * 64:hh * 64 + 32, :]
                    for cc in range(2):
                        oc = ((l * 2 + d) * 2 + cc) * 4 + seg
                        nr[bi, l, d, cc * 128:(cc + 1) * 128] = orr[:, oc]
    return (y_prompt, y_sample, nh, ng, nr)
```

```python
import numpy as np
from contextlib import ExitStack
import concourse.bass as bass
import concourse.mybir as mybir
from concourse.bass_utils import run_bass_kernel_spmd

F32 = mybir.dt.float32
F32R = mybir.dt.float32r
AF = mybir.ActivationFunctionType
ALU = mybir.AluOpType
ENGS = ("tensor", "vector", "scalar", "gpsimd", "sync")

D = 1024; L = 2; T = 1024; NB = 8; NCH = 32
PROJ_W = 3360


class V:
    def __init__(self, ap, name, p0, p1, f0, f1):
        self.ap, self.name, self.p0, self.p1, self.f0, self.f1 = ap, name, p0, p1, f0, f1

    def reg(self):
        return (self.name, self.p0, self.p1, self.f0, self.f1)

    def w(self, ap):
        return V(ap, self.name, self.p0, self.p1, self.f0, self.f1)

    def r(self, pattern, **kw):
        return self.w(self.ap.rearrange(pattern, **kw))

    def bc(self, dt=F32R):
        return self.w(self.ap.bitcast(dt))


class TL:
    def __init__(self, handle, name, P, F, base=0):
        self.h, self.name, self.P, self.F, self.base = handle, name, P, F, base

    def v(self, p0=0, p1=None, f0=0, f1=None):
        p1 = self.P if p1 is None else p1
        f1 = self.F if f1 is None else f1
        return V(self.h[p0:p1, self.base + f0:self.base + f1], self.name, p0, p1, self.base + f0, self.base + f1)

    def sub(self, f0, F):
        return TL(self.h, self.name, self.P, F, self.base + f0)


def _ov(a, b):
    return a[0] == b[0] and a[1] < b[2] and b[1] < a[2] and a[3] < b[4] and b[3] < a[4]


def _ct(a, b):
    return a[0] == b[0] and a[1] <= b[1] and b[2] <= a[2] and a[3] <= b[3] and b[4] <= a[4]


class Prog:
    def __init__(self, nc, es, n_dma_sems=12):
        self.nc = nc
        self.ops = {e: [] for e in ENGS}
        self.cnt = {e: 0 for e in ENGS}
        self.sems = {e: es.enter_context(nc.semaphore("s_" + e)) for e in ENGS}
        self.dma_sems = [es.enter_context(nc.semaphore("d%d" % i)) for i in range(n_dma_sems)]
        self.es = es
        self.nfixed = n_dma_sems
        self.dma_uses = [0] * n_dma_sems
        self.dma_next = 0
        self.waited = {e: {} for e in ENGS}
        self.recs = {}

    def _deps(self, eng, reads, writes):
        deps = []
        for v in reads:
            rg = v.reg()
            for (r2, kind, sk, val, e2) in self.recs.get(rg[0], ()):
                if kind == "w" and _ov(rg, r2):
                    if e2 == eng and eng == "tensor":
                        continue
                    deps.append((sk, val))
        for v in writes:
            rg = v.reg()
            if eng == "tensor":
                rg = (rg[0], rg[1], rg[2], 0, 1 << 30)
            for (r2, kind, sk, val, e2) in self.recs.get(rg[0], ()):
                if _ov(rg, r2):
                    if e2 == eng and eng == "tensor":
                        continue
                    deps.append((sk, val))
        return deps

    def _record(self, eng, reads, writes, sk, val):
        for v in writes:
            rg = v.reg()
            lst = self.recs.setdefault(rg[0], [])
            lst[:] = [x for x in lst if not _ct(rg, x[0])]
            lst.append((rg, "w", sk, val, eng))
        for v in reads:
            rg = v.reg()
            lst = self.recs.setdefault(rg[0], [])
            lst[:] = [x for x in lst if not (x[1] == "r" and x[2] == sk and _ct(rg, x[0]))]
            lst.append((rg, "r", sk, val, eng))

    def _sem(self, sk):
        return self.sems[sk] if isinstance(sk, str) else self.dma_sems[sk]

    def _emit_waits(self, eng, deps):
        best = {}
        for sk, val in deps:
            if val > best.get(sk, 0):
                best[sk] = val
        for sk, val in best.items():
            if val > self.waited[eng].get(sk, 0):
                self.waited[eng][sk] = val
                sem = self._sem(sk)
                self.ops[eng].append(lambda E, sem=sem, val=val: E.wait_ge(sem, val))

    def op(self, eng, fn, reads=(), writes=()):
        self._emit_waits(eng, self._deps(eng, reads, writes))
        self.cnt[eng] += 1
        sem = self.sems[eng]
        self.ops[eng].append(lambda E, fn=fn, sem=sem: fn(E).then_inc(sem, 1))
        self._record(eng, reads, writes, eng, self.cnt[eng])

    def dma(self, eng, out_ap, in_ap, reads=(), writes=()):
        if eng == "gpsimd":
            deps = self._deps(eng, reads, writes)
            self._emit_waits(eng, deps)
            sem = self.es.enter_context(self.nc.semaphore("w%d" % len(self.dma_sems)))
            self.dma_sems.append(sem)
            self.dma_uses.append(1)
            slot = len(self.dma_sems) - 1
            self.ops[eng].append(lambda E, o=out_ap, i=in_ap, sem=sem: E.dma_start(out=o, in_=i).then_inc(sem, 16))
            self._record("dma", reads, writes, slot, 16)
            return
        slot = self.dma_next
        self.dma_next = (self.dma_next + 1) % self.nfixed
        deps = self._deps(eng, reads, writes)
        if self.dma_uses[slot] > 0:
            deps.append((slot, 16 * self.dma_uses[slot]))
        self._emit_waits(eng, deps)
        self.dma_uses[slot] += 1
        sem = self.dma_sems[slot]
        self.ops[eng].append(lambda E, o=out_ap, i=in_ap, sem=sem: E.dma_start(out=o, in_=i).then_inc(sem, 16))
        self._record("dma", reads, writes, slot, 16 * self.dma_uses[slot])

    def finish(self, eng="sync"):
        self._emit_waits(eng, [(s, 16 * u) for s, u in enumerate(self.dma_uses) if u > 0])

    def emit(self, block):
        for e in ENGS:
            ops = self.ops[e]
            if not ops:
                continue

            def body(E, ops=ops):
                for o in ops:
                    o(E)

            getattr(block, e)(body)


class Packer:
    def __init__(self):
        self.cols = []
        self.off = {}
        self.n = 0

    def add(self, name, arr):
        arr = np.asarray(arr, np.float32)
        assert arr.shape[0] == 128
        self.off[name] = self.n
        self.cols.append(arr)
        self.n += arr.shape[1]

    def get(self):
        return np.ascontiguousarray(np.concatenate(self.cols, axis=1))


def fm(vec):
    return np.asarray(vec, np.float32).reshape(-1, 128).T


def kslot(mat):
    out = np.zeros((128, 8, 512), np.float32)
    out[:, :, :mat.shape[1]] = mat.reshape(8, 128, mat.shape[1]).transpose(1, 0, 2)
    return out


def build_stream(inp):
    slots = []
    def ada_slots(l, g0, n):
        if l >= L:
            return
        aw = inp["ada_w"][l]
        for g in range(g0, g0 + n):
            slots.append(kslot(aw[:, 512 * g:512 * g + 512]).reshape(128, 4096))

    for l in range(L):
        if l == 0:
            ada_slots(0, 0, 4)
        wi = inp["w_in"][l]; wo = inp["w_out"][l]

        def chunks(cols_list):
            s = np.zeros((128, 8, 512), np.float32)
            for i, m in enumerate(cols_list):
                if m is not None:
                    s[:, :, i * 128:(i + 1) * 128] = m.reshape(8, 128, 128).transpose(1, 0, 2)
            return s

        def wcols(c0, n=128):
            m = np.zeros((1024, 128), np.float32)
            m[:, :n] = wi[:, c0:c0 + n]
            return m

        def woslot(r0):
            s = np.zeros((128, 4096), np.float32)
            s[:, :2048] = wo[r0:r0 + 256].reshape(2, 128, 1024).transpose(1, 0, 2).reshape(128, 2048)
            return s

        dB, dC, dV = 2592, 2848, 3104
        slots.append(chunks([wcols(dB), wcols(dC), wcols(dV), wcols(dB + 128)]).reshape(128, 4096))
        s2 = chunks([wcols(dC + 128), wcols(dV + 128), None, None])
        for d in range(2):
            for cc in range(2):
                for wh, w in enumerate((inp["rg_w_r"][l], inp["rg_w_i"][l])):
                    i = (d * 2 + cc) * 2 + wh
                    bd = np.zeros((128, 128), np.float32)
                    for nn in range(2):
                        bd[nn * 64:(nn + 1) * 64, nn * 64:(nn + 1) * 64] = w[d, 2 * cc + nn]
                    s2[:, i, 256:384] = bd
        slots.append(s2.reshape(128, 4096))
        if l == 0:
            ada_slots(0, 4, 2)
        slots.append(woslot(768))
        cX, cG = 2080, 2336
        slots.append(chunks([wcols(cX), wcols(cX + 128), wcols(cG), wcols(cG + 128)]).reshape(128, 4096))
        slots.append(woslot(512))
        def padqk(c0, ch):
            m = np.zeros((1024, 128), np.float32)
            for hh in range(2):
                h = 2 * ch + hh
                m[:, hh * 64:hh * 64 + 32] = wi[:, c0 + h * 32:c0 + h * 32 + 32]
            return m
        sv = chunks([wcols(1536), wcols(1664), None, None])
        for d in range(2):
            for ch in range(2):
                m = np.zeros((128, 128), np.float32)
                for hh in range(2):
                    h = 2 * ch + hh
                    m[d * 16:(d + 1) * 16, hh * 64:hh * 64 + 32] = inp["gla_wa2"][l, d][:, h * 32:(h + 1) * 32]
                sv[:, d * 2 + ch, 256:384] = m
        slots.append(sv.reshape(128, 4096))
        for ch in range(2):
            slots.append(chunks([padqk(1280, ch), padqk(1408, ch), wcols(1792 + ch * 128), wcols(2048, 32)]).reshape(128, 4096))
            if l == 0:
                ada_slots(0, 6 + 2 * ch, 2)
            ada_slots(l + 1, 3 * ch, 3)
        slots.append(woslot(256))
        slots.append(chunks([wcols(256), wcols(384), None, None]).reshape(128, 4096))
        for ch in range(2):
            slots.append(chunks([wcols(0 + ch * 128), wcols(512 + ch * 128), wcols(768 + ch * 128),
                                 wcols(1024 + ch * 128)]).reshape(128, 4096))
            if l == 0 and ch == 0:
                ada_slots(0, 10, 2)
            ada_slots(l + 1, 6 + 3 * ch, 3)
        slots.append(woslot(0))
        w1 = inp["mlp_w1"][l]; w2 = inp["mlp_w2"][l]
        for g in range(8):
            slots.append(kslot(w1[:, g * 512:(g + 1) * 512]).reshape(128, 4096))
            slots.append(np.ascontiguousarray(
                w2[g * 512:(g + 1) * 512].reshape(4, 128, 1024).transpose(1, 0, 2)).reshape(128, 4096))
    return np.ascontiguousarray(np.stack(slots, 0))


def build_consts():
    pk = Packer()
    s = np.arange(128)[:, None]; t = np.arange(128)[None, :]
    same = (s // 32) == (t // 32)
    pk.add("mf", (same & (s <= t)).astype(np.float32))
    pk.add("mb", (same & (s >= t)).astype(np.float32))
    pk.add("mf128", (s <= t).astype(np.float32))
    pk.add("mb128", (s >= t).astype(np.float32))
    pk.add("ident", np.eye(128, dtype=np.float32))
    o64 = np.zeros((128, 128), np.float32); o64[:64, :64] = 1 / 64.; o64[64:, 64:] = 1 / 64.
    pk.add("o64", o64)
    pk.add("bm", ((np.arange(128)[:, None] // 32) == np.arange(4)[None, :]).astype(np.float32))
    mFB = np.ones((128, 1025), np.float32); mFB[:, 0::32] = 0
    pk.add("mFB", mFB)
    pk.add("one", np.ones((128, 1), np.float32))
    return pk


def build_params(inp, core):
    pk = Packer()
    samp = core < 4
    crow = inp["c"][core] if samp else inp["c_ctx"]
    pk.add("cvec", fm(crow))
    fl = np.zeros((128, 2), np.float32); fl[:, 0] = 1.0 if samp else 0.0; fl[:, 1] = 0.0 if samp else 1.0
    pk.add("flags", fl)
    pk.add("fng", fm(inp["final_norm_g"]))
    for l in range(L):
        pk.add("adab%d" % l, fm(inp["ada_b"][l]))
        pk.add("n1g%d" % l, fm(inp["norm1_g"][l]))
        pk.add("n2g%d" % l, fm(inp["norm2_g"][l]))
        pk.add("lbz%d" % l, fm(inp["hgrn_lb_logits"][l].reshape(-1)))
        pk.add("hgain%d" % l, fm(inp["hgrn_norm_g"][l]))
        pk.add("ggain%d" % l, fm(inp["gla_norm_g"][l]))
        b = np.zeros((128, 4), np.float32)
        for d in range(2):
            for ch in range(2):
                for hh in range(2):
                    h = 2 * ch + hh
                    b[hh * 64:hh * 64 + 32, d * 2 + ch] = inp["gla_ba2"][l, d, h * 32:(h + 1) * 32]
        pk.add("ba2%d" % l, b)
        cw = np.zeros((128, 16), np.float32)
        for d in range(2):
            for cc in range(2):
                for j in range(4):
                    cw[:, (d * 2 + cc) * 4 + j] = inp["rg_conv_w"][l, d, j, cc * 128:(cc + 1) * 128]
        pk.add("rgcw%d" % l, cw)
        pk.add("rgcb%d" % l, fm(inp["rg_conv_b"][l].reshape(-1)))
        pk.add("rgbr%d" % l, fm(inp["rg_b_r"][l].reshape(-1)))
        pk.add("rgbi%d" % l, fm(inp["rg_b_i"][l].reshape(-1)))
        pk.add("rglam%d" % l, fm(inp["rg_lambda"][l].reshape(-1)))
        sw = np.zeros((128, 6), np.float32)
        for cc in range(2):
            for j in range(3):
                sw[:, cc * 3 + j] = inp["sconv_w"][l, j, cc * 128:(cc + 1) * 128]
        pk.add("scw%d" % l, sw)
        if samp:
            pk.add("str%d" % l, fm(inp["state_rglru"][core, l].reshape(-1)))
        else:
            pk.add("str%d" % l, np.zeros((128, 4), np.float32))
    return pk


def build_st0(inp, core):
    st = np.zeros((128, 16, 64), np.float32)
    if core < 4:
        for m, key in enumerate(("state_hgrn", "state_gla")):
            s = inp[key][core]
            dk = s.shape[3]
            for l in range(L):
                for d in range(2):
                    for ch in range(2):
                        i = ((m * 2 + l) * 2 + d) * 2 + ch
                        for hh in range(2):
                            st[hh * 64:hh * 64 + dk, i, :] = s[l, d, 2 * ch + hh]
    return st.reshape(128, 1024)


def build_program(po, co, n_slots):
    nc = bass.Bass("TRN2", target_bir_lowering=False)
    xT = nc.dram_tensor("xT", [D, T], F32, kind="ExternalInput").ap()
    wst = nc.dram_tensor("wst", [n_slots, 128, 4096], F32, kind="ExternalInput").ap()
    prm = nc.dram_tensor("prm", [128, po.n], F32, kind="ExternalInput").ap()
    cst = nc.dram_tensor("cst", [128, co.n], F32, kind="ExternalInput").ap()
    st0 = nc.dram_tensor("st0", [128, 1024], F32, kind="ExternalInput").ap()
    onesd = nc.dram_tensor("onesd", [128, 128], F32, kind="ExternalInput").ap()
    yT = nc.dram_tensor("yT", [D, T], F32, kind="ExternalOutput").ap()
    osA = nc.dram_tensor("osA", [32, 128, 64], F32, kind="ExternalOutput").ap()
    osB = nc.dram_tensor("osB", [32, 128, 64], F32, kind="ExternalOutput").ap()
    osr = nc.dram_tensor("osr", [128, 32], F32, kind="ExternalOutput").ap()

    with ExitStack() as es:
        def sb(name, P, Fn, dt=F32):
            return TL(es.enter_context(nc.sbuf_tensor(name, [P, Fn], dt)), name, P, Fn)

        def psum(name):
            return TL(es.enter_context(nc.psum_tensor(name, [128, 512], F32)), name, 128, 512)

        p = Prog(nc, es)
        X = sb("X", 128, 8192); H = sb("H", 128, 8192)
        RING = sb("RING", 128, 8192, F32R)
        W = sb("W", 128, 12288)
        FR = sb("FR", 128, 4096, F32R)
        OM = FR.sub(0, 2048); SREP = FR.sub(2048, 1024); HID = FR.sub(0, 4096)
        PRM = sb("PRM", 128, po.n); CST = sb("CST", 128, co.n)
        ONES = sb("ONES", 128, 128, F32R)
        MOD = sb("MOD", 128, 48); GS = sb("GS", 128, 16)
        SM = sb("SM", 128, 64)
        OSR = sb("OSR", 128, 32)
        MODN = sb("MODN", 128, 96); ACH = sb("ACH", 128, 64)
        WA2T = sb("WA2T", 128, 512)
        PS = [psum("P%d" % i) for i in range(8)]
        PJ = [PS[0], PS[1]]; PMS = PS[2]; PSC = PS[3]; PAC = PS[4]; PDS = PS[5]; PTR = PS[6]; PX = PS[7]

        def wt(i):
            return W.sub(i * 1024, 1024)
        tQ, tKF, tLAF, tG, tO, tCUM, tKH, tTMP = [wt(i) for i in range(8)]
        DS = W.sub(8192, 2048); SRING = W.sub(10240, 1024); AUXC = W.sub(11264, 1024)
        KHT = sb("KHT", 128, 256, F32R); VBD = sb("VBD", 128, 1024, F32R)
        tQT = sb("QT", 128, 1024, F32R); tKT = sb("KT", 128, 1024, F32R); PT = sb("PTt", 128, 512, F32R)
        VPAD = sb("VPAD", 128, 4096, F32R); SRR = sb("SRR", 128, 1024, F32R)
        SQT = W.sub(8192, 1024)

        def prm_v(name, c0=0, n=1):
            o = po.off[name] + c0
            return PRM.v(f0=o, f1=o + n)

        def cst_v(name, c0=0, n=1, p0=0, p1=128):
            o = co.off[name] + c0
            return CST.v(p0=p0, p1=p1, f0=o, f1=o + n)

        flag_s = prm_v("flags", 0); flag_p = prm_v("flags", 1)

        def act(out, in_, func, bias=None, scale=None, extra_reads=()):
            kw = {}
            rd = [in_] + list(extra_reads)
            if bias is not None:
                if isinstance(bias, V):
                    kw["bias"] = bias.ap; rd.append(bias)
                else:
                    kw["bias"] = bias
            if scale is not None:
                if isinstance(scale, V):
                    kw["scale"] = scale.ap; rd.append(scale)
                else:
                    kw["scale"] = scale
            p.op("scalar", lambda E: E.activation(out=out.ap, in_=in_.ap, func=func, **kw), reads=rd, writes=[out])

        def tt(out, a, b, op, eng="vector"):
            p.op(eng, lambda E: E.tensor_tensor(out=out.ap, in0=a.ap, in1=b.ap, op=op), reads=[a, b], writes=[out])

        def ts(out, a, s1, op0, s2=None, op1=None, eng="vector"):
            rd = [a]
            a1 = s1.ap if isinstance(s1, V) else s1
            if isinstance(s1, V): rd.append(s1)
            kw = {"scalar2": None}
            if op1 is not None:
                kw["scalar2"] = s2.ap if isinstance(s2, V) else s2
                kw["op1"] = op1
                if isinstance(s2, V): rd.append(s2)
            p.op(eng, lambda E: E.tensor_scalar(out=out.ap, in0=a.ap, scalar1=a1, op0=op0, **kw), reads=rd, writes=[out])

        def stt(out, a, s, b, op0, op1):
            rd = [a, b]
            sa = s.ap if isinstance(s, V) else s
            if isinstance(s, V): rd.append(s)
            p.op("vector", lambda E: E.scalar_tensor_tensor(out=out.ap, in0=a.ap, scalar=sa, in1=b.ap, op0=op0, op1=op1),
                 reads=rd, writes=[out])

        def cp(out, in_, eng="vector"):
            if eng == "scalar":
                p.op("scalar", lambda E: E.copy(out.ap, in_.ap), reads=[in_], writes=[out])
            else:
                p.op(eng, lambda E: E.tensor_copy(out.ap, in_.ap), reads=[in_], writes=[out])

        def mm(out, lhsT, rhs, start=True, stop=True):
            p.op("tensor", lambda E: E.matmul(out.ap, lhsT.ap, rhs.ap, start=start, stop=stop),
                 reads=[lhsT, rhs], writes=[out])

        def memset(v, val, eng="gpsimd"):
            p.op(eng, lambda E: E.memset(v.ap, val), writes=[v])

        state = {"cur": -1, "issued": 0, "pj": 0, "rb": 0, "rb2": 0}

        def issue(i):
            if i < n_slots and state["issued"] <= i:
                rv = RING.v(f0=(i % 2) * 4096, f1=(i % 2) * 4096 + 4096)
                p.dma("gpsimd", rv.ap, wst[i], writes=[rv])
                state["issued"] = i + 1

        def prefetch2():
            issue(state["cur"] + 2)

        def next_slot():
            state["cur"] += 1
            i = state["cur"]
            issue(i)
            issue(i + 1)
            return RING.sub((i % 2) * 4096, 4096)

        def sl_k(sl, k, c0, n):
            return sl.v(f0=k * 512 + c0, f1=k * 512 + c0 + n)

        def Hk(k, t0, n):
            return H.v(f0=k * 1024 + t0, f1=k * 1024 + t0 + n).bc()

        def proj_chunk(sl, ci, evac):
            for half in range(2):
                P = PJ[state["pj"]]; state["pj"] ^= 1
                for k in range(8):
                    mm(P.v(), sl_k(sl, k, ci * 128, 128), Hk(k, half * 512, 512), start=(k == 0), stop=(k == 7))
                evac(P.v(), half)

        def proj_vtok(sl, ci0):
            for b in range(NB):
                P = PJ[state["pj"]]; state["pj"] ^= 1
                for k in range(8):
                    mm(P.v(f1=256), Hk(k, b * 128, 128), sl_k(sl, k, ci0 * 128, 256), start=(k == 0), stop=(k == 7))
                src = P.v(f1=256).r("p (a h e) -> p a h e", a=2, h=2)
                dst = VPAD.v(f0=b * 512, f1=b * 512 + 512)
                base = dst.ap
                d4 = bass.AP(tensor=base.tensor, offset=base.offset, ap=[base.ap[0], [256, 2], [192, 2], [1, 64]])
                p.op("scalar", lambda E, s=src, d4=d4: E.copy(d4, s.ap), reads=[src], writes=[dst])

        for c in range(8):
            p.dma("sync", X.v(f0=c * 1024, f1=(c + 1) * 1024).ap, xT[c * 128:(c + 1) * 128, :],
                  writes=[X.v(f0=c * 1024, f1=(c + 1) * 1024)])
        p.dma("sync", PRM.v().ap, prm, writes=[PRM.v()])
        p.dma("sync", CST.v().ap, cst, writes=[CST.v()])
        p.dma("gpsimd", ONES.v().ap, onesd, writes=[ONES.v()])
        memset(SRING.v(), 0.0)
        zsrc = cst_v("mFB", 0, 1024)
        for i in range(4):
            ts(VPAD.v(f0=i * 1024, f1=i * 1024 + 1024), zsrc, 0.0, ALU.mult)
        ts(SRR.v(), zsrc, 0.0, ALU.mult)
        memset(OSR.v(), 0.0)

        def make_srep():
            act(SM.v(f0=0, f1=8), prm_v("cvec", 0, 8), AF.Silu)
            srep3 = SREP.v().r("p (k m) -> p k m", k=8)
            sin3 = SM.v(f0=0, f1=8)
            sin_ap = sin3.ap.rearrange("p (k o) -> p k o", o=1).broadcast_to([128, 8, 128])
            p.op("vector", lambda E: E.tensor_copy(srep3.ap, sin_ap), reads=[sin3], writes=[SREP.v()])

        def rmsnorm_to_H(gs_v_fn, shift_fn):
            rms_stats()
            rms_apply(gs_v_fn, shift_fn)

        def rms_stats():
            sb_ = [PMS, PX]
            for half in range(2):
                for c in range(8):
                    act(H.v(f0=c * 1024 + half * 512, f1=c * 1024 + half * 512 + 512).bc(),
                        X.v(f0=c * 1024 + half * 512, f1=c * 1024 + half * 512 + 512), AF.Square)
                for k in range(8):
                    mm(sb_[half].v(), ONES.v(), Hk(k, half * 512, 512), start=(k == 0), stop=(k == 7))
                act(tTMP.v(f0=half * 512, f1=half * 512 + 512), sb_[half].v(), AF.Ln, bias=prm_eps, scale=1.0 / D)
                act(tTMP.v(f0=half * 512, f1=half * 512 + 512), tTMP.v(f0=half * 512, f1=half * 512 + 512), AF.Exp, scale=-0.5)

        def rms_apply(gs_v_fn, shift_fn):
            for c in range(8):
                xv = X.v(f0=c * 1024, f1=(c + 1) * 1024); hv = H.v(f0=c * 1024, f1=(c + 1) * 1024)
                if shift_fn is None:
                    ot = (tQ if c % 2 == 0 else tKF).v()
                    stt(ot, xv, gs_v_fn(c), tTMP.v(), ALU.mult, ALU.mult)
                    p.dma("sync", yT[c * 128:(c + 1) * 128, :], ot.ap, reads=[ot])
                else:
                    tmp = (tCUM if c % 2 == 0 else tKH).v()
                    stt(tmp, xv, gs_v_fn(c), tTMP.v(), ALU.mult, ALU.mult)
                    act(hv.bc(), tmp, AF.Identity, bias=shift_fn(c))

        memset(SM.v(f0=8, f1=9), 1e-6, eng="vector")
        prm_eps = SM.v(f0=8, f1=9)

        def hv2(t, h):
            return t.v(f0=h * 512, f1=h * 512 + 512)

        def gla_prep_part1(d, Kt, LAt, CS=32):
            NCK = T // CS
            NH = NCK // 2
            th = []
            if CS == 32:
                for h in range(2):
                    m = cst_v("mFB", (0 if d == 0 else 1) + h * 512, 512)
                    cv = hv2(tCUM, h); lv = hv2(LAt, h)
                    if d == 0:
                        th.append(lambda cv=cv, lv=lv, m=m: p.op("vector", lambda E: E.tensor_tensor_scan(
                            out=cv.ap, data0=m.ap, data1=lv.ap, initial=0.0, op0=ALU.mult, op1=ALU.add), reads=[m, lv], writes=[cv]))
                    else:
                        th.append(lambda cv=cv, lv=lv, m=m: p.op("vector", lambda E: E.tensor_tensor_scan(
                            out=cv.ap[:, ::-1], data0=m.ap[:, ::-1], data1=lv.ap[:, ::-1], initial=0.0,
                            op0=ALU.mult, op1=ALU.add), reads=[m, lv], writes=[cv]))
            else:
                onesv = cst_v("mf128", 127, 1)
                ones_b = onesv.ap.broadcast_to([128, CS])

                def scans(n0, n1):
                    for n in range(n0, n1):
                        cv = tCUM.v(f0=n * CS, f1=(n + 1) * CS); lv = LAt.v(f0=n * CS, f1=(n + 1) * CS)
                        if d == 0:
                            p.op("vector", lambda E, cv=cv, lv=lv: E.tensor_tensor_scan(out=cv.ap, data0=ones_b, data1=lv.ap, initial=0.0,
                                                                                    op0=ALU.mult, op1=ALU.add), reads=[onesv, lv], writes=[cv])
                        else:
                            p.op("vector", lambda E, cv=cv, lv=lv: E.tensor_tensor_scan(out=cv.ap[:, ::-1], data0=ones_b, data1=lv.ap[:, ::-1],
                                                                                    initial=0.0, op0=ALU.mult, op1=ALU.add),
                                 reads=[onesv, lv], writes=[cv])
                th.append(lambda: scans(0, NH))
                th.append(lambda: scans(NH, NCK))
            li = CS - 1 if d == 0 else 0
            for h in range(2):
                cvh = hv2(tCUM, h); tmh = hv2(tTMP, h)
                c3 = cvh.r("p (n j) -> p n j", j=CS)
                lastb = c3.ap[:, :, li:li + 1].broadcast_to([128, NH, CS])
                t3 = tmh.r("p (n j) -> p n j", j=CS)
                th.append(lambda c3=c3, lastb=lastb, t3=t3, cvh=cvh, tmh=tmh: p.op(
                    "vector", lambda E: E.tensor_tensor(out=t3.ap, in0=lastb, in1=c3.ap, op=ALU.subtract), reads=[cvh], writes=[tmh]))
            for h in range(2):
                th.append(lambda h=h: act(hv2(tTMP, h), hv2(tTMP, h), AF.Exp))
            late = [lambda h=h: tt(hv2(tKH, h), hv2(Kt, h), hv2(tTMP, h), ALU.mult) for h in range(2)]
            cumv = tCUM.v()
            c3a = cumv.r("p (n j) -> p n j", j=CS)
            last = c3a.w(c3a.ap[:, :, li:li + 1])
            ach = ACH.v(f0=d * 32, f1=d * 32 + NCK)
            th.append(lambda: p.op("scalar", lambda E: E.activation(out=ach.ap, in_=last.ap.rearrange("p n o -> p (n o)"), func=AF.Exp),
                                   reads=[cumv], writes=[ach]))
            return th, late, ach

        def gla_prep_part2(Kt):
            for h in range(2):
                act(hv2(tTMP, h), hv2(tCUM, h), AF.Exp)
            for h in range(2):
                tt(hv2(tQT, h), hv2(tQ, h), hv2(tTMP, h), ALU.mult)
            for h in range(2):
                act(hv2(tTMP, h), hv2(tCUM, h), AF.Exp, scale=-1.0)
            for h in range(2):
                tt(hv2(tKT, h), hv2(Kt, h), hv2(tTMP, h), ALU.mult)

        def gla_core(mix_id, l, ch, prep_hooks, gain_v, os_dram, CS=32):
            ident = cst_v("ident", 0, 128)
            prefetch2()
            for d in range(2):
                si = ((mix_id * 2 + l) * 2 + d) * 2 + ch
                dv = SRING.v(f0=0, f1=64)
                p.dma("sync", dv.ap, st0[:, si * 64:si * 64 + 64], writes=[dv])
                CPB = 128 // CS; SEGC = 256 // CS
                if d == 0:
                    th0, late0, ach = gla_prep_part1(0, tKF, tLAF, CS)
                    for f in list(prep_hooks[0]) + th0 + late0:
                        f()
                    th1, late1, ach1 = gla_prep_part1(1, tKF, tLAF, CS)
                    pending = list(prep_hooks[1]) + th1
                else:
                    for f in pending + late1:
                        f()
                    pending = []
                    ach = ach1
                blocks = list(range(NB)) if d == 0 else list(range(NB - 1, -1, -1))
                gla_prep_part2(tKF)
                def ds_front(b):
                    ptr = PTR.v(f0=(b % 2) * 128, f1=(b % 2) * 128 + 128)
                    p.op("tensor", lambda E, b=b, ptr=ptr: E.transpose(ptr.ap, tKH.v(f0=b * 128, f1=b * 128 + 128).ap, ident.ap),
                         reads=[tKH.v(f0=b * 128, f1=b * 128 + 128), ident], writes=[ptr])
                    cp(KHT.v(f0=(b % 2) * 128, f1=(b % 2) * 128 + 128), ptr, eng="scalar")
                    if CS == 128:
                        return
                    vsrc = VPAD.v(f0=b * 512 + ch * 256, f1=b * 512 + ch * 256 + 256).bc(F32)
                    base = vsrc.ap
                    vin = bass.AP(tensor=base.tensor, offset=base.offset, ap=[base.ap[0], [192, 2], [0, 4], [1, 64]])
                    bmv = cst_v("bm", 0, 4)
                    bmin = bass.AP(tensor=bmv.ap.tensor, offset=bmv.ap.offset, ap=[bmv.ap.ap[0], [0, 2], [1, 4], [0, 64]])
                    vb = VBD.v(f0=(b % 2) * 512, f1=(b % 2) * 512 + 512)
                    vout = vb.r("p (h n e) -> p h n e", h=2, n=4)
                    p.op("vector", lambda E, vout=vout, vin=vin, bmin=bmin: E.tensor_tensor(out=vout.ap, in0=vin, in1=bmin, op=ALU.mult),
                         reads=[vsrc, bmv], writes=[vb])

                def ds_back(b):
                    vb = VBD.v(f0=(b % 2) * 512, f1=(b % 2) * 512 + 512)
                    pds = PDS if b % 2 == 0 else PMS
                    if CS == 128:
                        mm(pds.v(f1=256), KHT.v(f0=(b % 2) * 128, f1=(b % 2) * 128 + 128),
                           VPAD.v(f0=b * 512 + ch * 256, f1=b * 512 + ch * 256 + 256))
                        cp(DS.v(p0=0, p1=64, f0=b * 64, f1=b * 64 + 64), pds.v(p0=0, p1=64, f0=0, f1=64), eng="scalar")
                        cp(DS.v(p0=64, p1=128, f0=b * 64, f1=b * 64 + 64), pds.v(p0=64, p1=128, f0=192, f1=256), eng="scalar")
                        return
                    mm(pds.v(), KHT.v(f0=(b % 2) * 128, f1=(b % 2) * 128 + 128), vb)
                    cp(DS.v(p0=0, p1=64, f0=b * 256, f1=b * 256 + 256), pds.v(p0=0, p1=64, f0=0, f1=256), eng="scalar")
                    cp(DS.v(p0=64, p1=128, f0=b * 256, f1=b * 256 + 256), pds.v(p0=64, p1=128, f0=256, f1=512), eng="scalar")

                ds_front(blocks[0])
                def cast_state(sl_):
                    for hh in range(2):
                        cp(SRR.v(p0=hh * 64, p1=hh * 64 + 64, f0=sl_ * 128 + hh * 64, f1=sl_ * 128 + hh * 64 + 64),
                           SRING.v(p0=hh * 64, p1=hh * 64 + 64, f0=sl_ * 64, f1=sl_ * 64 + 64), eng="scalar")
                cast_state(0)
                mask = cst_v("mf" if CS == 32 else "mf128", 0, 128) if d == 0 else cst_v("mb" if CS == 32 else "mb128", 0, 128)
                SCB = [(PSC, PX), (PSC, PX)]
                ACB = [PAC, PJ[0], PJ[1]]
                st_ = {"slot": 0}

                def stage_scores(i):
                    b = blocks[i]
                    for hh in range(2):
                        mm(SCB[i % 2][hh].v(f1=128),
                           tKT.v(p0=hh * 64, p1=hh * 64 + 64, f0=b * 128, f1=b * 128 + 128),
                           tQT.v(p0=hh * 64, p1=hh * 64 + 64, f0=b * 128, f1=b * 128 + 128))
                    for hh in range(2):
                        tt(PT.v(f0=(i % 2) * 256 + hh * 128, f1=(i % 2) * 256 + hh * 128 + 128), SCB[i % 2][hh].v(f1=128), mask, ALU.mult)

                def stage_main(i):
                    b = blocks[i]
                    accb = ACB[i % 3]
                    acc = accb.v(f1=128)
                    for hh in range(2):
                        h = 2 * ch + hh
                        mm(acc, VPAD.v(f0=b * 512 + h * 128, f1=b * 512 + h * 128 + 128),
                           PT.v(f0=(i % 2) * 256 + hh * 128, f1=(i % 2) * 256 + hh * 128 + 128), start=(hh == 0), stop=False)
                    chunks = list(range(CPB * b, CPB * b + CPB)) if d == 0 else list(range(CPB * b + CPB - 1, CPB * b - 1, -1))
                    for ci, n in enumerate(chunks):
                        slot = st_["slot"]
                        mm(accb.v(f0=(n % CPB) * CS, f1=(n % CPB) * CS + CS),
                           SRR.v(f0=slot * 128, f1=slot * 128 + 128), tQT.v(f0=n * CS, f1=n * CS + CS),
                           start=False, stop=(ci == CPB - 1))
                        nslot = (slot + 1) % 8
                        stt(SRING.v(f0=nslot * 64, f1=nslot * 64 + 64), SRING.v(f0=slot * 64, f1=slot * 64 + 64),
                            V(ach.ap[:, n:n + 1], ach.name, 0, 128, ach.f0 + n, ach.f0 + n + 1),
                            DS.v(f0=n * 64, f1=n * 64 + 64), ALU.mult, ALU.add)
                        slot = nslot
                        st_["slot"] = slot
                        bnd = (n % SEGC == SEGC - 1) if d == 0 else (n % SEGC == 0)
                        if bnd:
                            seg = n // SEGC
                            oi = ((l * 2 + d) * 4 + seg) * 2 + ch
                            sv2 = SRING.v(f0=slot * 64, f1=slot * 64 + 64)
                            p.dma("sync", os_dram[oi], sv2.ap, reads=[sv2])
                            ts(sv2, sv2, flag_s, ALU.mult)
                        cast_state(slot)

                def stage_evac(i):
                    b = blocks[i]
                    acc = ACB[i % 3].v(f1=128)
                    ov = tO.v(f0=b * 128, f1=b * 128 + 128)
                    if d == 0:
                        cp(ov, acc, eng="scalar")
                    else:
                        tt(ov, ov, acc, ALU.add)

                for step in range(NB + 2):
                    if step < NB:
                        stage_scores(step)
                        if step + 1 < NB:
                            ds_front(blocks[step + 1])
                        ds_back(blocks[step])
                    if 1 <= step <= NB:
                        stage_main(step - 1)
                    if step >= 2:
                        stage_evac(step - 2)
                    if d == 0 and pending:
                        for _ in range(2 if len(pending) > NB + 1 - step else 1):
                            if pending:
                                pending.pop(0)()
            o64 = cst_v("o64", 0, 128)
            hb = [PMS, PX]
            for h in range(2):
                act(hv2(tTMP, h), hv2(tO, h), AF.Square)
            for h in range(2):
                mm(hb[h].v(), o64, hv2(tTMP, h))
                act(hv2(tCUM, h), hb[h].v(), AF.Ln, bias=prm_eps)
            for h in range(2):
                act(hv2(tCUM, h), hv2(tCUM, h), AF.Exp, scale=-0.5)
            for h in range(2):
                tt(hv2(tO, h), hv2(tO, h), hv2(tCUM, h), ALU.mult)
            for h in range(2):
                stt(OM.v(f0=ch * 1024 + h * 512, f1=ch * 1024 + h * 512 + 512).bc(), hv2(tO, h), gain_v, hv2(tG, h), ALU.mult, ALU.mult)

        def wout_partial(l):
            sl = next_slot()
            for fo in range(8):
                for half in range(2):
                    P = [PX, PSC, PAC, PDS][state["rb2"] % 4]; state["rb2"] += 1
                    for kk in range(2):
                        mm(P.v(), sl.v(f0=kk * 1024 + fo * 128, f1=kk * 1024 + fo * 128 + 128),
                           OM.v(f0=kk * 1024 + half * 512, f1=kk * 1024 + half * 512 + 512).bc(), start=(kk == 0), stop=(kk == 1))
                    xv = X.v(f0=fo * 1024 + half * 512, f1=fo * 1024 + half * 512 + 512)
                    stt(xv, P.v(), MOD.v(f0=16 + fo, f1=17 + fo), xv, ALU.mult, ALU.add)

        one11 = cst_v("one", 0, 1, p0=0, p1=1)

        def ada_slot(g, lay, stg=None):
            stg = DS if stg is None else stg
            sl = next_slot()
            for k in range(8):
                mm(PMS.v(), SREP.v(f0=k * 128, f1=k * 128 + 128), sl_k(sl, k, 0, 512), start=(k == 0), stop=(k == 7))
            cp(stg.v(p0=0, p1=1, f0=0, f1=512), PMS.v(p0=0, p1=1), eng="scalar")
            for j in range(4):
                mm(PTR.v(f0=256 + 4 * g + j, f1=257 + 4 * g + j), stg.v(p0=0, p1=1, f0=j * 128, f1=j * 128 + 128), one11)
            o = (lay % 2) * 48
            cp(MODN.v(f0=o + 4 * g, f1=o + 4 * g + 4), PTR.v(f0=256 + 4 * g, f1=260 + 4 * g))

        def mod_commit(lay, c0, c1):
            o = (lay % 2) * 48
            tt(MOD.v(f0=c0, f1=c1), MODN.v(f0=o + c0, f1=o + c1), prm_v("adab%d" % lay, c0, c1 - c0), ALU.add)

        for l in range(L):
            if l == 0:
                make_srep()
                rms_stats()
                for g in range(4):
                    ada_slot(g, 0)
                mod_commit(0, 0, 16)
            else:
                mod_commit(l, 0, 48)
                stt(GS.v(f0=8, f1=16), MOD.v(f0=32, f1=40), 1.0, prm_v("n2g%d" % l, 0, 8), ALU.add, ALU.mult)
            stt(GS.v(f0=0, f1=8), MOD.v(f0=8, f1=16), 1.0, prm_v("n1g%d" % l, 0, 8), ALU.add, ALU.mult)
            LB = SM.v(f0=48, f1=52); OML = SM.v(f0=52, f1=56)
            if l == 0:
                memset(LB, 0.0, eng="vector")
            else:
                tt(LB, prm_v("lbz1", 0, 4), prm_v("lbz0", 0, 4), ALU.subtract)
                act(LB, LB, AF.Sigmoid)
            ts(OML, LB, -1.0, ALU.mult, 1.0, ALU.add)
            C8 = SM.v(f0=56, f1=60)
            act(C8, prm_v("rglam%d" % l, 0, 4), AF.Exp, scale=-1.0)
            act(C8, C8, AF.Ln, bias=1.0)
            ts(C8, C8, -8.0, ALU.mult)
            NBA = SM.v(f0=60, f1=64)
            ts(NBA, prm_v("ba2%d" % l, 0, 4), -1.0, ALU.mult)

            if l > 0:
                rms_stats()
            rms_apply(lambda c: GS.v(f0=c, f1=c + 1), lambda c: MOD.v(f0=c, f1=c + 1))

            slD1 = next_slot()
            TB, TC, TV, U, Y = tQ, tKF, tLAF, tCUM, DS.sub(0, 1024)
            scw = lambda cc, j: prm_v("scw%d" % l, cc * 3 + j)
            WP = SM.v(f0=9, f1=11)

            def sconv(cc):
                ts(WP.w(WP.ap[:, 0:1]), scw(cc, 0), flag_p, ALU.mult)
                ts(WP.w(WP.ap[:, 1:2]), scw(cc, 2), flag_p, ALU.mult)
                tt(U.v(), TC.v(), TV.v(), ALU.mult)
                ts(Y.v(), U.v(), scw(cc, 1), ALU.mult)
                y3 = Y.v().r("p (s j) -> p s j", j=64); u3 = U.v().r("p (s j) -> p s j", j=64)
                w0v = scw(cc, 0); w2v = scw(cc, 2)
                p.op("vector", lambda E: E.scalar_tensor_tensor(out=y3.ap[:, :, 1:64], in0=u3.ap[:, :, 0:63], scalar=w0v.ap,
                                                                in1=y3.ap[:, :, 1:64], op0=ALU.mult, op1=ALU.add),
                     reads=[U.v(), Y.v(), w0v], writes=[Y.v()])
                p.op("vector", lambda E: E.scalar_tensor_tensor(out=y3.ap[:, :, 0:63], in0=u3.ap[:, :, 1:64], scalar=w2v.ap,
                                                                in1=y3.ap[:, :, 0:63], op0=ALU.mult, op1=ALU.add),
                     reads=[U.v(), Y.v(), w2v], writes=[Y.v()])
                y4 = Y.v().r("p (a b j) -> p a b j", a=4, b=4); u4 = U.v().r("p (a b j) -> p a b j", a=4, b=4)
                p.op("vector", lambda E: E.scalar_tensor_tensor(out=y4.ap[:, :, 1:4, 0:1], in0=u4.ap[:, :, 0:3, 63:64], scalar=WP.ap[:, 0:1],
                                                                in1=y4.ap[:, :, 1:4, 0:1], op0=ALU.mult, op1=ALU.add),
                     reads=[U.v(), Y.v(), WP], writes=[Y.v()])
                p.op("vector", lambda E: E.scalar_tensor_tensor(out=y4.ap[:, :, 0:3, 63:64], in0=u4.ap[:, :, 1:4, 0:1], scalar=WP.ap[:, 1:2],
                                                                in1=y4.ap[:, :, 0:3, 63:64], op0=ALU.mult, op1=ALU.add),
                     reads=[U.v(), Y.v(), WP], writes=[Y.v()])
                tt(OM.v(f0=cc * 1024, f1=cc * 1024 + 1024).bc(), TB.v(), Y.v(), ALU.mult)

            def ev_copy(tile):
                return lambda pv, half: cp(tile.v(f0=half * 512, f1=half * 512 + 512), pv, eng="scalar")

            proj_chunk(slD1, 0, ev_copy(TB)); proj_chunk(slD1, 1, ev_copy(TC)); proj_chunk(slD1, 2, ev_copy(TV))
            sconv(0)
            proj_chunk(slD1, 3, ev_copy(TB))
            slD2 = next_slot()
            for i in range(8):
                cp(AUXC.v(f0=i * 128, f1=i * 128 + 128), sl_k(slD2, i, 256, 128).bc(F32))
            proj_chunk(slD2, 0, ev_copy(TC)); proj_chunk(slD2, 1, ev_copy(TV))
            sconv(1)
            if l == 0:
                for g in (4, 5):
                    ada_slot(g, 0, tKH)
                mod_commit(0, 16, 24)
            wout_partial(l)

            slC = next_slot()
            Ux = [tQ, tKF]; GG = [tLAF, tG]
            for cc in range(2):
                proj_chunk(slC, cc, ev_copy(Ux[cc]))
            for cc in range(2):
                proj_chunk(slC, 2 + cc, lambda pv, half, cc=cc: act(GG[cc].v(f0=half * 512, f1=half * 512 + 512), pv, AF.Gelu_apprx_tanh))
            XC, R, IG, HS = tCUM, DS.sub(0, 1024), DS.sub(1024, 1024), tKH
            GB = [PMS, PX, PSC, PAC]
            prefetch2()

            def hv_(t, h):
                return t.v(f0=h * 512, f1=h * 512 + 512)

            for cc in range(2):
                u = Ux[cc]
                for d in range(2):
                    di = d * 2 + cc
                    cw = [prm_v("rgcw%d" % l, di * 4 + j) for j in range(4)]
                    cb = prm_v("rgcb%d" % l, di)
                    c8 = SM.v(f0=56 + di, f1=57 + di)
                    WS = SM.v(f0=11, f1=15)
                    for j in range(4):
                        ts(WS.w(WS.ap[:, j:j + 1]), cw[j], flag_s, ALU.mult)
                    HSd = tO if d == 0 else HS
                    horder = [0, 1] if d == 0 else [1, 0]
                    x3 = XC.v().r("p (a j) -> p a j", j=256); u3 = u.v().r("p (a j) -> p a j", j=256)

                    def conv(h):
                        xh = hv_(XC, h)
                        ts(xh, hv_(u, h), cw[3], ALU.mult, cb, ALU.add)
                        a0, a1 = 2 * h, 2 * h + 2
                        for s in range(1, 4):
                            if d == 0:
                                o1, i1 = x3.ap[:, a0:a1, s:256], u3.ap[:, a0:a1, 0:256 - s]
                                c0 = max(a0, 1)
                                o2, i2 = x3.ap[:, c0:a1, 0:s], u3.ap[:, c0 - 1:a1 - 1, 256 - s:256]
                            else:
                                o1, i1 = x3.ap[:, a0:a1, 0:256 - s], u3.ap[:, a0:a1, s:256]
                                c1 = min(a1, 3)
                                o2, i2 = x3.ap[:, a0:c1, 256 - s:256], u3.ap[:, a0 + 1:c1 + 1, 0:s]
                            cws = cw[3 - s]
                            p.op("vector", lambda E, o1=o1, i1=i1, cws=cws: E.scalar_tensor_tensor(out=o1, in0=i1, scalar=cws.ap, in1=o1,
                                                                                      op0=ALU.mult, op1=ALU.add),
                                 reads=[u.v(), xh, cws], writes=[xh])
                            p.op("vector", lambda E, o2=o2, i2=i2, s=s: E.scalar_tensor_tensor(out=o2, in0=i2, scalar=WS.ap[:, 3 - s:4 - s], in1=o2,
                                                                                      op0=ALU.mult, op1=ALU.add),
                                 reads=[u.v(), xh, WS], writes=[xh])

                    def gates(h):
                        for wh, (dst, bname) in enumerate(((R, "rgbr%d" % l), (IG, "rgbi%d" % l))):
                            wm = AUXC.v(f0=(di * 2 + wh) * 128, f1=(di * 2 + wh) * 128 + 128)
                            gb = GB[h * 2 + wh]
                            mm(gb.v(), wm, hv_(XC, h))
                            act(hv_(dst, h), gb.v(), AF.Sigmoid, bias=prm_v(bname, di))

                    def scans(h):
                        segs = [2 * h, 2 * h + 1] if d == 0 else [2 * h + 1, 2 * h]
                        for sg in segs:
                            first = (sg == 0) if d == 0 else (sg == 3)
                            if first:
                                init = prm_v("str%d" % l, di)
                            else:
                                prev = HSd.v(f0=sg * 256 - 1, f1=sg * 256) if d == 0 else HSd.v(f0=sg * 256 + 256, f1=sg * 256 + 257)
                                init = ACH.v(f0=sg, f1=sg + 1)
                                ts(init, prev, flag_s, ALU.mult)
                            hv = HSd.v(f0=sg * 256, f1=sg * 256 + 256); av = R.v(f0=sg * 256, f1=sg * 256 + 256); bv = IG.v(f0=sg * 256, f1=sg * 256 + 256)
                            if d == 0:
                                p.op("vector", lambda E, hv=hv, av=av, bv=bv, init=init: E.tensor_tensor_scan(
                                    out=hv.ap, data0=av.ap, data1=bv.ap, initial=init.ap, op0=ALU.mult, op1=ALU.add),
                                    reads=[av, bv, init], writes=[hv])
                            else:
                                p.op("vector", lambda E, hv=hv, av=av, bv=bv, init=init: E.tensor_tensor_scan(
                                    out=hv.ap[:, ::-1], data0=av.ap[:, ::-1], data1=bv.ap[:, ::-1], initial=init.ap, op0=ALU.mult, op1=ALU.add),
                                    reads=[av, bv, init], writes=[hv])

                    for h in horder:
                        conv(h)
                    for h in horder:
                        gates(h)
                    for h in horder:
                        act(hv_(R, h), hv_(R, h), AF.Exp, scale=c8)
                    for h in horder:
                        tt(hv_(tTMP, h), hv_(R, h), hv_(R, h), ALU.mult)
                    for h in horder:
                        act(hv_(tTMP, h), hv_(tTMP, h), AF.Ln, bias=1.0, scale=-1.0)
                    for h in horder:
                        act(hv_(tTMP, h), hv_(tTMP, h), AF.Exp, scale=0.5)
                    for h in horder:
                        tt(hv_(IG, h), hv_(IG, h), hv_(tTMP, h), ALU.mult)
                    for h in horder:
                        tt(hv_(IG, h), hv_(IG, h), hv_(XC, h), ALU.mult)
                    for h in horder:
                        scans(h)
                        if d == 1:
                            tt(hv_(tO, h), hv_(tO, h), hv_(HS, h), ALU.add)
                    oc = ((l * 2 + d) * 2 + cc) * 4
                    h4 = HSd.v().r("p (a j) -> p a j", j=256)
                    fin = h4.w(h4.ap[:, :, 255:256] if d == 0 else h4.ap[:, :, 0:1])
                    o4 = OSR.v(f0=oc, f1=oc + 4)
                    p.op("vector", lambda E, o4=o4, fin=fin: E.tensor_copy(o4.ap.rearrange("p (a o) -> p a o", o=1), fin.ap),
                         reads=[HSd.v()], writes=[o4])
                tt(OM.v(f0=cc * 1024, f1=cc * 1024 + 1024).bc(), tO.v(), GG[cc].v(), ALU.mult)
            wout_partial(l)

            slBv = next_slot()
            proj_vtok(slBv, 0)
            for i in range(4):
                cp(WA2T.v(f0=i * 128, f1=i * 128 + 128), sl_k(slBv, i, 256, 128).bc(F32))
            sc = 32 ** -0.5
            for pair in range(2):
                slB = next_slot()
                proj_chunk(slB, 0, lambda pv, half: act(tQ.v(f0=half * 512, f1=half * 512 + 512), pv, AF.Identity, scale=sc))
                proj_chunk(slB, 1, ev_copy(tKF))
                proj_chunk(slB, 2, lambda pv, half: act(tG.v(f0=half * 512, f1=half * 512 + 512), pv, AF.Silu))
                proj_chunk(slB, 3, ev_copy(AUXC))

                def gla_hook(d, pair=pair):
                    i = d * 2 + pair
                    th = []
                    for half in range(2):
                        def f(half=half, i=i):
                            mm(PMS.v(), WA2T.v(f0=i * 128, f1=i * 128 + 128), AUXC.v(f0=half * 512, f1=half * 512 + 512))
                            act(tLAF.v(f0=half * 512, f1=half * 512 + 512), PMS.v(), AF.Exp, bias=SM.v(f0=60 + i, f1=61 + i), scale=-1.0)
                        th.append(f)
                    for h in range(2):
                        th.append(lambda h=h: act(hv2(tLAF, h), hv2(tLAF, h), AF.Ln, bias=1.0))
                    for h in range(2):
                        th.append(lambda h=h: ts(hv2(tLAF, h), hv2(tLAF, h), -1.0 / 16.0, ALU.mult))
                    return th
                gla_core(1, l, pair, [gla_hook(0), gla_hook(1)], prm_v("ggain%d" % l, pair), osB, CS=128)
                if l == 0:
                    for g in (6 + 2 * pair, 7 + 2 * pair):
                        ada_slot(g, 0)
                if l + 1 < L:
                    for g in range(3 * pair, 3 * pair + 3):
                        ada_slot(g, l + 1)
            wout_partial(l)

            slAv = next_slot()
            proj_vtok(slAv, 0)
            for pair in range(2):
                slA = next_slot()
                proj_chunk(slA, 0, ev_copy(tQ))
                proj_chunk(slA, 1, lambda pv, half: act(tLAF.v(f0=half * 512, f1=half * 512 + 512), pv, AF.Sigmoid))
                proj_chunk(slA, 2, lambda pv, half: act(AUXC.v(f0=half * 512, f1=half * 512 + 512), pv, AF.Sigmoid))
                proj_chunk(slA, 3, lambda pv, half: act(tG.v(f0=half * 512, f1=half * 512 + 512), pv, AF.Silu))

                def hg_hook(d, pair=pair):
                    i = d * 2 + pair
                    src = tLAF if d == 0 else AUXC
                    th = []
                    for h in range(2):
                        th.append(lambda h=h: ts(hv2(tLAF, h), hv2(src, h), SM.v(f0=52 + i, f1=53 + i), ALU.mult, SM.v(f0=48 + i, f1=49 + i), ALU.add))
                    for h in range(2):
                        th.append(lambda h=h: ts(hv2(tKF, h), hv2(tLAF, h), -1.0, ALU.mult, 1.0, ALU.add))
                    for h in range(2):
                        th.append(lambda h=h: ts(hv2(tLAF, h), hv2(tLAF, h), 1e-20, ALU.max))
                    for h in range(2):
                        th.append(lambda h=h: act(hv2(tLAF, h), hv2(tLAF, h), AF.Ln))
                    return th
                gla_core(0, l, pair, [hg_hook(0), hg_hook(1)], prm_v("hgain%d" % l, pair), osA)
                if l == 0 and pair == 0:
                    for g in (10, 11):
                        ada_slot(g, 0)
                    mod_commit(0, 24, 48)
                    stt(GS.v(f0=8, f1=16), MOD.v(f0=32, f1=40), 1.0, prm_v("n2g%d" % l, 0, 8), ALU.add, ALU.mult)
                if l + 1 < L:
                    for g in range(6 + 3 * pair, 6 + 3 * pair + 3):
                        ada_slot(g, l + 1)
            wout_partial(l)

            rmsnorm_to_H(lambda c: GS.v(f0=8 + c, f1=9 + c), lambda c: MOD.v(f0=24 + c, f1=25 + c))
            W1B = [PJ[0], PJ[1], PTR, PMS]
            W2B = [PX, PSC, PAC, PDS]
            for g in range(8):
                sl = next_slot()
                for jj in range(4):
                    for half in range(2):
                        P = W1B[state["rb"] % 4]; state["rb"] += 1
                        for k in range(8):
                            mm(P.v(), sl_k(sl, k, jj * 128, 128), Hk(k, half * 512, 512), start=(k == 0), stop=(k == 7))
                        hv = HID.v(f0=jj * 1024 + half * 512, f1=jj * 1024 + half * 512 + 512)
                        sq = SQT.v(f0=(state["rb"] % 2) * 512, f1=(state["rb"] % 2) * 512 + 512)
                        act(sq, P.v(), AF.Square)
                        stt(hv, P.v(), 0.0, sq, ALU.is_gt, ALU.mult)
                sl = next_slot()
                for fo in range(8):
                    for half in range(2):
                        P = W2B[state["rb2"] % 4]; state["rb2"] += 1
                        for j in range(4):
                            mm(P.v(), sl.v(f0=j * 1024 + fo * 128, f1=j * 1024 + fo * 128 + 128),
                               HID.v(f0=j * 1024 + half * 512, f1=j * 1024 + half * 512 + 512), start=(j == 0), stop=(j == 3))
                        xv = X.v(f0=fo * 1024 + half * 512, f1=fo * 1024 + half * 512 + 512)
                        stt(xv, P.v(), MOD.v(f0=40 + fo, f1=41 + fo), xv, ALU.mult, ALU.add)

        rmsnorm_to_H(lambda c: prm_v("fng", c), None)
        p.dma("sync", osr, OSR.v().ap, reads=[OSR.v()])
        p.finish("sync")
        with nc.Block() as block:
            p.emit(block)
    return nc


def kernel(**inp):
    inp = {k: np.asarray(v) for k, v in inp.items()}
    wst = build_stream(inp)
    co = build_consts()
    cst = co.get()
    pos = [build_params(inp, c) for c in range(8)]
    nc = build_program(pos[0], co, wst.shape[0])
    ones = np.ones((128, 128), np.float32)
    in_maps = []
    for c in range(8):
        if c < 4:
            x = inp["x_sample"][c]
        else:
            x = inp["x_prompt"][4 * (c - 4):4 * (c - 4) + 4].reshape(1024, 1024)
        in_maps.append({"xT": np.ascontiguousarray(x.T), "wst": wst, "prm": pos[c].get(), "cst": cst,
                        "st0": build_st0(inp, c), "onesd": ones})
    res = run_bass_kernel_spmd(nc, in_maps, core_ids=list(range(8)))
    R = res.results
    y_sample = np.stack([R[c]["yT"].T for c in range(4)], 0).astype(np.float32)
    y_prompt = np.concatenate([R[c]["yT"].T.reshape(4, 256, 1024) for c in range(4, 8)], 0).astype(np.float32)
    nh = np.zeros((16, L, 2, 4, 64, 64), np.float32)
    ng = np.zeros((16, L, 2, 4, 32, 64), np.float32)
    nr = np.zeros((16, L, 2, 256), np.float32)
    for c in range(4, 8):
        oa, ob, orr = R[c]["osA"], R[c]["osB"], R[c]["osr"]
        for seg in range(4):
            bi = 4 * (c - 4) + seg
            for l in range(L):
                for d in range(2):
                    for ch in range(2):
                        oi = ((l * 2 + d) * 4 + seg) * 2 + ch
                        for hh in range(2):
                            nh[bi, l, d, 2 * ch + hh] = oa[oi, hh * 64:hh * 64 + 64, :]
                            ng[bi, l, d, 2 * ch + hh] = ob[oi, hh * 64:hh * 64 + 32, :]
                    for cc in range(2):
                        oc = ((l * 2 + d) * 2 + cc) * 4 + seg
                        nr[bi, l, d, cc * 128:(cc + 1) * 128] = orr[:, oc]
    return (y_prompt, y_sample, nh, ng, nr)
```

```python
import numpy as np
from contextlib import ExitStack
import concourse.bass as bass
import concourse.mybir as mybir
from concourse.bass_utils import run_bass_kernel_spmd

F32 = mybir.dt.float32
F32R = mybir.dt.float32r
AF = mybir.ActivationFunctionType
ALU = mybir.AluOpType
ENGS = ("tensor", "vector", "scalar", "gpsimd", "sync")

D = 1024; L = 2; T = 1024; NB = 8; NCH = 32
PROJ_W = 3360


class V:
    def __init__(self, ap, name, p0, p1, f0, f1):
        self.ap, self.name, self.p0, self.p1, self.f0, self.f1 = ap, name, p0, p1, f0, f1

    def reg(self):
        return (self.name, self.p0, self.p1, self.f0, self.f1)

    def w(self, ap):
        return V(ap, self.name, self.p0, self.p1, self.f0, self.f1)

    def r(self, pattern, **kw):
        return self.w(self.ap.rearrange(pattern, **kw))

    def bc(self, dt=F32R):
        return self.w(self.ap.bitcast(dt))


class TL:
    def __init__(self, handle, name, P, F, base=0):
        self.h, self.name, self.P, self.F, self.base = handle, name, P, F, base

    def v(self, p0=0, p1=None, f0=0, f1=None):
        p1 = self.P if p1 is None else p1
        f1 = self.F if f1 is None else f1
        return V(self.h[p0:p1, self.base + f0:self.base + f1], self.name, p0, p1, self.base + f0, self.base + f1)

    def sub(self, f0, F):
        return TL(self.h, self.name, self.P, F, self.base + f0)


def _ov(a, b):
    return a[0] == b[0] and a[1] < b[2] and b[1] < a[2] and a[3] < b[4] and b[3] < a[4]


def _ct(a, b):
    return a[0] == b[0] and a[1] <= b[1] and b[2] <= a[2] and a[3] <= b[3] and b[4] <= a[4]


class Prog:
    def __init__(self, nc, es, n_dma_sems=12):
        self.nc = nc
        self.ops = {e: [] for e in ENGS}
        self.cnt = {e: 0 for e in ENGS}
        self.sems = {e: es.enter_context(nc.semaphore("s_" + e)) for e in ENGS}
        self.dma_sems = [es.enter_context(nc.semaphore("d%d" % i)) for i in range(n_dma_sems)]
        self.es = es
        self.nfixed = n_dma_sems
        self.dma_uses = [0] * n_dma_sems
        self.dma_next = 0
        self.waited = {e: {} for e in ENGS}
        self.recs = {}

    def _deps(self, eng, reads, writes):
        deps = []
        for v in reads:
            rg = v.reg()
            for (r2, kind, sk, val, e2) in self.recs.get(rg[0], ()):
                if kind == "w" and _ov(rg, r2):
                    if e2 == eng and eng == "tensor":
                        continue
                    deps.append((sk, val))
        for v in writes:
            rg = v.reg()
            if eng == "tensor":
                rg = (rg[0], rg[1], rg[2], 0, 1 << 30)
            for (r2, kind, sk, val, e2) in self.recs.get(rg[0], ()):
                if _ov(rg, r2):
                    if e2 == eng and eng == "tensor":
                        continue
                    deps.append((sk, val))
        return deps

    def _record(self, eng, reads, writes, sk, val):
        for v in writes:
            rg = v.reg()
            lst = self.recs.setdefault(rg[0], [])
            lst[:] = [x for x in lst if not _ct(rg, x[0])]
            lst.append((rg, "w", sk, val, eng))
        for v in reads:
            rg = v.reg()
            lst = self.recs.setdefault(rg[0], [])
            lst[:] = [x for x in lst if not (x[1] == "r" and x[2] == sk and _ct(rg, x[0]))]
            lst.append((rg, "r", sk, val, eng))

    def _sem(self, sk):
        return self.sems[sk] if isinstance(sk, str) else self.dma_sems[sk]

    def _emit_waits(self, eng, deps):
        best = {}
        for sk, val in deps:
            if val > best.get(sk, 0):
                best[sk] = val
        for sk, val in best.items():
            if val > self.waited[eng].get(sk, 0):
                self.waited[eng][sk] = val
                sem = self._sem(sk)
                self.ops[eng].append(lambda E, sem=sem, val=val: E.wait_ge(sem, val))

    def op(self, eng, fn, reads=(), writes=()):
        self._emit_waits(eng, self._deps(eng, reads, writes))
        self.cnt[eng] += 1
        sem = self.sems[eng]
        self.ops[eng].append(lambda E, fn=fn, sem=sem: fn(E).then_inc(sem, 1))
        self._record(eng, reads, writes, eng, self.cnt[eng])

    def dma(self, eng, out_ap, in_ap, reads=(), writes=()):
        if eng == "gpsimd":
            deps = self._deps(eng, reads, writes)
            self._emit_waits(eng, deps)
            sem = self.es.enter_context(self.nc.semaphore("w%d" % len(self.dma_sems)))
            self.dma_sems.append(sem)
            self.dma_uses.append(1)
            slot = len(self.dma_sems) - 1
            self.ops[eng].append(lambda E, o=out_ap, i=in_ap, sem=sem: E.dma_start(out=o, in_=i).then_inc(sem, 16))
            self._record("dma", reads, writes, slot, 16)
            return
        slot = self.dma_next
        self.dma_next = (self.dma_next + 1) % self.nfixed
        deps = self._deps(eng, reads, writes)
        if self.dma_uses[slot] > 0:
            deps.append((slot, 16 * self.dma_uses[slot]))
        self._emit_waits(eng, deps)
        self.dma_uses[slot] += 1
        sem = self.dma_sems[slot]
        self.ops[eng].append(lambda E, o=out_ap, i=in_ap, sem=sem: E.dma_start(out=o, in_=i).then_inc(sem, 16))
        self._record("dma", reads, writes, slot, 16 * self.dma_uses[slot])

    def finish(self, eng="sync"):
        self._emit_waits(eng, [(s, 16 * u) for s, u in enumerate(self.dma_uses) if u > 0])

    def emit(self, block):
        for e in ENGS:
            ops = self.ops[e]
            if not ops:
                continue

            def body(E, ops=ops):
                for o in ops:
                    o(E)

            getattr(block, e)(body)


class Packer:
    def __init__(self):
        self.cols = []
        self.off = {}
        self.n = 0

    def add(self, name, arr):
        arr = np.asarray(arr, np.float32)
        assert arr.shape[0] == 128
        self.off[name] = self.n
        self.cols.append(arr)
        self.n += arr.shape[1]

    def get(self):
        return np.ascontiguousarray(np.concatenate(self.cols, axis=1))


def fm(vec):
    return np.asarray(vec, np.float32).reshape(-1, 128).T


def kslot(mat):
    out = np.zeros((128, 8, 512), np.float32)
    out[:, :, :mat.shape[1]] = mat.reshape(8, 128, mat.shape[1]).transpose(1, 0, 2)
    return out


def build_stream(inp):
    slots = []
    def ada_slots(l, g0, n):
        if l >= L:
            return
        aw = inp["ada_w"][l]
        for g in range(g0, g0 + n):
            slots.append(kslot(aw[:, 512 * g:512 * g + 512]).reshape(128, 4096))

    for l in range(L):
        if l == 0:
            ada_slots(0, 0, 4)
        wi = inp["w_in"][l]; wo = inp["w_out"][l]

        def chunks(cols_list):
            s = np.zeros((128, 8, 512), np.float32)
            for i, m in enumerate(cols_list):
                if m is not None:
                    s[:, :, i * 128:(i + 1) * 128] = m.reshape(8, 128, 128).transpose(1, 0, 2)
            return s

        def wcols(c0, n=128):
            m = np.zeros((1024, 128), np.float32)
            m[:, :n] = wi[:, c0:c0 + n]
            return m

        def woslot(r0):
            s = np.zeros((128, 4096), np.float32)
            s[:, :2048] = wo[r0:r0 + 256].reshape(2, 128, 1024).transpose(1, 0, 2).reshape(128, 2048)
            return s

        dB, dC, dV = 2592, 2848, 3104
        slots.append(chunks([wcols(dB), wcols(dC), wcols(dV), wcols(dB + 128)]).reshape(128, 4096))
        s2 = chunks([wcols(dC + 128), wcols(dV + 128), None, None])
        for d in range(2):
            for cc in range(2):
                for wh, w in enumerate((inp["rg_w_r"][l], inp["rg_w_i"][l])):
                    i = (d * 2 + cc) * 2 + wh
                    bd = np.zeros((128, 128), np.float32)
                    for nn in range(2):
                        bd[nn * 64:(nn + 1) * 64, nn * 64:(nn + 1) * 64] = w[d, 2 * cc + nn]
                    s2[:, i, 256:384] = bd
        slots.append(s2.reshape(128, 4096))
        if l == 0:
            ada_slots(0, 4, 2)
        slots.append(woslot(768))
        cX, cG = 2080, 2336
        slots.append(chunks([wcols(cX), wcols(cX + 128), wcols(cG), wcols(cG + 128)]).reshape(128, 4096))
        slots.append(woslot(512))
        def padqk(c0, ch):
            m = np.zeros((1024, 128), np.float32)
            for hh in range(2):
                h = 2 * ch + hh
                m[:, hh * 64:hh * 64 + 32] = wi[:, c0 + h * 32:c0 + h * 32 + 32]
            return m
        sv = chunks([wcols(1536), wcols(1664), None, None])
        for d in range(2):
            for ch in range(2):
                m = np.zeros((128, 128), np.float32)
                for hh in range(2):
                    h = 2 * ch + hh
                    m[d * 16:(d + 1) * 16, hh * 64:hh * 64 + 32] = inp["gla_wa2"][l, d][:, h * 32:(h + 1) * 32]
                sv[:, d * 2 + ch, 256:384] = m
        slots.append(sv.reshape(128, 4096))
        for ch in range(2):
            slots.append(chunks([padqk(1280, ch), padqk(1408, ch), wcols(1792 + ch * 128), wcols(2048, 32)]).reshape(128, 4096))
            if l == 0:
                ada_slots(0, 6 + 2 * ch, 2)
            ada_slots(l + 1, 3 * ch, 3)
        slots.append(woslot(256))
        slots.append(chunks([wcols(256), wcols(384), None, None]).reshape(128, 4096))
        for ch in range(2):
            slots.append(chunks([wcols(0 + ch * 128), wcols(512 + ch * 128), wcols(768 + ch * 128),
                                 wcols(1024 + ch * 128)]).reshape(128, 4096))
            if l == 0 and ch == 0:
                ada_slots(0, 10, 2)
            ada_slots(l + 1, 6 + 3 * ch, 3)
        slots.append(woslot(0))
        w1 = inp["mlp_w1"][l]; w2 = inp["mlp_w2"][l]
        for g in range(8):
            slots.append(kslot(w1[:, g * 512:(g + 1) * 512]).reshape(128, 4096))
            slots.append(np.ascontiguousarray(
                w2[g * 512:(g + 1) * 512].reshape(4, 128, 1024).transpose(1, 0, 2)).reshape(128, 4096))
    return np.ascontiguousarray(np.stack(slots, 0))


def build_consts():
    pk = Packer()
    s = np.arange(128)[:, None]; t = np.arange(128)[None, :]
    same = (s // 32) == (t // 32)
    pk.add("mf", (same & (s <= t)).astype(np.float32))
    pk.add("mb", (same & (s >= t)).astype(np.float32))
    pk.add("mf128", (s <= t).astype(np.float32))
    pk.add("mb128", (s >= t).astype(np.float32))
    pk.add("ident", np.eye(128, dtype=np.float32))
    o64 = np.zeros((128, 128), np.float32); o64[:64, :64] = 1 / 64.; o64[64:, 64:] = 1 / 64.
    pk.add("o64", o64)
    pk.add("bm", ((np.arange(128)[:, None] // 32) == np.arange(4)[None, :]).astype(np.float32))
    mFB = np.ones((128, 1025), np.float32); mFB[:, 0::32] = 0
    pk.add("mFB", mFB)
    pk.add("one", np.ones((128, 1), np.float32))
    return pk


def build_params(inp, core):
    pk = Packer()
    samp = core < 4
    crow = inp["c"][core] if samp else inp["c_ctx"]
    pk.add("cvec", fm(crow))
    fl = np.zeros((128, 2), np.float32); fl[:, 0] = 1.0 if samp else 0.0; fl[:, 1] = 0.0 if samp else 1.0
    pk.add("flags", fl)
    pk.add("fng", fm(inp["final_norm_g"]))
    for l in range(L):
        pk.add("adab%d" % l, fm(inp["ada_b"][l]))
        pk.add("n1g%d" % l, fm(inp["norm1_g"][l]))
        pk.add("n2g%d" % l, fm(inp["norm2_g"][l]))
        pk.add("lbz%d" % l, fm(inp["hgrn_lb_logits"][l].reshape(-1)))
        pk.add("hgain%d" % l, fm(inp["hgrn_norm_g"][l]))
        pk.add("ggain%d" % l, fm(inp["gla_norm_g"][l]))
        b = np.zeros((128, 4), np.float32)
        for d in range(2):
            for ch in range(2):
                for hh in range(2):
                    h = 2 * ch + hh
                    b[hh * 64:hh * 64 + 32, d * 2 + ch] = inp["gla_ba2"][l, d, h * 32:(h + 1) * 32]
        pk.add("ba2%d" % l, b)
        cw = np.zeros((128, 16), np.float32)
        for d in range(2):
            for cc in range(2):
                for j in range(4):
                    cw[:, (d * 2 + cc) * 4 + j] = inp["rg_conv_w"][l, d, j, cc * 128:(cc + 1) * 128]
        pk.add("rgcw%d" % l, cw)
        pk.add("rgcb%d" % l, fm(inp["rg_conv_b"][l].reshape(-1)))
        pk.add("rgbr%d" % l, fm(inp["rg_b_r"][l].reshape(-1)))
        pk.add("rgbi%d" % l, fm(inp["rg_b_i"][l].reshape(-1)))
        pk.add("rglam%d" % l, fm(inp["rg_lambda"][l].reshape(-1)))
        sw = np.zeros((128, 6), np.float32)
        for cc in range(2):
            for j in range(3):
                sw[:, cc * 3 + j] = inp["sconv_w"][l, j, cc * 128:(cc + 1) * 128]
        pk.add("scw%d" % l, sw)
        if samp:
            pk.add("str%d" % l, fm(inp["state_rglru"][core, l].reshape(-1)))
        else:
            pk.add("str%d" % l, np.zeros((128, 4), np.float32))
    return pk


def build_st0(inp, core):
    st = np.zeros((128, 16, 64), np.float32)
    if core < 4:
        for m, key in enumerate(("state_hgrn", "state_gla")):
            s = inp[key][core]
            dk = s.shape[3]
            for l in range(L):
                for d in range(2):
                    for ch in range(2):
                        i = ((m * 2 + l) * 2 + d) * 2 + ch
                        for hh in range(2):
                            st[hh * 64:hh * 64 + dk, i, :] = s[l, d, 2 * ch + hh]
    return st.reshape(128, 1024)


def build_program(po, co, n_slots):
    nc = bass.Bass("TRN2", target_bir_lowering=False)
    xT = nc.dram_tensor("xT", [D, T], F32, kind="ExternalInput").ap()
    wst = nc.dram_tensor("wst", [n_slots, 128, 4096], F32, kind="ExternalInput").ap()
    prm = nc.dram_tensor("prm", [128, po.n], F32, kind="ExternalInput").ap()
    cst = nc.dram_tensor("cst", [128, co.n], F32, kind="ExternalInput").ap()
    st0 = nc.dram_tensor("st0", [128, 1024], F32, kind="ExternalInput").ap()
    onesd = nc.dram_tensor("onesd", [128, 128], F32, kind="ExternalInput").ap()
    yT = nc.dram_tensor("yT", [D, T], F32, kind="ExternalOutput").ap()
    osA = nc.dram_tensor("osA", [32, 128, 64], F32, kind="ExternalOutput").ap()
    osB = nc.dram_tensor("osB", [32, 128, 64], F32, kind="ExternalOutput").ap()
    osr = nc.dram_tensor("osr", [128, 32], F32, kind="ExternalOutput").ap()

    with ExitStack() as es:
        def sb(name, P, Fn, dt=F32):
            return TL(es.enter_context(nc.sbuf_tensor(name, [P, Fn], dt)), name, P, Fn)

        def psum(name):
            return TL(es.enter_context(nc.psum_tensor(name, [128, 512], F32)), name, 128, 512)

        p = Prog(nc, es)
        X = sb("X", 128, 8192); H = sb("H", 128, 8192)
        RING = sb("RING", 128, 8192, F32R)
        W = sb("W", 128, 12288)
        FR = sb("FR", 128, 4096, F32R)
        OM = FR.sub(0, 2048); SREP = FR.sub(2048, 1024); HID = FR.sub(0, 4096)
        PRM = sb("PRM", 128, po.n); CST = sb("CST", 128, co.n)
        ONES = sb("ONES", 128, 128, F32R)
        MOD = sb("MOD", 128, 48); GS = sb("GS", 128, 16)
        SM = sb("SM", 128, 64)
        OSR = sb("OSR", 128, 32)
        MODN = sb("MODN", 128, 96); ACH = sb("ACH", 128, 64)
        WA2T = sb("WA2T", 128, 512)
        PS = [psum("P%d" % i) for i in range(8)]
        PJ = [PS[0], PS[1]]; PMS = PS[2]; PSC = PS[3]; PAC = PS[4]; PDS = PS[5]; PTR = PS[6]; PX = PS[7]

        def wt(i):
            return W.sub(i * 1024, 1024)
        tQ, tKF, tLAF, tG, tO, tCUM, tKH, tTMP = [wt(i) for i in range(8)]
        DS = W.sub(8192, 2048); SRING = W.sub(10240, 1024); AUXC = W.sub(11264, 1024)
        KHT = sb("KHT", 128, 256, F32R); VBD = sb("VBD", 128, 1024, F32R)
        tQT = sb("QT", 128, 1024, F32R); tKT = sb("KT", 128, 1024, F32R); PT = sb("PTt", 128, 512, F32R)
        VPAD = sb("VPAD", 128, 4096, F32R); SRR = sb("SRR", 128, 1024, F32R)
        SQT = W.sub(8192, 1024)

        def prm_v(name, c0=0, n=1):
            o = po.off[name] + c0
            return PRM.v(f0=o, f1=o + n)

        def cst_v(name, c0=0, n=1, p0=0, p1=128):
            o = co.off[name] + c0
            return CST.v(p0=p0, p1=p1, f0=o, f1=o + n)

        flag_s = prm_v("flags", 0); flag_p = prm_v("flags", 1)

        def act(out, in_, func, bias=None, scale=None, extra_reads=()):
            kw = {}
            rd = [in_] + list(extra_reads)
            if bias is not None:
                if isinstance(bias, V):
                    kw["bias"] = bias.ap; rd.append(bias)
                else:
                    kw["bias"] = bias
            if scale is not None:
                if isinstance(scale, V):
                    kw["scale"] = scale.ap; rd.append(scale)
                else:
                    kw["scale"] = scale
            p.op("scalar", lambda E: E.activation(out=out.ap, in_=in_.ap, func=func, **kw), reads=rd, writes=[out])

        def tt(out, a, b, op, eng="vector"):
            p.op(eng, lambda E: E.tensor_tensor(out=out.ap, in0=a.ap, in1=b.ap, op=op), reads=[a, b], writes=[out])

        def ts(out, a, s1, op0, s2=None, op1=None, eng="vector"):
            rd = [a]
            a1 = s1.ap if isinstance(s1, V) else s1
            if isinstance(s1, V): rd.append(s1)
            kw = {"scalar2": None}
            if op1 is not None:
                kw["scalar2"] = s2.ap if isinstance(s2, V) else s2
                kw["op1"] = op1
                if isinstance(s2, V): rd.append(s2)
            p.op(eng, lambda E: E.tensor_scalar(out=out.ap, in0=a.ap, scalar1=a1, op0=op0, **kw), reads=rd, writes=[out])

        def stt(out, a, s, b, op0, op1):
            rd = [a, b]
            sa = s.ap if isinstance(s, V) else s
            if isinstance(s, V): rd.append(s)
            p.op("vector", lambda E: E.scalar_tensor_tensor(out=out.ap, in0=a.ap, scalar=sa, in1=b.ap, op0=op0, op1=op1),
                 reads=rd, writes=[out])

        def cp(out, in_, eng="vector"):
            if eng == "scalar":
                p.op("scalar", lambda E: E.copy(out.ap, in_.ap), reads=[in_], writes=[out])
            else:
                p.op(eng, lambda E: E.tensor_copy(out.ap, in_.ap), reads=[in_], writes=[out])

        def mm(out, lhsT, rhs, start=True, stop=True):
            p.op("tensor", lambda E: E.matmul(out.ap, lhsT.ap, rhs.ap, start=start, stop=stop),
                 reads=[lhsT, rhs], writes=[out])

        def memset(v, val, eng="gpsimd"):
            p.op(eng, lambda E: E.memset(v.ap, val), writes=[v])

        state = {"cur": -1, "issued": 0, "pj": 0, "rb": 0, "rb2": 0}

        def issue(i):
            if i < n_slots and state["issued"] <= i:
                rv = RING.v(f0=(i % 2) * 4096, f1=(i % 2) * 4096 + 4096)
                p.dma("gpsimd", rv.ap, wst[i], writes=[rv])
                state["issued"] = i + 1

        def prefetch2():
            issue(state["cur"] + 2)

        def next_slot():
            state["cur"] += 1
            i = state["cur"]
            issue(i)
            issue(i + 1)
            return RING.sub((i % 2) * 4096, 4096)

        def sl_k(sl, k, c0, n):
            return sl.v(f0=k * 512 + c0, f1=k * 512 + c0 + n)

        def Hk(k, t0, n):
            return H.v(f0=k * 1024 + t0, f1=k * 1024 + t0 + n).bc()

        def proj_chunk(sl, ci, evac):
            for half in range(2):
                P = PJ[state["pj"]]; state["pj"] ^= 1
                for k in range(8):
                    mm(P.v(), sl_k(sl, k, ci * 128, 128), Hk(k, half * 512, 512), start=(k == 0), stop=(k == 7))
                evac(P.v(), half)

        def proj_vtok(sl, ci0):
            for b in range(NB):
                P = PJ[state["pj"]]; state["pj"] ^= 1
                for k in range(8):
                    mm(P.v(f1=256), Hk(k, b * 128, 128), sl_k(sl, k, ci0 * 128, 256), start=(k == 0), stop=(k == 7))
                src = P.v(f1=256).r("p (a h e) -> p a h e", a=2, h=2)
                dst = VPAD.v(f0=b * 512, f1=b * 512 + 512)
                base = dst.ap
                d4 = bass.AP(tensor=base.tensor, offset=base.offset, ap=[base.ap[0], [256, 2], [192, 2], [1, 64]])
                p.op("scalar", lambda E, s=src, d4=d4: E.copy(d4, s.ap), reads=[src], writes=[dst])

        for c in range(8):
            p.dma("sync", X.v(f0=c * 1024, f1=(c + 1) * 1024).ap, xT[c * 128:(c + 1) * 128, :],
                  writes=[X.v(f0=c * 1024, f1=(c + 1) * 1024)])
        p.dma("sync", PRM.v().ap, prm, writes=[PRM.v()])
        p.dma("sync", CST.v().ap, cst, writes=[CST.v()])
        p.dma("gpsimd", ONES.v().ap, onesd, writes=[ONES.v()])
        memset(SRING.v(), 0.0)
        zsrc = cst_v("mFB", 0, 1024)
        for i in range(4):
            ts(VPAD.v(f0=i * 1024, f1=i * 1024 + 1024), zsrc, 0.0, ALU.mult)
        ts(SRR.v(), zsrc, 0.0, ALU.mult)
        memset(OSR.v(), 0.0)

        def make_srep():
            act(SM.v(f0=0, f1=8), prm_v("cvec", 0, 8), AF.Silu)
            srep3 = SREP.v().r("p (k m) -> p k m", k=8)
            sin3 = SM.v(f0=0, f1=8)
            sin_ap = sin3.ap.rearrange("p (k o) -> p k o", o=1).broadcast_to([128, 8, 128])
            p.op("vector", lambda E: E.tensor_copy(srep3.ap, sin_ap), reads=[sin3], writes=[SREP.v()])

        def rmsnorm_to_H(gs_v_fn, shift_fn):
            rms_stats()
            rms_apply(gs_v_fn, shift_fn)

        def rms_stats():
            sb_ = [PMS, PX]
            for half in range(2):
                for c in range(8):
                    act(H.v(f0=c * 1024 + half * 512, f1=c * 1024 + half * 512 + 512).bc(),
                        X.v(f0=c * 1024 + half * 512, f1=c * 1024 + half * 512 + 512), AF.Square)
                for k in range(8):
                    mm(sb_[half].v(), ONES.v(), Hk(k, half * 512, 512), start=(k == 0), stop=(k == 7))
                act(tTMP.v(f0=half * 512, f1=half * 512 + 512), sb_[half].v(), AF.Ln, bias=prm_eps, scale=1.0 / D)
                act(tTMP.v(f0=half * 512, f1=half * 512 + 512), tTMP.v(f0=half * 512, f1=half * 512 + 512), AF.Exp, scale=-0.5)

        def rms_apply(gs_v_fn, shift_fn):
            for c in range(8):
                xv = X.v(f0=c * 1024, f1=(c + 1) * 1024); hv = H.v(f0=c * 1024, f1=(c + 1) * 1024)
                if shift_fn is None:
                    ot = (tQ if c % 2 == 0 else tKF).v()
                    stt(ot, xv, gs_v_fn(c), tTMP.v(), ALU.mult, ALU.mult)
                    p.dma("sync", yT[c * 128:(c + 1) * 128, :], ot.ap, reads=[ot])
                else:
                    tmp = (tCUM if c % 2 == 0 else tKH).v()
                    stt(tmp, xv, gs_v_fn(c), tTMP.v(), ALU.mult, ALU.mult)
                    act(hv.bc(), tmp, AF.Identity, bias=shift_fn(c))

        memset(SM.v(f0=8, f1=9), 1e-6, eng="vector")
        prm_eps = SM.v(f0=8, f1=9)

        def hv2(t, h):
            return t.v(f0=h * 512, f1=h * 512 + 512)

        def gla_prep_part1(d, Kt, LAt, CS=32):
            NCK = T // CS
            NH = NCK // 2
            th = []
            if CS == 32:
                for h in range(2):
                    m = cst_v("mFB", (0 if d == 0 else 1) + h * 512, 512)
                    cv = hv2(tCUM, h); lv = hv2(LAt, h)
                    if d == 0:
                        th.append(lambda cv=cv, lv=lv, m=m: p.op("vector", lambda E: E.tensor_tensor_scan(
                            out=cv.ap, data0=m.ap, data1=lv.ap, initial=0.0, op0=ALU.mult, op1=ALU.add), reads=[m, lv], writes=[cv]))
                    else:
                        th.append(lambda cv=cv, lv=lv, m=m: p.op("vector", lambda E: E.tensor_tensor_scan(
                            out=cv.ap[:, ::-1], data0=m.ap[:, ::-1], data1=lv.ap[:, ::-1], initial=0.0,
                            op0=ALU.mult, op1=ALU.add), reads=[m, lv], writes=[cv]))
            else:
                onesv = cst_v("mf128", 127, 1)
                ones_b = onesv.ap.broadcast_to([128, CS])

                def scans(n0, n1):
                    for n in range(n0, n1):
                        cv = tCUM.v(f0=n * CS, f1=(n + 1) * CS); lv = LAt.v(f0=n * CS, f1=(n + 1) * CS)
                        if d == 0:
                            p.op("vector", lambda E, cv=cv, lv=lv: E.tensor_tensor_scan(out=cv.ap, data0=ones_b, data1=lv.ap, initial=0.0,
                                                                                    op0=ALU.mult, op1=ALU.add), reads=[onesv, lv], writes=[cv])
                        else:
                            p.op("vector", lambda E, cv=cv, lv=lv: E.tensor_tensor_scan(out=cv.ap[:, ::-1], data0=ones_b, data1=lv.ap[:, ::-1],
                                                                                    initial=0.0, op0=ALU.mult, op1=ALU.add),
                                 reads=[onesv, lv], writes=[cv])
                th.append(lambda: scans(0, NH))
                th.append(lambda: scans(NH, NCK))
            li = CS - 1 if d == 0 else 0
            for h in range(2):
                cvh = hv2(tCUM, h); tmh = hv2(tTMP, h)
                c3 = cvh.r("p (n j) -> p n j", j=CS)
                lastb = c3.ap[:, :, li:li + 1].broadcast_to([128, NH, CS])
                t3 = tmh.r("p (n j) -> p n j", j=CS)
                th.append(lambda c3=c3, lastb=lastb, t3=t3, cvh=cvh, tmh=tmh: p.op(
                    "vector", lambda E: E.tensor_tensor(out=t3.ap, in0=lastb, in1=c3.ap, op=ALU.subtract), reads=[cvh], writes=[tmh]))
            for h in range(2):
                th.append(lambda h=h: act(hv2(tTMP, h), hv2(tTMP, h), AF.Exp))
            late = [lambda h=h: tt(hv2(tKH, h), hv2(Kt, h), hv2(tTMP, h), ALU.mult) for h in range(2)]
            cumv = tCUM.v()
            c3a = cumv.r("p (n j) -> p n j", j=CS)
            last = c3a.w(c3a.ap[:, :, li:li + 1])
            ach = ACH.v(f0=d * 32, f1=d * 32 + NCK)
            th.append(lambda: p.op("scalar", lambda E: E.activation(out=ach.ap, in_=last.ap.rearrange("p n o -> p (n o)"), func=AF.Exp),
                                   reads=[cumv], writes=[ach]))
            return th, late, ach

        def gla_prep_part2(Kt):
            for h in range(2):
                act(hv2(tTMP, h), hv2(tCUM, h), AF.Exp)
            for h in range(2):
                tt(hv2(tQT, h), hv2(tQ, h), hv2(tTMP, h), ALU.mult)
            for h in range(2):
                act(hv2(tTMP, h), hv2(tCUM, h), AF.Exp, scale=-1.0)
            for h in range(2):
                tt(hv2(tKT, h), hv2(Kt, h), hv2(tTMP, h), ALU.mult)

        def gla_core(mix_id, l, ch, prep_hooks, gain_v, os_dram, CS=32):
            ident = cst_v("ident", 0, 128)
            prefetch2()
            for d in range(2):
                si = ((mix_id * 2 + l) * 2 + d) * 2 + ch
                dv = SRING.v(f0=0, f1=64)
                p.dma("sync", dv.ap, st0[:, si * 64:si * 64 + 64], writes=[dv])
                CPB = 128 // CS; SEGC = 256 // CS
                if d == 0:
                    th0, late0, ach = gla_prep_part1(0, tKF, tLAF, CS)
                    for f in list(prep_hooks[0]) + th0 + late0:
                        f()
                    th1, late1, ach1 = gla_prep_part1(1, tKF, tLAF, CS)
                    pending = list(prep_hooks[1]) + th1
                else:
                    for f in pending + late1:
                        f()
                    pending = []
                    ach = ach1
                blocks = list(range(NB)) if d == 0 else list(range(NB - 1, -1, -1))
                gla_prep_part2(tKF)
                def ds_front(b):
                    ptr = PTR.v(f0=(b % 2) * 128, f1=(b % 2) * 128 + 128)
                    p.op("tensor", lambda E, b=b, ptr=ptr: E.transpose(ptr.ap, tKH.v(f0=b * 128, f1=b * 128 + 128).ap, ident.ap),
                         reads=[tKH.v(f0=b * 128, f1=b * 128 + 128), ident], writes=[ptr])
                    cp(KHT.v(f0=(b % 2) * 128, f1=(b % 2) * 128 + 128), ptr, eng="scalar")
                    if CS == 128:
                        return
                    vsrc = VPAD.v(f0=b * 512 + ch * 256, f1=b * 512 + ch * 256 + 256).bc(F32)
                    base = vsrc.ap
                    vin = bass.AP(tensor=base.tensor, offset=base.offset, ap=[base.ap[0], [192, 2], [0, 4], [1, 64]])
                    bmv = cst_v("bm", 0, 4)
                    bmin = bass.AP(tensor=bmv.ap.tensor, offset=bmv.ap.offset, ap=[bmv.ap.ap[0], [0, 2], [1, 4], [0, 64]])
                    vb = VBD.v(f0=(b % 2) * 512, f1=(b % 2) * 512 + 512)
                    vout = vb.r("p (h n e) -> p h n e", h=2, n=4)
                    p.op("vector", lambda E, vout=vout, vin=vin, bmin=bmin: E.tensor_tensor(out=vout.ap, in0=vin, in1=bmin, op=ALU.mult),
                         reads=[vsrc, bmv], writes=[vb])

                def ds_back(b):
                    vb = VBD.v(f0=(b % 2) * 512, f1=(b % 2) * 512 + 512)
                    pds = PDS if b % 2 == 0 else PMS
                    if CS == 128:
                        mm(pds.v(f1=256), KHT.v(f0=(b % 2) * 128, f1=(b % 2) * 128 + 128),
                           VPAD.v(f0=b * 512 + ch * 256, f1=b * 512 + ch * 256 + 256))
                        cp(DS.v(p0=0, p1=64, f0=b * 64, f1=b * 64 + 64), pds.v(p0=0, p1=64, f0=0, f1=64), eng="scalar")
                        cp(DS.v(p0=64, p1=128, f0=b * 64, f1=b * 64 + 64), pds.v(p0=64, p1=128, f0=192, f1=256), eng="scalar")
                        return
                    mm(pds.v(), KHT.v(f0=(b % 2) * 128, f1=(b % 2) * 128 + 128), vb)
                    cp(DS.v(p0=0, p1=64, f0=b * 256, f1=b * 256 + 256), pds.v(p0=0, p1=64, f0=0, f1=256), eng="scalar")
                    cp(DS.v(p0=64, p1=128, f0=b * 256, f1=b * 256 + 256), pds.v(p0=64, p1=128, f0=256, f1=512), eng="scalar")

                ds_front(blocks[0])
                def cast_state(sl_):
                    for hh in range(2):
                        cp(SRR.v(p0=hh * 64, p1=hh * 64 + 64, f0=sl_ * 128 + hh * 64, f1=sl_ * 128 + hh * 64 + 64),
                           SRING.v(p0=hh * 64, p1=hh * 64 + 64, f0=sl_ * 64, f1=sl_ * 64 + 64), eng="scalar")
                cast_state(0)
                mask = cst_v("mf" if CS == 32 else "mf128", 0, 128) if d == 0 else cst_v("mb" if CS == 32 else "mb128", 0, 128)
                SCB = [(PSC, PX), (PSC, PX)]
                ACB = [PAC, PJ[0], PJ[1]]
                st_ = {"slot": 0}

                def stage_scores(i):
                    b = blocks[i]
                    for hh in range(2):
                        mm(SCB[i % 2][hh].v(f1=128),
                           tKT.v(p0=hh * 64, p1=hh * 64 + 64, f0=b * 128, f1=b * 128 + 128),
                           tQT.v(p0=hh * 64, p1=hh * 64 + 64, f0=b * 128, f1=b * 128 + 128))
                    for hh in range(2):
                        tt(PT.v(f0=(i % 2) * 256 + hh * 128, f1=(i % 2) * 256 + hh * 128 + 128), SCB[i % 2][hh].v(f1=128), mask, ALU.mult)

                def stage_main(i):
                    b = blocks[i]
                    accb = ACB[i % 3]
                    acc = accb.v(f1=128)
                    for hh in range(2):
                        h = 2 * ch + hh
                        mm(acc, VPAD.v(f0=b * 512 + h * 128, f1=b * 512 + h * 128 + 128),
                           PT.v(f0=(i % 2) * 256 + hh * 128, f1=(i % 2) * 256 + hh * 128 + 128), start=(hh == 0), stop=False)
                    chunks = list(range(CPB * b, CPB * b + CPB)) if d == 0 else list(range(CPB * b + CPB - 1, CPB * b - 1, -1))
                    for ci, n in enumerate(chunks):
                        slot = st_["slot"]
                        mm(accb.v(f0=(n % CPB) * CS, f1=(n % CPB) * CS + CS),
                           SRR.v(f0=slot * 128, f1=slot * 128 + 128), tQT.v(f0=n * CS, f1=n * CS + CS),
                           start=False, stop=(ci == CPB - 1))
                        nslot = (slot + 1) % 8
                        stt(SRING.v(f0=nslot * 64, f1=nslot * 64 + 64), SRING.v(f0=slot * 64, f1=slot * 64 + 64),
                            V(ach.ap[:, n:n + 1], ach.name, 0, 128, ach.f0 + n, ach.f0 + n + 1),
                            DS.v(f0=n * 64, f1=n * 64 + 64), ALU.mult, ALU.add)
                        slot = nslot
                        st_["slot"] = slot
                        bnd = (n % SEGC == SEGC - 1) if d == 0 else (n % SEGC == 0)
                        if bnd:
                            seg = n // SEGC
                            oi = ((l * 2 + d) * 4 + seg) * 2 + ch
                            sv2 = SRING.v(f0=slot * 64, f1=slot * 64 + 64)
                            p.dma("sync", os_dram[oi], sv2.ap, reads=[sv2])
                            ts(sv2, sv2, flag_s, ALU.mult)
                        cast_state(slot)

                def stage_evac(i):
                    b = blocks[i]
                    acc = ACB[i % 3].v(f1=128)
                    ov = tO.v(f0=b * 128, f1=b * 128 + 128)
                    if d == 0:
                        cp(ov, acc, eng=("vector" if CS == 32 else "scalar"))
                    else:
                        tt(ov, ov, acc, ALU.add)

                for step in range(NB + 2):
                    if step < NB:
                        stage_scores(step)
                        if step + 1 < NB:
                            ds_front(blocks[step + 1])
                        ds_back(blocks[step])
                    if 1 <= step <= NB:
                        stage_main(step - 1)
                    if step >= 2:
                        stage_evac(step - 2)
                    if d == 0 and pending:
                        for _ in range(2 if len(pending) > NB + 1 - step else 1):
                            if pending:
                                pending.pop(0)()
            o64 = cst_v("o64", 0, 128)
            hb = [PMS, PX]
            for h in range(2):
                act(hv2(tTMP, h), hv2(tO, h), AF.Square)
            for h in range(2):
                mm(hb[h].v(), o64, hv2(tTMP, h))
                act(hv2(tCUM, h), hb[h].v(), AF.Ln, bias=prm_eps)
            for h in range(2):
                act(hv2(tCUM, h), hv2(tCUM, h), AF.Exp, scale=-0.5)
            for h in range(2):
                tt(hv2(tO, h), hv2(tO, h), hv2(tCUM, h), ALU.mult)
            for h in range(2):
                stt(OM.v(f0=ch * 1024 + h * 512, f1=ch * 1024 + h * 512 + 512).bc(), hv2(tO, h), gain_v, hv2(tG, h), ALU.mult, ALU.mult)

        def wout_partial(l):
            sl = next_slot()
            for fo in range(8):
                for half in range(2):
                    P = [PX, PSC, PAC, PDS][state["rb2"] % 4]; state["rb2"] += 1
                    for kk in range(2):
                        mm(P.v(), sl.v(f0=kk * 1024 + fo * 128, f1=kk * 1024 + fo * 128 + 128),
                           OM.v(f0=kk * 1024 + half * 512, f1=kk * 1024 + half * 512 + 512).bc(), start=(kk == 0), stop=(kk == 1))
                    xv = X.v(f0=fo * 1024 + half * 512, f1=fo * 1024 + half * 512 + 512)
                    stt(xv, P.v(), MOD.v(f0=16 + fo, f1=17 + fo), xv, ALU.mult, ALU.add)

        one11 = cst_v("one", 0, 1, p0=0, p1=1)

        def ada_slot(g, lay, stg=None):
            stg = DS if stg is None else stg
            sl = next_slot()
            for k in range(8):
                mm(PMS.v(), SREP.v(f0=k * 128, f1=k * 128 + 128), sl_k(sl, k, 0, 512), start=(k == 0), stop=(k == 7))
            cp(stg.v(p0=0, p1=1, f0=0, f1=512), PMS.v(p0=0, p1=1), eng="scalar")
            for j in range(4):
                mm(PTR.v(f0=256 + 4 * g + j, f1=257 + 4 * g + j), stg.v(p0=0, p1=1, f0=j * 128, f1=j * 128 + 128), one11)
            o = (lay % 2) * 48
            cp(MODN.v(f0=o + 4 * g, f1=o + 4 * g + 4), PTR.v(f0=256 + 4 * g, f1=260 + 4 * g))

        def mod_commit(lay, c0, c1):
            o = (lay % 2) * 48
            tt(MOD.v(f0=c0, f1=c1), MODN.v(f0=o + c0, f1=o + c1), prm_v("adab%d" % lay, c0, c1 - c0), ALU.add)

        for l in range(L):
            if l == 0:
                make_srep()
                rms_stats()
                for g in range(4):
                    ada_slot(g, 0)
                mod_commit(0, 0, 16)
            else:
                mod_commit(l, 0, 48)
                stt(GS.v(f0=8, f1=16), MOD.v(f0=32, f1=40), 1.0, prm_v("n2g%d" % l, 0, 8), ALU.add, ALU.mult)
            stt(GS.v(f0=0, f1=8), MOD.v(f0=8, f1=16), 1.0, prm_v("n1g%d" % l, 0, 8), ALU.add, ALU.mult)
            LB = SM.v(f0=48, f1=52); OML = SM.v(f0=52, f1=56)
            if l == 0:
                memset(LB, 0.0, eng="vector")
            else:
                tt(LB, prm_v("lbz1", 0, 4), prm_v("lbz0", 0, 4), ALU.subtract)
                act(LB, LB, AF.Sigmoid)
            ts(OML, LB, -1.0, ALU.mult, 1.0, ALU.add)
            C8 = SM.v(f0=56, f1=60)
            act(C8, prm_v("rglam%d" % l, 0, 4), AF.Exp, scale=-1.0)
            act(C8, C8, AF.Ln, bias=1.0)
            ts(C8, C8, -8.0, ALU.mult)
            NBA = SM.v(f0=60, f1=64)
            ts(NBA, prm_v("ba2%d" % l, 0, 4), -1.0, ALU.mult)

            if l > 0:
                rms_stats()
            rms_apply(lambda c: GS.v(f0=c, f1=c + 1), lambda c: MOD.v(f0=c, f1=c + 1))

            slD1 = next_slot()
            TB, TC, TV, U, Y = tQ, tKF, tLAF, tCUM, DS.sub(0, 1024)
            scw = lambda cc, j: prm_v("scw%d" % l, cc * 3 + j)
            WP = SM.v(f0=9, f1=11)

            def sconv(cc):
                ts(WP.w(WP.ap[:, 0:1]), scw(cc, 0), flag_p, ALU.mult)
                ts(WP.w(WP.ap[:, 1:2]), scw(cc, 2), flag_p, ALU.mult)
                tt(U.v(), TC.v(), TV.v(), ALU.mult)
                ts(Y.v(), U.v(), scw(cc, 1), ALU.mult)
                y3 = Y.v().r("p (s j) -> p s j", j=64); u3 = U.v().r("p (s j) -> p s j", j=64)
                w0v = scw(cc, 0); w2v = scw(cc, 2)
                p.op("vector", lambda E: E.scalar_tensor_tensor(out=y3.ap[:, :, 1:64], in0=u3.ap[:, :, 0:63], scalar=w0v.ap,
                                                                in1=y3.ap[:, :, 1:64], op0=ALU.mult, op1=ALU.add),
                     reads=[U.v(), Y.v(), w0v], writes=[Y.v()])
                p.op("vector", lambda E: E.scalar_tensor_tensor(out=y3.ap[:, :, 0:63], in0=u3.ap[:, :, 1:64], scalar=w2v.ap,
                                                                in1=y3.ap[:, :, 0:63], op0=ALU.mult, op1=ALU.add),
                     reads=[U.v(), Y.v(), w2v], writes=[Y.v()])
                y4 = Y.v().r("p (a b j) -> p a b j", a=4, b=4); u4 = U.v().r("p (a b j) -> p a b j", a=4, b=4)
                p.op("vector", lambda E: E.scalar_tensor_tensor(out=y4.ap[:, :, 1:4, 0:1], in0=u4.ap[:, :, 0:3, 63:64], scalar=WP.ap[:, 0:1],
                                                                in1=y4.ap[:, :, 1:4, 0:1], op0=ALU.mult, op1=ALU.add),
                     reads=[U.v(), Y.v(), WP], writes=[Y.v()])
                p.op("vector", lambda E: E.scalar_tensor_tensor(out=y4.ap[:, :, 0:3, 63:64], in0=u4.ap[:, :, 1:4, 0:1], scalar=WP.ap[:, 1:2],
                                                                in1=y4.ap[:, :, 0:3, 63:64], op0=ALU.mult, op1=ALU.add),
                     reads=[U.v(), Y.v(), WP], writes=[Y.v()])
                tt(OM.v(f0=cc * 1024, f1=cc * 1024 + 1024).bc(), TB.v(), Y.v(), ALU.mult)

            def ev_copy(tile):
                return lambda pv, half: cp(tile.v(f0=half * 512, f1=half * 512 + 512), pv, eng="scalar")

            proj_chunk(slD1, 0, ev_copy(TB)); proj_chunk(slD1, 1, ev_copy(TC)); proj_chunk(slD1, 2, ev_copy(TV))
            sconv(0)
            proj_chunk(slD1, 3, ev_copy(TB))
            slD2 = next_slot()
            for i in range(8):
                cp(AUXC.v(f0=i * 128, f1=i * 128 + 128), sl_k(slD2, i, 256, 128).bc(F32))
            proj_chunk(slD2, 0, ev_copy(TC)); proj_chunk(slD2, 1, ev_copy(TV))
            sconv(1)
            if l == 0:
                for g in (4, 5):
                    ada_slot(g, 0, tKH)
                mod_commit(0, 16, 24)
            wout_partial(l)

            slC = next_slot()
            Ux = [tQ, tKF]; GG = [tLAF, tG]
            for cc in range(2):
                proj_chunk(slC, cc, ev_copy(Ux[cc]))
            for cc in range(2):
                proj_chunk(slC, 2 + cc, lambda pv, half, cc=cc: act(GG[cc].v(f0=half * 512, f1=half * 512 + 512), pv, AF.Gelu_apprx_tanh))
            XC, R, IG, HS = tCUM, DS.sub(0, 1024), DS.sub(1024, 1024), tKH
            GB = [PMS, PX, PSC, PAC]
            prefetch2()

            def hv_(t, h):
                return t.v(f0=h * 512, f1=h * 512 + 512)

            for cc in range(2):
                u = Ux[cc]
                for d in range(2):
                    di = d * 2 + cc
                    cw = [prm_v("rgcw%d" % l, di * 4 + j) for j in range(4)]
                    cb = prm_v("rgcb%d" % l, di)
                    c8 = SM.v(f0=56 + di, f1=57 + di)
                    WS = SM.v(f0=11, f1=15)
                    for j in range(4):
                        ts(WS.w(WS.ap[:, j:j + 1]), cw[j], flag_s, ALU.mult)
                    HSd = tO if d == 0 else HS
                    horder = [0, 1] if d == 0 else [1, 0]
                    x3 = XC.v().r("p (a j) -> p a j", j=256); u3 = u.v().r("p (a j) -> p a j", j=256)

                    def conv(h):
                        xh = hv_(XC, h)
                        ts(xh, hv_(u, h), cw[3], ALU.mult, cb, ALU.add)
                        a0, a1 = 2 * h, 2 * h + 2
                        for s in range(1, 4):
                            if d == 0:
                                o1, i1 = x3.ap[:, a0:a1, s:256], u3.ap[:, a0:a1, 0:256 - s]
                                c0 = max(a0, 1)
                                o2, i2 = x3.ap[:, c0:a1, 0:s], u3.ap[:, c0 - 1:a1 - 1, 256 - s:256]
                            else:
                                o1, i1 = x3.ap[:, a0:a1, 0:256 - s], u3.ap[:, a0:a1, s:256]
                                c1 = min(a1, 3)
                                o2, i2 = x3.ap[:, a0:c1, 256 - s:256], u3.ap[:, a0 + 1:c1 + 1, 0:s]
                            cws = cw[3 - s]
                            p.op("vector", lambda E, o1=o1, i1=i1, cws=cws: E.scalar_tensor_tensor(out=o1, in0=i1, scalar=cws.ap, in1=o1,
                                                                                      op0=ALU.mult, op1=ALU.add),
                                 reads=[u.v(), xh, cws], writes=[xh])
                            p.op("vector", lambda E, o2=o2, i2=i2, s=s: E.scalar_tensor_tensor(out=o2, in0=i2, scalar=WS.ap[:, 3 - s:4 - s], in1=o2,
                                                                                      op0=ALU.mult, op1=ALU.add),
                                 reads=[u.v(), xh, WS], writes=[xh])

                    def gates(h):
                        for wh, (dst, bname) in enumerate(((R, "rgbr%d" % l), (IG, "rgbi%d" % l))):
                            wm = AUXC.v(f0=(di * 2 + wh) * 128, f1=(di * 2 + wh) * 128 + 128)
                            gb = GB[h * 2 + wh]
                            mm(gb.v(), wm, hv_(XC, h))
                            act(hv_(dst, h), gb.v(), AF.Sigmoid, bias=prm_v(bname, di))

                    def scans(h):
                        segs = [2 * h, 2 * h + 1] if d == 0 else [2 * h + 1, 2 * h]
                        for sg in segs:
                            first = (sg == 0) if d == 0 else (sg == 3)
                            if first:
                                init = prm_v("str%d" % l, di)
                            else:
                                prev = HSd.v(f0=sg * 256 - 1, f1=sg * 256) if d == 0 else HSd.v(f0=sg * 256 + 256, f1=sg * 256 + 257)
                                init = ACH.v(f0=sg, f1=sg + 1)
                                ts(init, prev, flag_s, ALU.mult)
                            hv = HSd.v(f0=sg * 256, f1=sg * 256 + 256); av = R.v(f0=sg * 256, f1=sg * 256 + 256); bv = IG.v(f0=sg * 256, f1=sg * 256 + 256)
                            if d == 0:
                                p.op("vector", lambda E, hv=hv, av=av, bv=bv, init=init: E.tensor_tensor_scan(
                                    out=hv.ap, data0=av.ap, data1=bv.ap, initial=init.ap, op0=ALU.mult, op1=ALU.add),
                                    reads=[av, bv, init], writes=[hv])
                            else:
                                p.op("vector", lambda E, hv=hv, av=av, bv=bv, init=init: E.tensor_tensor_scan(
                                    out=hv.ap[:, ::-1], data0=av.ap[:, ::-1], data1=bv.ap[:, ::-1], initial=init.ap, op0=ALU.mult, op1=ALU.add),
                                    reads=[av, bv, init], writes=[hv])

                    for h in horder:
                        conv(h)
                    for h in horder:
                        gates(h)
                    for h in horder:
                        act(hv_(R, h), hv_(R, h), AF.Exp, scale=c8)
                    for h in horder:
                        tt(hv_(tTMP, h), hv_(R, h), hv_(R, h), ALU.mult)
                    for h in horder:
                        act(hv_(tTMP, h), hv_(tTMP, h), AF.Ln, bias=1.0, scale=-1.0)
                    for h in horder:
                        act(hv_(tTMP, h), hv_(tTMP, h), AF.Exp, scale=0.5)
                    for h in horder:
                        tt(hv_(IG, h), hv_(IG, h), hv_(tTMP, h), ALU.mult)
                    for h in horder:
                        tt(hv_(IG, h), hv_(IG, h), hv_(XC, h), ALU.mult)
                    for h in horder:
                        scans(h)
                        if d == 1:
                            tt(hv_(tO, h), hv_(tO, h), hv_(HS, h), ALU.add)
                    oc = ((l * 2 + d) * 2 + cc) * 4
                    h4 = HSd.v().r("p (a j) -> p a j", j=256)
                    fin = h4.w(h4.ap[:, :, 255:256] if d == 0 else h4.ap[:, :, 0:1])
                    o4 = OSR.v(f0=oc, f1=oc + 4)
                    p.op("vector", lambda E, o4=o4, fin=fin: E.tensor_copy(o4.ap.rearrange("p (a o) -> p a o", o=1), fin.ap),
                         reads=[HSd.v()], writes=[o4])
                tt(OM.v(f0=cc * 1024, f1=cc * 1024 + 1024).bc(), tO.v(), GG[cc].v(), ALU.mult)
            wout_partial(l)

            slBv = next_slot()
            proj_vtok(slBv, 0)
            for i in range(4):
                cp(WA2T.v(f0=i * 128, f1=i * 128 + 128), sl_k(slBv, i, 256, 128).bc(F32))
            sc = 32 ** -0.5
            for pair in range(2):
                slB = next_slot()
                proj_chunk(slB, 0, lambda pv, half: act(tQ.v(f0=half * 512, f1=half * 512 + 512), pv, AF.Identity, scale=sc))
                proj_chunk(slB, 1, ev_copy(tKF))
                proj_chunk(slB, 2, lambda pv, half: act(tG.v(f0=half * 512, f1=half * 512 + 512), pv, AF.Silu))
                proj_chunk(slB, 3, ev_copy(AUXC))

                def gla_hook(d, pair=pair):
                    i = d * 2 + pair
                    th = []
                    for half in range(2):
                        def f(half=half, i=i):
                            mm(PMS.v(), WA2T.v(f0=i * 128, f1=i * 128 + 128), AUXC.v(f0=half * 512, f1=half * 512 + 512))
                            act(tLAF.v(f0=half * 512, f1=half * 512 + 512), PMS.v(), AF.Exp, bias=SM.v(f0=60 + i, f1=61 + i), scale=-1.0)
                        th.append(f)
                    for h in range(2):
                        th.append(lambda h=h: act(hv2(tLAF, h), hv2(tLAF, h), AF.Ln, bias=1.0))
                    for h in range(2):
                        th.append(lambda h=h: ts(hv2(tLAF, h), hv2(tLAF, h), -1.0 / 16.0, ALU.mult))
                    return th
                gla_core(1, l, pair, [gla_hook(0), gla_hook(1)], prm_v("ggain%d" % l, pair), osB, CS=128)
                if l == 0:
                    for g in (6 + 2 * pair, 7 + 2 * pair):
                        ada_slot(g, 0)
                if l + 1 < L:
                    for g in range(3 * pair, 3 * pair + 3):
                        ada_slot(g, l + 1)
            wout_partial(l)

            slAv = next_slot()
            proj_vtok(slAv, 0)
            for pair in range(2):
                slA = next_slot()
                proj_chunk(slA, 0, ev_copy(tQ))
                proj_chunk(slA, 1, lambda pv, half: act(tLAF.v(f0=half * 512, f1=half * 512 + 512), pv, AF.Sigmoid))
                proj_chunk(slA, 2, lambda pv, half: act(AUXC.v(f0=half * 512, f1=half * 512 + 512), pv, AF.Sigmoid))
                proj_chunk(slA, 3, lambda pv, half: act(tG.v(f0=half * 512, f1=half * 512 + 512), pv, AF.Silu))

                def hg_hook(d, pair=pair):
                    i = d * 2 + pair
                    src = tLAF if d == 0 else AUXC
                    th = []
                    for h in range(2):
                        th.append(lambda h=h: ts(hv2(tLAF, h), hv2(src, h), SM.v(f0=52 + i, f1=53 + i), ALU.mult, SM.v(f0=48 + i, f1=49 + i), ALU.add))
                    for h in range(2):
                        th.append(lambda h=h: ts(hv2(tKF, h), hv2(tLAF, h), -1.0, ALU.mult, 1.0, ALU.add))
                    for h in range(2):
                        th.append(lambda h=h: ts(hv2(tLAF, h), hv2(tLAF, h), 1e-20, ALU.max))
                    for h in range(2):
                        th.append(lambda h=h: act(hv2(tLAF, h), hv2(tLAF, h), AF.Ln))
                    return th
                gla_core(0, l, pair, [hg_hook(0), hg_hook(1)], prm_v("hgain%d" % l, pair), osA)
                if l == 0 and pair == 0:
                    for g in (10, 11):
                        ada_slot(g, 0)
                    mod_commit(0, 24, 48)
                    stt(GS.v(f0=8, f1=16), MOD.v(f0=32, f1=40), 1.0, prm_v("n2g%d" % l, 0, 8), ALU.add, ALU.mult)
                if l + 1 < L:
                    for g in range(6 + 3 * pair, 6 + 3 * pair + 3):
                        ada_slot(g, l + 1)
            wout_partial(l)

            rmsnorm_to_H(lambda c: GS.v(f0=8 + c, f1=9 + c), lambda c: MOD.v(f0=24 + c, f1=25 + c))
            W1B = [PJ[0], PJ[1], PTR, PMS]
            W2B = [PX, PSC, PAC, PDS]
            for g in range(8):
                sl = next_slot()
                for jj in range(4):
                    for half in range(2):
                        P = W1B[state["rb"] % 4]; state["rb"] += 1
                        for k in range(8):
                            mm(P.v(), sl_k(sl, k, jj * 128, 128), Hk(k, half * 512, 512), start=(k == 0), stop=(k == 7))
                        hv = HID.v(f0=jj * 1024 + half * 512, f1=jj * 1024 + half * 512 + 512)
                        sq = SQT.v(f0=(state["rb"] % 2) * 512, f1=(state["rb"] % 2) * 512 + 512)
                        act(sq, P.v(), AF.Square)
                        stt(hv, P.v(), 0.0, sq, ALU.is_gt, ALU.mult)
                sl = next_slot()
                for fo in range(8):
                    for half in range(2):
                        P = W2B[state["rb2"] % 4]; state["rb2"] += 1
                        for j in range(4):
                            mm(P.v(), sl.v(f0=j * 1024 + fo * 128, f1=j * 1024 + fo * 128 + 128),
                               HID.v(f0=j * 1024 + half * 512, f1=j * 1024 + half * 512 + 512), start=(j == 0), stop=(j == 3))
                        xv = X.v(f0=fo * 1024 + half * 512, f1=fo * 1024 + half * 512 + 512)
                        stt(xv, P.v(), MOD.v(f0=40 + fo, f1=41 + fo), xv, ALU.mult, ALU.add)

        rmsnorm_to_H(lambda c: prm_v("fng", c), None)
        p.dma("sync", osr, OSR.v().ap, reads=[OSR.v()])
        p.finish("sync")
        with nc.Block() as block:
            p.emit(block)
    return nc


def kernel(**inp):
    inp = {k: np.asarray(v) for k, v in inp.items()}
    wst = build_stream(inp)
    co = build_consts()
    cst = co.get()
    pos = [build_params(inp, c) for c in range(8)]
    nc = build_program(pos[0], co, wst.shape[0])
    ones = np.ones((128, 128), np.float32)
    in_maps = []
    for c in range(8):
        if c < 4:
            x = inp["x_sample"][c]
        else:
            x = inp["x_prompt"][4 * (c - 4):4 * (c - 4) + 4].reshape(1024, 1024)
        in_maps.append({"xT": np.ascontiguousarray(x.T), "wst": wst, "prm": pos[c].get(), "cst": cst,
                        "st0": build_st0(inp, c), "onesd": ones})
    res = run_bass_kernel_spmd(nc, in_maps, core_ids=list(range(8)))
    R = res.results
    y_sample = np.stack([R[c]["yT"].T for c in range(4)], 0).astype(np.float32)
    y_prompt = np.concatenate([R[c]["yT"].T.reshape(4, 256, 1024) for c in range(4, 8)], 0).astype(np.float32)
    nh = np.zeros((16, L, 2, 4, 64, 64), np.float32)
    ng = np.zeros((16, L, 2, 4, 32, 64), np.float32)
    nr = np.zeros((16, L, 2, 256), np.float32)
    for c in range(4, 8):
        oa, ob, orr = R[c]["osA"], R[c]["osB"], R[c]["osr"]
        for seg in range(4):
            bi = 4 * (c - 4) + seg
            for l in range(L):
                for d in range(2):
                    for ch in range(2):
                        oi = ((l * 2 + d) * 4 + seg) * 2 + ch
                        for hh in range(2):
                            nh[bi, l, d, 2 * ch + hh] = oa[oi, hh * 64:hh * 64 + 64, :]
                            ng[bi, l, d, 2 * ch + hh] = ob[oi, hh * 64:hh * 64 + 32, :]
                    for cc in range(2):
                        oc = ((l * 2 + d) * 2 + cc) * 4 + seg
                        nr[bi, l, d, cc * 128:(cc + 1) * 128] = orr[:, oc]
    return (y_prompt, y_sample, nh, ng, nr)
```

```python
import numpy as np
from contextlib import ExitStack
import concourse.bass as bass
import concourse.mybir as mybir
from concourse.bass_utils import run_bass_kernel_spmd

F32 = mybir.dt.float32
F32R = mybir.dt.float32r
AF = mybir.ActivationFunctionType
ALU = mybir.AluOpType
ENGS = ("tensor", "vector", "scalar", "gpsimd", "sync")

D = 1024; L = 2; T = 1024; NB = 8; NCH = 32
PROJ_W = 3360


class V:
    def __init__(self, ap, name, p0, p1, f0, f1):
        self.ap, self.name, self.p0, self.p1, self.f0, self.f1 = ap, name, p0, p1, f0, f1

    def reg(self):
        return (self.name, self.p0, self.p1, self.f0, self.f1)

    def w(self, ap):
        return V(ap, self.name, self.p0, self.p1, self.f0, self.f1)

    def r(self, pattern, **kw):
        return self.w(self.ap.rearrange(pattern, **kw))

    def bc(self, dt=F32R):
        return self.w(self.ap.bitcast(dt))


class TL:
    def __init__(self, handle, name, P, F, base=0):
        self.h, self.name, self.P, self.F, self.base = handle, name, P, F, base

    def v(self, p0=0, p1=None, f0=0, f1=None):
        p1 = self.P if p1 is None else p1
        f1 = self.F if f1 is None else f1
        return V(self.h[p0:p1, self.base + f0:self.base + f1], self.name, p0, p1, self.base + f0, self.base + f1)

    def sub(self, f0, F):
        return TL(self.h, self.name, self.P, F, self.base + f0)


def _ov(a, b):
    return a[0] == b[0] and a[1] < b[2] and b[1] < a[2] and a[3] < b[4] and b[3] < a[4]


def _ct(a, b):
    return a[0] == b[0] and a[1] <= b[1] and b[2] <= a[2] and a[3] <= b[3] and b[4] <= a[4]


class Prog:
    def __init__(self, nc, es, n_dma_sems=12):
        self.nc = nc
        self.ops = {e: [] for e in ENGS}
        self.cnt = {e: 0 for e in ENGS}
        self.sems = {e: es.enter_context(nc.semaphore("s_" + e)) for e in ENGS}
        self.dma_sems = [es.enter_context(nc.semaphore("d%d" % i)) for i in range(n_dma_sems)]
        self.es = es
        self.nfixed = n_dma_sems
        self.dma_uses = [0] * n_dma_sems
        self.dma_next = 0
        self.waited = {e: {} for e in ENGS}
        self.recs = {}

    def _deps(self, eng, reads, writes):
        deps = []
        for v in reads:
            rg = v.reg()
            for (r2, kind, sk, val, e2) in self.recs.get(rg[0], ()):
                if kind == "w" and _ov(rg, r2):
                    if e2 == eng and eng == "tensor":
                        continue
                    deps.append((sk, val))
        for v in writes:
            rg = v.reg()
            if eng == "tensor":
                rg = (rg[0], rg[1], rg[2], 0, 1 << 30)
            for (r2, kind, sk, val, e2) in self.recs.get(rg[0], ()):
                if _ov(rg, r2):
                    if e2 == eng and eng == "tensor":
                        continue
                    deps.append((sk, val))
        return deps

    def _record(self, eng, reads, writes, sk, val):
        for v in writes:
            rg = v.reg()
            lst = self.recs.setdefault(rg[0], [])
            lst[:] = [x for x in lst if not _ct(rg, x[0])]
            lst.append((rg, "w", sk, val, eng))
        for v in reads:
            rg = v.reg()
            lst = self.recs.setdefault(rg[0], [])
            lst[:] = [x for x in lst if not (x[1] == "r" and x[2] == sk and _ct(rg, x[0]))]
            lst.append((rg, "r", sk, val, eng))

    def _sem(self, sk):
        return self.sems[sk] if isinstance(sk, str) else self.dma_sems[sk]

    def _emit_waits(self, eng, deps):
        best = {}
        for sk, val in deps:
            if val > best.get(sk, 0):
                best[sk] = val
        for sk, val in best.items():
            if val > self.waited[eng].get(sk, 0):
                self.waited[eng][sk] = val
                sem = self._sem(sk)
                self.ops[eng].append(lambda E, sem=sem, val=val: E.wait_ge(sem, val))

    def op(self, eng, fn, reads=(), writes=()):
        self._emit_waits(eng, self._deps(eng, reads, writes))
        self.cnt[eng] += 1
        sem = self.sems[eng]
        self.ops[eng].append(lambda E, fn=fn, sem=sem: fn(E).then_inc(sem, 1))
        self._record(eng, reads, writes, eng, self.cnt[eng])

    def dma(self, eng, out_ap, in_ap, reads=(), writes=()):
        if eng == "gpsimd":
            deps = self._deps(eng, reads, writes)
            self._emit_waits(eng, deps)
            sem = self.es.enter_context(self.nc.semaphore("w%d" % len(self.dma_sems)))
            self.dma_sems.append(sem)
            self.dma_uses.append(1)
            slot = len(self.dma_sems) - 1
            self.ops[eng].append(lambda E, o=out_ap, i=in_ap, sem=sem: E.dma_start(out=o, in_=i).then_inc(sem, 16))
            self._record("dma", reads, writes, slot, 16)
            return
        slot = self.dma_next
        self.dma_next = (self.dma_next + 1) % self.nfixed
        deps = self._deps(eng, reads, writes)
        if self.dma_uses[slot] > 0:
            deps.append((slot, 16 * self.dma_uses[slot]))
        self._emit_waits(eng, deps)
        self.dma_uses[slot] += 1
        sem = self.dma_sems[slot]
        self.ops[eng].append(lambda E, o=out_ap, i=in_ap, sem=sem: E.dma_start(out=o, in_=i).then_inc(sem, 16))
        self._record("dma", reads, writes, slot, 16 * self.dma_uses[slot])

    def finish(self, eng="sync"):
        self._emit_waits(eng, [(s, 16 * u) for s, u in enumerate(self.dma_uses) if u > 0])

    def emit(self, block):
        for e in ENGS:
            ops = self.ops[e]
            if not ops:
                continue

            def body(E, ops=ops):
                for o in ops:
                    o(E)

            getattr(block, e)(body)


class Packer:
    def __init__(self):
        self.cols = []
        self.off = {}
        self.n = 0

    def add(self, name, arr):
        arr = np.asarray(arr, np.float32)
        assert arr.shape[0] == 128
        self.off[name] = self.n
        self.cols.append(arr)
        self.n += arr.shape[1]

    def get(self):
        return np.ascontiguousarray(np.concatenate(self.cols, axis=1))


def fm(vec):
    return np.asarray(vec, np.float32).reshape(-1, 128).T


def kslot(mat):
    out = np.zeros((128, 8, 512), np.float32)
    out[:, :, :mat.shape[1]] = mat.reshape(8, 128, mat.shape[1]).transpose(1, 0, 2)
    return out


def build_stream(inp):
    slots = []
    def ada_slots(l, g0, n):
        if l >= L:
            return
        aw = inp["ada_w"][l]
        for g in range(g0, g0 + n):
            slots.append(kslot(aw[:, 512 * g:512 * g + 512]).reshape(128, 4096))

    for l in range(L):
        if l == 0:
            ada_slots(0, 0, 4)
        wi = inp["w_in"][l]; wo = inp["w_out"][l]

        def chunks(cols_list):
            s = np.zeros((128, 8, 512), np.float32)
            for i, m in enumerate(cols_list):
                if m is not None:
                    s[:, :, i * 128:(i + 1) * 128] = m.reshape(8, 128, 128).transpose(1, 0, 2)
            return s

        def wcols(c0, n=128):
            m = np.zeros((1024, 128), np.float32)
            m[:, :n] = wi[:, c0:c0 + n]
            return m

        def woslot(r0):
            s = np.zeros((128, 4096), np.float32)
            s[:, :2048] = wo[r0:r0 + 256].reshape(2, 128, 1024).transpose(1, 0, 2).reshape(128, 2048)
            return s

        dB, dC, dV = 2592, 2848, 3104
        slots.append(chunks([wcols(dB), wcols(dC), wcols(dV), wcols(dB + 128)]).reshape(128, 4096))
        s2 = chunks([wcols(dC + 128), wcols(dV + 128), None, None])
        for d in range(2):
            for cc in range(2):
                for wh, w in enumerate((inp["rg_w_r"][l], inp["rg_w_i"][l])):
                    i = (d * 2 + cc) * 2 + wh
                    bd = np.zeros((128, 128), np.float32)
                    for nn in range(2):
                        bd[nn * 64:(nn + 1) * 64, nn * 64:(nn + 1) * 64] = w[d, 2 * cc + nn]
                    s2[:, i, 256:384] = bd
        slots.append(s2.reshape(128, 4096))
        if l == 0:
            ada_slots(0, 4, 2)
        slots.append(woslot(768))
        cX, cG = 2080, 2336
        slots.append(chunks([wcols(cX), wcols(cX + 128), wcols(cG), wcols(cG + 128)]).reshape(128, 4096))
        slots.append(woslot(512))
        def padqk(c0, ch):
            m = np.zeros((1024, 128), np.float32)
            for hh in range(2):
                h = 2 * ch + hh
                m[:, hh * 64:hh * 64 + 32] = wi[:, c0 + h * 32:c0 + h * 32 + 32]
            return m
        sv = chunks([wcols(1536), wcols(1664), None, None])
        for d in range(2):
            for ch in range(2):
                m = np.zeros((128, 128), np.float32)
                for hh in range(2):
                    h = 2 * ch + hh
                    m[d * 16:(d + 1) * 16, hh * 64:hh * 64 + 32] = inp["gla_wa2"][l, d][:, h * 32:(h + 1) * 32]
                sv[:, d * 2 + ch, 256:384] = m
        slots.append(sv.reshape(128, 4096))
        for ch in range(2):
            slots.append(chunks([padqk(1280, ch), padqk(1408, ch), wcols(1792 + ch * 128), wcols(2048, 32)]).reshape(128, 4096))
            if l == 0:
                ada_slots(0, 6 + 2 * ch, 2)
            ada_slots(l + 1, 3 * ch, 3)
        slots.append(woslot(256))
        slots.append(chunks([wcols(256), wcols(384), None, None]).reshape(128, 4096))
        for ch in range(2):
            slots.append(chunks([wcols(0 + ch * 128), wcols(512 + ch * 128), wcols(768 + ch * 128),
                                 wcols(1024 + ch * 128)]).reshape(128, 4096))
            if l == 0 and ch == 0:
                ada_slots(0, 10, 2)
            ada_slots(l + 1, 6 + 3 * ch, 3)
        slots.append(woslot(0))
        w1 = inp["mlp_w1"][l]; w2 = inp["mlp_w2"][l]
        for g in range(8):
            slots.append(kslot(w1[:, g * 512:(g + 1) * 512]).reshape(128, 4096))
            slots.append(np.ascontiguousarray(
                w2[g * 512:(g + 1) * 512].reshape(4, 128, 1024).transpose(1, 0, 2)).reshape(128, 4096))
    return np.ascontiguousarray(np.stack(slots, 0))


def build_consts():
    pk = Packer()
    s = np.arange(128)[:, None]; t = np.arange(128)[None, :]
    same = (s // 32) == (t // 32)
    pk.add("mf", (same & (s <= t)).astype(np.float32))
    pk.add("mb", (same & (s >= t)).astype(np.float32))
    pk.add("mf128", (s <= t).astype(np.float32))
    pk.add("mb128", (s >= t).astype(np.float32))
    pk.add("ident", np.eye(128, dtype=np.float32))
    o64 = np.zeros((128, 128), np.float32); o64[:64, :64] = 1 / 64.; o64[64:, 64:] = 1 / 64.
    pk.add("o64", o64)
    pk.add("bm", ((np.arange(128)[:, None] // 32) == np.arange(4)[None, :]).astype(np.float32))
    mFB = np.ones((128, 1025), np.float32); mFB[:, 0::32] = 0
    pk.add("mFB", mFB)
    pk.add("one", np.ones((128, 1), np.float32))
    return pk


def build_params(inp, core):
    pk = Packer()
    samp = core < 4
    crow = inp["c"][core] if samp else inp["c_ctx"]
    pk.add("cvec", fm(crow))
    fl = np.zeros((128, 2), np.float32); fl[:, 0] = 1.0 if samp else 0.0; fl[:, 1] = 0.0 if samp else 1.0
    pk.add("flags", fl)
    pk.add("fng", fm(inp["final_norm_g"]))
    for l in range(L):
        pk.add("adab%d" % l, fm(inp["ada_b"][l]))
        pk.add("n1g%d" % l, fm(inp["norm1_g"][l]))
        pk.add("n2g%d" % l, fm(inp["norm2_g"][l]))
        pk.add("lbz%d" % l, fm(inp["hgrn_lb_logits"][l].reshape(-1)))
        pk.add("hgain%d" % l, fm(inp["hgrn_norm_g"][l]))
        pk.add("ggain%d" % l, fm(inp["gla_norm_g"][l]))
        b = np.zeros((128, 4), np.float32)
        for d in range(2):
            for ch in range(2):
                for hh in range(2):
                    h = 2 * ch + hh
                    b[hh * 64:hh * 64 + 32, d * 2 + ch] = inp["gla_ba2"][l, d, h * 32:(h + 1) * 32]
        pk.add("ba2%d" % l, b)
        cw = np.zeros((128, 16), np.float32)
        for d in range(2):
            for cc in range(2):
                for j in range(4):
                    cw[:, (d * 2 + cc) * 4 + j] = inp["rg_conv_w"][l, d, j, cc * 128:(cc + 1) * 128]
        pk.add("rgcw%d" % l, cw)
        pk.add("rgcb%d" % l, fm(inp["rg_conv_b"][l].reshape(-1)))
        pk.add("rgbr%d" % l, fm(inp["rg_b_r"][l].reshape(-1)))
        pk.add("rgbi%d" % l, fm(inp["rg_b_i"][l].reshape(-1)))
        pk.add("rglam%d" % l, fm(inp["rg_lambda"][l].reshape(-1)))
        sw = np.zeros((128, 6), np.float32)
        for cc in range(2):
            for j in range(3):
                sw[:, cc * 3 + j] = inp["sconv_w"][l, j, cc * 128:(cc + 1) * 128]
        pk.add("scw%d" % l, sw)
        if samp:
            pk.add("str%d" % l, fm(inp["state_rglru"][core, l].reshape(-1)))
        else:
            pk.add("str%d" % l, np.zeros((128, 4), np.float32))
    return pk


def build_st0(inp, core):
    st = np.zeros((128, 16, 64), np.float32)
    if core < 4:
        for m, key in enumerate(("state_hgrn", "state_gla")):
            s = inp[key][core]
            dk = s.shape[3]
            for l in range(L):
                for d in range(2):
                    for ch in range(2):
                        i = ((m * 2 + l) * 2 + d) * 2 + ch
                        for hh in range(2):
                            st[hh * 64:hh * 64 + dk, i, :] = s[l, d, 2 * ch + hh]
    return st.reshape(128, 1024)


def build_program(po, co, n_slots):
    nc = bass.Bass("TRN2", target_bir_lowering=False)
    xT = nc.dram_tensor("xT", [D, T], F32, kind="ExternalInput").ap()
    wst = nc.dram_tensor("wst", [n_slots, 128, 4096], F32, kind="ExternalInput").ap()
    prm = nc.dram_tensor("prm", [128, po.n], F32, kind="ExternalInput").ap()
    cst = nc.dram_tensor("cst", [128, co.n], F32, kind="ExternalInput").ap()
    st0 = nc.dram_tensor("st0", [128, 1024], F32, kind="ExternalInput").ap()
    onesd = nc.dram_tensor("onesd", [128, 128], F32, kind="ExternalInput").ap()
    yT = nc.dram_tensor("yT", [D, T], F32, kind="ExternalOutput").ap()
    osA = nc.dram_tensor("osA", [32, 128, 64], F32, kind="ExternalOutput").ap()
    osB = nc.dram_tensor("osB", [32, 128, 64], F32, kind="ExternalOutput").ap()
    osr = nc.dram_tensor("osr", [128, 32], F32, kind="ExternalOutput").ap()

    with ExitStack() as es:
        def sb(name, P, Fn, dt=F32):
            return TL(es.enter_context(nc.sbuf_tensor(name, [P, Fn], dt)), name, P, Fn)

        def psum(name):
            return TL(es.enter_context(nc.psum_tensor(name, [128, 512], F32)), name, 128, 512)

        p = Prog(nc, es)
        X = sb("X", 128, 8192); H = sb("H", 128, 8192)
        RING = sb("RING", 128, 8192, F32R)
        W = sb("W", 128, 12288)
        FR = sb("FR", 128, 4096, F32R)
        OM = FR.sub(0, 2048); SREP = FR.sub(2048, 1024); HID = FR.sub(0, 4096)
        PRM = sb("PRM", 128, po.n); CST = sb("CST", 128, co.n)
        ONES = sb("ONES", 128, 128, F32R)
        MOD = sb("MOD", 128, 48); GS = sb("GS", 128, 16)
        SM = sb("SM", 128, 64)
        OSR = sb("OSR", 128, 32)
        MODN = sb("MODN", 128, 96); ACH = sb("ACH", 128, 64)
        WA2T = sb("WA2T", 128, 512)
        PS = [psum("P%d" % i) for i in range(8)]
        PJ = [PS[0], PS[1]]; PMS = PS[2]; PSC = PS[3]; PAC = PS[4]; PDS = PS[5]; PTR = PS[6]; PX = PS[7]

        def wt(i):
            return W.sub(i * 1024, 1024)
        tQ, tKF, tLAF, tG, tO, tCUM, tKH, tTMP = [wt(i) for i in range(8)]
        DS = W.sub(8192, 2048); SRING = W.sub(10240, 1024); AUXC = W.sub(11264, 1024)
        KHT = sb("KHT", 128, 256, F32R); VBD = sb("VBD", 128, 1024, F32R)
        tQT = sb("QT", 128, 1024, F32R); tKT = sb("KT", 128, 1024, F32R); PT = sb("PTt", 128, 512, F32R)
        VPAD = sb("VPAD", 128, 4096, F32R); SRR = sb("SRR", 128, 1024, F32R)
        SQT = W.sub(8192, 1024)

        def prm_v(name, c0=0, n=1):
            o = po.off[name] + c0
            return PRM.v(f0=o, f1=o + n)

        def cst_v(name, c0=0, n=1, p0=0, p1=128):
            o = co.off[name] + c0
            return CST.v(p0=p0, p1=p1, f0=o, f1=o + n)

        flag_s = prm_v("flags", 0); flag_p = prm_v("flags", 1)

        def act(out, in_, func, bias=None, scale=None, extra_reads=()):
            kw = {}
            rd = [in_] + list(extra_reads)
            if bias is not None:
                if isinstance(bias, V):
                    kw["bias"] = bias.ap; rd.append(bias)
                else:
                    kw["bias"] = bias
            if scale is not None:
                if isinstance(scale, V):
                    kw["scale"] = scale.ap; rd.append(scale)
                else:
                    kw["scale"] = scale
            p.op("scalar", lambda E: E.activation(out=out.ap, in_=in_.ap, func=func, **kw), reads=rd, writes=[out])

        def tt(out, a, b, op, eng="vector"):
            p.op(eng, lambda E: E.tensor_tensor(out=out.ap, in0=a.ap, in1=b.ap, op=op), reads=[a, b], writes=[out])

        def ts(out, a, s1, op0, s2=None, op1=None, eng="vector"):
            rd = [a]
            a1 = s1.ap if isinstance(s1, V) else s1
            if isinstance(s1, V): rd.append(s1)
            kw = {"scalar2": None}
            if op1 is not None:
                kw["scalar2"] = s2.ap if isinstance(s2, V) else s2
                kw["op1"] = op1
                if isinstance(s2, V): rd.append(s2)
            p.op(eng, lambda E: E.tensor_scalar(out=out.ap, in0=a.ap, scalar1=a1, op0=op0, **kw), reads=rd, writes=[out])

        def stt(out, a, s, b, op0, op1):
            rd = [a, b]
            sa = s.ap if isinstance(s, V) else s
            if isinstance(s, V): rd.append(s)
            p.op("vector", lambda E: E.scalar_tensor_tensor(out=out.ap, in0=a.ap, scalar=sa, in1=b.ap, op0=op0, op1=op1),
                 reads=rd, writes=[out])

        def cp(out, in_, eng="vector"):
            if eng == "scalar":
                p.op("scalar", lambda E: E.copy(out.ap, in_.ap), reads=[in_], writes=[out])
            else:
                p.op(eng, lambda E: E.tensor_copy(out.ap, in_.ap), reads=[in_], writes=[out])

        def mm(out, lhsT, rhs, start=True, stop=True):
            p.op("tensor", lambda E: E.matmul(out.ap, lhsT.ap, rhs.ap, start=start, stop=stop),
                 reads=[lhsT, rhs], writes=[out])

        def memset(v, val, eng="gpsimd"):
            p.op(eng, lambda E: E.memset(v.ap, val), writes=[v])

        state = {"cur": -1, "issued": 0, "pj": 0, "rb": 0, "rb2": 0}

        def issue(i):
            if i < n_slots and state["issued"] <= i:
                rv = RING.v(f0=(i % 2) * 4096, f1=(i % 2) * 4096 + 4096)
                p.dma("gpsimd", rv.ap, wst[i], writes=[rv])
                state["issued"] = i + 1

        def prefetch2():
            issue(state["cur"] + 2)

        def next_slot():
            state["cur"] += 1
            i = state["cur"]
            issue(i)
            issue(i + 1)
            return RING.sub((i % 2) * 4096, 4096)

        def sl_k(sl, k, c0, n):
            return sl.v(f0=k * 512 + c0, f1=k * 512 + c0 + n)

        def Hk(k, t0, n):
            return H.v(f0=k * 1024 + t0, f1=k * 1024 + t0 + n).bc()

        def proj_chunk(sl, ci, evac):
            for half in range(2):
                P = PJ[state["pj"]]; state["pj"] ^= 1
                for k in range(8):
                    mm(P.v(), sl_k(sl, k, ci * 128, 128), Hk(k, half * 512, 512), start=(k == 0), stop=(k == 7))
                evac(P.v(), half)

        def proj_vtok(sl, ci0):
            for b in range(NB):
                P = PJ[state["pj"]]; state["pj"] ^= 1
                for k in range(8):
                    mm(P.v(f1=256), Hk(k, b * 128, 128), sl_k(sl, k, ci0 * 128, 256), start=(k == 0), stop=(k == 7))
                src = P.v(f1=256).r("p (a h e) -> p a h e", a=2, h=2)
                dst = VPAD.v(f0=b * 512, f1=b * 512 + 512)
                base = dst.ap
                d4 = bass.AP(tensor=base.tensor, offset=base.offset, ap=[base.ap[0], [256, 2], [192, 2], [1, 64]])
                p.op("scalar", lambda E, s=src, d4=d4: E.copy(d4, s.ap), reads=[src], writes=[dst])

        for c in range(8):
            p.dma("sync", X.v(f0=c * 1024, f1=(c + 1) * 1024).ap, xT[c * 128:(c + 1) * 128, :],
                  writes=[X.v(f0=c * 1024, f1=(c + 1) * 1024)])
        p.dma("sync", PRM.v().ap, prm, writes=[PRM.v()])
        p.dma("sync", CST.v().ap, cst, writes=[CST.v()])
        p.dma("gpsimd", ONES.v().ap, onesd, writes=[ONES.v()])
        memset(SRING.v(), 0.0)
        zsrc = cst_v("mFB", 0, 1024)
        for i in range(4):
            ts(VPAD.v(f0=i * 1024, f1=i * 1024 + 1024), zsrc, 0.0, ALU.mult)
        ts(SRR.v(), zsrc, 0.0, ALU.mult)
        memset(OSR.v(), 0.0)

        def make_srep():
            act(SM.v(f0=0, f1=8), prm_v("cvec", 0, 8), AF.Silu)
            srep3 = SREP.v().r("p (k m) -> p k m", k=8)
            sin3 = SM.v(f0=0, f1=8)
            sin_ap = sin3.ap.rearrange("p (k o) -> p k o", o=1).broadcast_to([128, 8, 128])
            p.op("vector", lambda E: E.tensor_copy(srep3.ap, sin_ap), reads=[sin3], writes=[SREP.v()])

        def rmsnorm_to_H(gs_v_fn, shift_fn):
            rms_stats()
            rms_apply(gs_v_fn, shift_fn)

        def rms_stats():
            sb_ = [PMS, PX]
            for half in range(2):
                for c in range(8):
                    act(H.v(f0=c * 1024 + half * 512, f1=c * 1024 + half * 512 + 512).bc(),
                        X.v(f0=c * 1024 + half * 512, f1=c * 1024 + half * 512 + 512), AF.Square)
                for k in range(8):
                    mm(sb_[half].v(), ONES.v(), Hk(k, half * 512, 512), start=(k == 0), stop=(k == 7))
                act(tTMP.v(f0=half * 512, f1=half * 512 + 512), sb_[half].v(), AF.Ln, bias=prm_eps, scale=1.0 / D)
                act(tTMP.v(f0=half * 512, f1=half * 512 + 512), tTMP.v(f0=half * 512, f1=half * 512 + 512), AF.Exp, scale=-0.5)

        def rms_apply(gs_v_fn, shift_fn):
            for c in range(8):
                xv = X.v(f0=c * 1024, f1=(c + 1) * 1024); hv = H.v(f0=c * 1024, f1=(c + 1) * 1024)
                if shift_fn is None:
                    ot = (tQ if c % 2 == 0 else tKF).v()
                    stt(ot, xv, gs_v_fn(c), tTMP.v(), ALU.mult, ALU.mult)
                    p.dma("sync", yT[c * 128:(c + 1) * 128, :], ot.ap, reads=[ot])
                else:
                    tmp = (tCUM if c % 2 == 0 else tKH).v()
                    stt(tmp, xv, gs_v_fn(c), tTMP.v(), ALU.mult, ALU.mult)
                    act(hv.bc(), tmp, AF.Identity, bias=shift_fn(c))

        memset(SM.v(f0=8, f1=9), 1e-6, eng="vector")
        prm_eps = SM.v(f0=8, f1=9)

        def hv2(t, h):
            return t.v(f0=h * 512, f1=h * 512 + 512)

        def gla_prep_part1(d, Kt, LAt, CS=32):
            NCK = T // CS
            NH = NCK // 2
            th = []
            if CS == 32:
                for h in range(2):
                    m = cst_v("mFB", (0 if d == 0 else 1) + h * 512, 512)
                    cv = hv2(tCUM, h); lv = hv2(LAt, h)
                    if d == 0:
                        th.append(lambda cv=cv, lv=lv, m=m: p.op("vector", lambda E: E.tensor_tensor_scan(
                            out=cv.ap, data0=m.ap, data1=lv.ap, initial=0.0, op0=ALU.mult, op1=ALU.add), reads=[m, lv], writes=[cv]))
                    else:
                        th.append(lambda cv=cv, lv=lv, m=m: p.op("vector", lambda E: E.tensor_tensor_scan(
                            out=cv.ap[:, ::-1], data0=m.ap[:, ::-1], data1=lv.ap[:, ::-1], initial=0.0,
                            op0=ALU.mult, op1=ALU.add), reads=[m, lv], writes=[cv]))
            else:
                onesv = cst_v("mf128", 127, 1)
                ones_b = onesv.ap.broadcast_to([128, CS])

                def scans(n0, n1):
                    for n in range(n0, n1):
                        cv = tCUM.v(f0=n * CS, f1=(n + 1) * CS); lv = LAt.v(f0=n * CS, f1=(n + 1) * CS)
                        if d == 0:
                            p.op("vector", lambda E, cv=cv, lv=lv: E.tensor_tensor_scan(out=cv.ap, data0=ones_b, data1=lv.ap, initial=0.0,
                                                                                    op0=ALU.mult, op1=ALU.add), reads=[onesv, lv], writes=[cv])
                        else:
                            p.op("vector", lambda E, cv=cv, lv=lv: E.tensor_tensor_scan(out=cv.ap[:, ::-1], data0=ones_b, data1=lv.ap[:, ::-1],
                                                                                    initial=0.0, op0=ALU.mult, op1=ALU.add),
                                 reads=[onesv, lv], writes=[cv])
                th.append(lambda: scans(0, NH))
                th.append(lambda: scans(NH, NCK))
            li = CS - 1 if d == 0 else 0
            for h in range(2):
                cvh = hv2(tCUM, h); tmh = hv2(tTMP, h)
                c3 = cvh.r("p (n j) -> p n j", j=CS)
                lastb = c3.ap[:, :, li:li + 1].broadcast_to([128, NH, CS])
                t3 = tmh.r("p (n j) -> p n j", j=CS)
                th.append(lambda c3=c3, lastb=lastb, t3=t3, cvh=cvh, tmh=tmh: p.op(
                    "vector", lambda E: E.tensor_tensor(out=t3.ap, in0=lastb, in1=c3.ap, op=ALU.subtract), reads=[cvh], writes=[tmh]))
            for h in range(2):
                th.append(lambda h=h: act(hv2(tTMP, h), hv2(tTMP, h), AF.Exp))
            late = [lambda h=h: tt(hv2(tKH, h), hv2(Kt, h), hv2(tTMP, h), ALU.mult) for h in range(2)]
            cumv = tCUM.v()
            c3a = cumv.r("p (n j) -> p n j", j=CS)
            last = c3a.w(c3a.ap[:, :, li:li + 1])
            ach = ACH.v(f0=d * 32, f1=d * 32 + NCK)
            th.append(lambda: p.op("scalar", lambda E: E.activation(out=ach.ap, in_=last.ap.rearrange("p n o -> p (n o)"), func=AF.Exp),
                                   reads=[cumv], writes=[ach]))
            return th, late, ach

        def gla_prep_part2(Kt):
            for h in range(2):
                act(hv2(tTMP, h), hv2(tCUM, h), AF.Exp)
            for h in range(2):
                tt(hv2(tQT, h), hv2(tQ, h), hv2(tTMP, h), ALU.mult)
            for h in range(2):
                act(hv2(tTMP, h), hv2(tCUM, h), AF.Exp, scale=-1.0)
            for h in range(2):
                tt(hv2(tKT, h), hv2(Kt, h), hv2(tTMP, h), ALU.mult)

        def gla_core(mix_id, l, ch, prep_hooks, gain_v, os_dram, CS=32):
            ident = cst_v("ident", 0, 128)
            prefetch2()
            for d in range(2):
                si = ((mix_id * 2 + l) * 2 + d) * 2 + ch
                dv = SRING.v(f0=0, f1=64)
                p.dma("sync", dv.ap, st0[:, si * 64:si * 64 + 64], writes=[dv])
                CPB = 128 // CS; SEGC = 256 // CS
                if d == 0:
                    th0, late0, ach = gla_prep_part1(0, tKF, tLAF, CS)
                    for f in list(prep_hooks[0]) + th0 + late0:
                        f()
                    th1, late1, ach1 = gla_prep_part1(1, tKF, tLAF, CS)
                    pending = list(prep_hooks[1]) + th1
                else:
                    for f in pending + late1:
                        f()
                    pending = []
                    ach = ach1
                blocks = list(range(NB)) if d == 0 else list(range(NB - 1, -1, -1))
                gla_prep_part2(tKF)
                def ds_front(b):
                    ptr = PTR.v(f0=(b % 2) * 128, f1=(b % 2) * 128 + 128)
                    p.op("tensor", lambda E, b=b, ptr=ptr: E.transpose(ptr.ap, tKH.v(f0=b * 128, f1=b * 128 + 128).ap, ident.ap),
                         reads=[tKH.v(f0=b * 128, f1=b * 128 + 128), ident], writes=[ptr])
                    cp(KHT.v(f0=(b % 2) * 128, f1=(b % 2) * 128 + 128), ptr, eng="scalar")
                    if CS == 128:
                        return
                    vsrc = VPAD.v(f0=b * 512 + ch * 256, f1=b * 512 + ch * 256 + 256).bc(F32)
                    base = vsrc.ap
                    vin = bass.AP(tensor=base.tensor, offset=base.offset, ap=[base.ap[0], [192, 2], [0, 4], [1, 64]])
                    bmv = cst_v("bm", 0, 4)
                    bmin = bass.AP(tensor=bmv.ap.tensor, offset=bmv.ap.offset, ap=[bmv.ap.ap[0], [0, 2], [1, 4], [0, 64]])
                    vb = VBD.v(f0=(b % 2) * 512, f1=(b % 2) * 512 + 512)
                    vout = vb.r("p (h n e) -> p h n e", h=2, n=4)
                    p.op("gpsimd", lambda E, vout=vout, vin=vin, bmin=bmin: E.tensor_tensor(out=vout.ap, in0=vin, in1=bmin, op=ALU.mult),
                         reads=[vsrc, bmv], writes=[vb])

                def ds_back(b):
                    vb = VBD.v(f0=(b % 2) * 512, f1=(b % 2) * 512 + 512)
                    pds = PDS if b % 2 == 0 else PMS
                    if CS == 128:
                        mm(pds.v(f1=256), KHT.v(f0=(b % 2) * 128, f1=(b % 2) * 128 + 128),
                           VPAD.v(f0=b * 512 + ch * 256, f1=b * 512 + ch * 256 + 256))
                        cp(DS.v(p0=0, p1=64, f0=b * 64, f1=b * 64 + 64), pds.v(p0=0, p1=64, f0=0, f1=64), eng="scalar")
                        cp(DS.v(p0=64, p1=128, f0=b * 64, f1=b * 64 + 64), pds.v(p0=64, p1=128, f0=192, f1=256), eng="scalar")
                        return
                    mm(pds.v(), KHT.v(f0=(b % 2) * 128, f1=(b % 2) * 128 + 128), vb)
                    cp(DS.v(p0=0, p1=64, f0=b * 256, f1=b * 256 + 256), pds.v(p0=0, p1=64, f0=0, f1=256), eng="scalar")
                    cp(DS.v(p0=64, p1=128, f0=b * 256, f1=b * 256 + 256), pds.v(p0=64, p1=128, f0=256, f1=512), eng="scalar")

                ds_front(blocks[0])
                def cast_state(sl_):
                    for hh in range(2):
                        cp(SRR.v(p0=hh * 64, p1=hh * 64 + 64, f0=sl_ * 128 + hh * 64, f1=sl_ * 128 + hh * 64 + 64),
                           SRING.v(p0=hh * 64, p1=hh * 64 + 64, f0=sl_ * 64, f1=sl_ * 64 + 64), eng="scalar")
                cast_state(0)
                mask = cst_v("mf" if CS == 32 else "mf128", 0, 128) if d == 0 else cst_v("mb" if CS == 32 else "mb128", 0, 128)
                SCB = [(PSC, PX), (PSC, PX)]
                ACB = [PAC, PJ[0], PJ[1]]
                st_ = {"slot": 0}

                def stage_scores(i):
                    b = blocks[i]
                    for hh in range(2):
                        mm(SCB[i % 2][hh].v(f1=128),
                           tKT.v(p0=hh * 64, p1=hh * 64 + 64, f0=b * 128, f1=b * 128 + 128),
                           tQT.v(p0=hh * 64, p1=hh * 64 + 64, f0=b * 128, f1=b * 128 + 128))
                    for hh in range(2):
                        tt(PT.v(f0=(i % 2) * 256 + hh * 128, f1=(i % 2) * 256 + hh * 128 + 128), SCB[i % 2][hh].v(f1=128), mask, ALU.mult)

                def stage_main(i):
                    b = blocks[i]
                    accb = ACB[i % 3]
                    acc = accb.v(f1=128)
                    for hh in range(2):
                        h = 2 * ch + hh
                        mm(acc, VPAD.v(f0=b * 512 + h * 128, f1=b * 512 + h * 128 + 128),
                           PT.v(f0=(i % 2) * 256 + hh * 128, f1=(i % 2) * 256 + hh * 128 + 128), start=(hh == 0), stop=False)
                    chunks = list(range(CPB * b, CPB * b + CPB)) if d == 0 else list(range(CPB * b + CPB - 1, CPB * b - 1, -1))
                    for ci, n in enumerate(chunks):
                        slot = st_["slot"]
                        mm(accb.v(f0=(n % CPB) * CS, f1=(n % CPB) * CS + CS),
                           SRR.v(f0=slot * 128, f1=slot * 128 + 128), tQT.v(f0=n * CS, f1=n * CS + CS),
                           start=False, stop=(ci == CPB - 1))
                        nslot = (slot + 1) % 8
                        stt(SRING.v(f0=nslot * 64, f1=nslot * 64 + 64), SRING.v(f0=slot * 64, f1=slot * 64 + 64),
                            V(ach.ap[:, n:n + 1], ach.name, 0, 128, ach.f0 + n, ach.f0 + n + 1),
                            DS.v(f0=n * 64, f1=n * 64 + 64), ALU.mult, ALU.add)
                        slot = nslot
                        st_["slot"] = slot
                        bnd = (n % SEGC == SEGC - 1) if d == 0 else (n % SEGC == 0)
                        if bnd:
                            seg = n // SEGC
                            oi = ((l * 2 + d) * 4 + seg) * 2 + ch
                            sv2 = SRING.v(f0=slot * 64, f1=slot * 64 + 64)
                            p.dma("sync", os_dram[oi], sv2.ap, reads=[sv2])
                            ts(sv2, sv2, flag_s, ALU.mult)
                        cast_state(slot)

                def stage_evac(i):
                    b = blocks[i]
                    acc = ACB[i % 3].v(f1=128)
                    ov = tO.v(f0=b * 128, f1=b * 128 + 128)
                    if d == 0:
                        cp(ov, acc, eng="scalar")
                    else:
                        tt(ov, ov, acc, ALU.add)

                for step in range(NB + 2):
                    if step < NB:
                        stage_scores(step)
                        if step + 1 < NB:
                            ds_front(blocks[step + 1])
                        ds_back(blocks[step])
                    if 1 <= step <= NB:
                        stage_main(step - 1)
                    if step >= 2:
                        stage_evac(step - 2)
                    if d == 0 and pending:
                        for _ in range(2 if len(pending) > NB + 1 - step else 1):
                            if pending:
                                pending.pop(0)()
            o64 = cst_v("o64", 0, 128)
            hb = [PMS, PX]
            for h in range(2):
                act(hv2(tTMP, h), hv2(tO, h), AF.Square)
            for h in range(2):
                mm(hb[h].v(), o64, hv2(tTMP, h))
                act(hv2(tCUM, h), hb[h].v(), AF.Ln, bias=prm_eps)
            for h in range(2):
                act(hv2(tCUM, h), hv2(tCUM, h), AF.Exp, scale=-0.5)
            for h in range(2):
                tt(hv2(tO, h), hv2(tO, h), hv2(tCUM, h), ALU.mult)
            for h in range(2):
                stt(OM.v(f0=ch * 1024 + h * 512, f1=ch * 1024 + h * 512 + 512).bc(), hv2(tO, h), gain_v, hv2(tG, h), ALU.mult, ALU.mult)

        def wout_partial(l):
            sl = next_slot()
            for fo in range(8):
                for half in range(2):
                    P = [PX, PSC, PAC, PDS][state["rb2"] % 4]; state["rb2"] += 1
                    for kk in range(2):
                        mm(P.v(), sl.v(f0=kk * 1024 + fo * 128, f1=kk * 1024 + fo * 128 + 128),
                           OM.v(f0=kk * 1024 + half * 512, f1=kk * 1024 + half * 512 + 512).bc(), start=(kk == 0), stop=(kk == 1))
                    xv = X.v(f0=fo * 1024 + half * 512, f1=fo * 1024 + half * 512 + 512)
                    stt(xv, P.v(), MOD.v(f0=16 + fo, f1=17 + fo), xv, ALU.mult, ALU.add)

        one11 = cst_v("one", 0, 1, p0=0, p1=1)

        def ada_slot(g, lay, stg=None):
            stg = DS if stg is None else stg
            sl = next_slot()
            for k in range(8):
                mm(PMS.v(), SREP.v(f0=k * 128, f1=k * 128 + 128), sl_k(sl, k, 0, 512), start=(k == 0), stop=(k == 7))
            cp(stg.v(p0=0, p1=1, f0=0, f1=512), PMS.v(p0=0, p1=1), eng="scalar")
            for j in range(4):
                mm(PTR.v(f0=256 + 4 * g + j, f1=257 + 4 * g + j), stg.v(p0=0, p1=1, f0=j * 128, f1=j * 128 + 128), one11)
            o = (lay % 2) * 48
            cp(MODN.v(f0=o + 4 * g, f1=o + 4 * g + 4), PTR.v(f0=256 + 4 * g, f1=260 + 4 * g))

        def mod_commit(lay, c0, c1):
            o = (lay % 2) * 48
            tt(MOD.v(f0=c0, f1=c1), MODN.v(f0=o + c0, f1=o + c1), prm_v("adab%d" % lay, c0, c1 - c0), ALU.add)

        for l in range(L):
            if l == 0:
                make_srep()
                rms_stats()
                for g in range(4):
                    ada_slot(g, 0)
                mod_commit(0, 0, 16)
            else:
                mod_commit(l, 0, 48)
                stt(GS.v(f0=8, f1=16), MOD.v(f0=32, f1=40), 1.0, prm_v("n2g%d" % l, 0, 8), ALU.add, ALU.mult)
            stt(GS.v(f0=0, f1=8), MOD.v(f0=8, f1=16), 1.0, prm_v("n1g%d" % l, 0, 8), ALU.add, ALU.mult)
            LB = SM.v(f0=48, f1=52); OML = SM.v(f0=52, f1=56)
            if l == 0:
                memset(LB, 0.0, eng="vector")
            else:
                tt(LB, prm_v("lbz1", 0, 4), prm_v("lbz0", 0, 4), ALU.subtract)
                act(LB, LB, AF.Sigmoid)
            ts(OML, LB, -1.0, ALU.mult, 1.0, ALU.add)
            C8 = SM.v(f0=56, f1=60)
            act(C8, prm_v("rglam%d" % l, 0, 4), AF.Exp, scale=-1.0)
            act(C8, C8, AF.Ln, bias=1.0)
            ts(C8, C8, -8.0, ALU.mult)
            NBA = SM.v(f0=60, f1=64)
            ts(NBA, prm_v("ba2%d" % l, 0, 4), -1.0, ALU.mult)

            if l > 0:
                rms_stats()
            rms_apply(lambda c: GS.v(f0=c, f1=c + 1), lambda c: MOD.v(f0=c, f1=c + 1))

            slD1 = next_slot()
            TB, TC, TV, U, Y = tQ, tKF, tLAF, tCUM, DS.sub(0, 1024)
            scw = lambda cc, j: prm_v("scw%d" % l, cc * 3 + j)
            WP = SM.v(f0=9, f1=11)

            def sconv(cc):
                ts(WP.w(WP.ap[:, 0:1]), scw(cc, 0), flag_p, ALU.mult)
                ts(WP.w(WP.ap[:, 1:2]), scw(cc, 2), flag_p, ALU.mult)
                tt(U.v(), TC.v(), TV.v(), ALU.mult)
                ts(Y.v(), U.v(), scw(cc, 1), ALU.mult)
                y3 = Y.v().r("p (s j) -> p s j", j=64); u3 = U.v().r("p (s j) -> p s j", j=64)
                w0v = scw(cc, 0); w2v = scw(cc, 2)
                p.op("vector", lambda E: E.scalar_tensor_tensor(out=y3.ap[:, :, 1:64], in0=u3.ap[:, :, 0:63], scalar=w0v.ap,
                                                                in1=y3.ap[:, :, 1:64], op0=ALU.mult, op1=ALU.add),
                     reads=[U.v(), Y.v(), w0v], writes=[Y.v()])
                p.op("vector", lambda E: E.scalar_tensor_tensor(out=y3.ap[:, :, 0:63], in0=u3.ap[:, :, 1:64], scalar=w2v.ap,
                                                                in1=y3.ap[:, :, 0:63], op0=ALU.mult, op1=ALU.add),
                     reads=[U.v(), Y.v(), w2v], writes=[Y.v()])
                y4 = Y.v().r("p (a b j) -> p a b j", a=4, b=4); u4 = U.v().r("p (a b j) -> p a b j", a=4, b=4)
                p.op("vector", lambda E: E.scalar_tensor_tensor(out=y4.ap[:, :, 1:4, 0:1], in0=u4.ap[:, :, 0:3, 63:64], scalar=WP.ap[:, 0:1],
                                                                in1=y4.ap[:, :, 1:4, 0:1], op0=ALU.mult, op1=ALU.add),
                     reads=[U.v(), Y.v(), WP], writes=[Y.v()])
                p.op("vector", lambda E: E.scalar_tensor_tensor(out=y4.ap[:, :, 0:3, 63:64], in0=u4.ap[:, :, 1:4, 0:1], scalar=WP.ap[:, 1:2],
                                                                in1=y4.ap[:, :, 0:3, 63:64], op0=ALU.mult, op1=ALU.add),
                     reads=[U.v(), Y.v(), WP], writes=[Y.v()])
                tt(OM.v(f0=cc * 1024, f1=cc * 1024 + 1024).bc(), TB.v(), Y.v(), ALU.mult)

            def ev_copy(tile):
                return lambda pv, half: cp(tile.v(f0=half * 512, f1=half * 512 + 512), pv, eng="scalar")

            proj_chunk(slD1, 0, ev_copy(TB)); proj_chunk(slD1, 1, ev_copy(TC)); proj_chunk(slD1, 2, ev_copy(TV))
            sconv(0)
            proj_chunk(slD1, 3, ev_copy(TB))
            slD2 = next_slot()
            for i in range(8):
                cp(AUXC.v(f0=i * 128, f1=i * 128 + 128), sl_k(slD2, i, 256, 128).bc(F32))
            proj_chunk(slD2, 0, ev_copy(TC)); proj_chunk(slD2, 1, ev_copy(TV))
            sconv(1)
            if l == 0:
                for g in (4, 5):
                    ada_slot(g, 0, tKH)
                mod_commit(0, 16, 24)
            wout_partial(l)

            slC = next_slot()
            Ux = [tQ, tKF]; GG = [tLAF, tG]
            for cc in range(2):
                proj_chunk(slC, cc, ev_copy(Ux[cc]))
            for cc in range(2):
                proj_chunk(slC, 2 + cc, lambda pv, half, cc=cc: act(GG[cc].v(f0=half * 512, f1=half * 512 + 512), pv, AF.Gelu_apprx_tanh))
            XC, R, IG, HS = tCUM, DS.sub(0, 1024), DS.sub(1024, 1024), tKH
            GB = [PMS, PX, PSC, PAC]
            prefetch2()

            def hv_(t, h):
                return t.v(f0=h * 512, f1=h * 512 + 512)

            for cc in range(2):
                u = Ux[cc]
                for d in range(2):
                    di = d * 2 + cc
                    cw = [prm_v("rgcw%d" % l, di * 4 + j) for j in range(4)]
                    cb = prm_v("rgcb%d" % l, di)
                    c8 = SM.v(f0=56 + di, f1=57 + di)
                    WS = SM.v(f0=11, f1=15)
                    for j in range(4):
                        ts(WS.w(WS.ap[:, j:j + 1]), cw[j], flag_s, ALU.mult)
                    HSd = tO if d == 0 else HS
                    horder = [0, 1] if d == 0 else [1, 0]
                    x3 = XC.v().r("p (a j) -> p a j", j=256); u3 = u.v().r("p (a j) -> p a j", j=256)

                    def conv(h):
                        xh = hv_(XC, h)
                        ts(xh, hv_(u, h), cw[3], ALU.mult, cb, ALU.add)
                        a0, a1 = 2 * h, 2 * h + 2
                        for s in range(1, 4):
                            if d == 0:
                                o1, i1 = x3.ap[:, a0:a1, s:256], u3.ap[:, a0:a1, 0:256 - s]
                                c0 = max(a0, 1)
                                o2, i2 = x3.ap[:, c0:a1, 0:s], u3.ap[:, c0 - 1:a1 - 1, 256 - s:256]
                            else:
                                o1, i1 = x3.ap[:, a0:a1, 0:256 - s], u3.ap[:, a0:a1, s:256]
                                c1 = min(a1, 3)
                                o2, i2 = x3.ap[:, a0:c1, 256 - s:256], u3.ap[:, a0 + 1:c1 + 1, 0:s]
                            cws = cw[3 - s]
                            p.op("vector", lambda E, o1=o1, i1=i1, cws=cws: E.scalar_tensor_tensor(out=o1, in0=i1, scalar=cws.ap, in1=o1,
                                                                                      op0=ALU.mult, op1=ALU.add),
                                 reads=[u.v(), xh, cws], writes=[xh])
                            p.op("vector", lambda E, o2=o2, i2=i2, s=s: E.scalar_tensor_tensor(out=o2, in0=i2, scalar=WS.ap[:, 3 - s:4 - s], in1=o2,
                                                                                      op0=ALU.mult, op1=ALU.add),
                                 reads=[u.v(), xh, WS], writes=[xh])

                    def gates(h):
                        for wh, (dst, bname) in enumerate(((R, "rgbr%d" % l), (IG, "rgbi%d" % l))):
                            wm = AUXC.v(f0=(di * 2 + wh) * 128, f1=(di * 2 + wh) * 128 + 128)
                            gb = GB[h * 2 + wh]
                            mm(gb.v(), wm, hv_(XC, h))
                            act(hv_(dst, h), gb.v(), AF.Sigmoid, bias=prm_v(bname, di))

                    def scans(h):
                        segs = [2 * h, 2 * h + 1] if d == 0 else [2 * h + 1, 2 * h]
                        for sg in segs:
                            first = (sg == 0) if d == 0 else (sg == 3)
                            if first:
                                init = prm_v("str%d" % l, di)
                            else:
                                prev = HSd.v(f0=sg * 256 - 1, f1=sg * 256) if d == 0 else HSd.v(f0=sg * 256 + 256, f1=sg * 256 + 257)
                                init = ACH.v(f0=sg, f1=sg + 1)
                                ts(init, prev, flag_s, ALU.mult)
                            hv = HSd.v(f0=sg * 256, f1=sg * 256 + 256); av = R.v(f0=sg * 256, f1=sg * 256 + 256); bv = IG.v(f0=sg * 256, f1=sg * 256 + 256)
                            if d == 0:
                                p.op("vector", lambda E, hv=hv, av=av, bv=bv, init=init: E.tensor_tensor_scan(
                                    out=hv.ap, data0=av.ap, data1=bv.ap, initial=init.ap, op0=ALU.mult, op1=ALU.add),
                                    reads=[av, bv, init], writes=[hv])
                            else:
                                p.op("vector", lambda E, hv=hv, av=av, bv=bv, init=init: E.tensor_tensor_scan(
                                    out=hv.ap[:, ::-1], data0=av.ap[:, ::-1], data1=bv.ap[:, ::-1], initial=init.ap, op0=ALU.mult, op1=ALU.add),
                                    reads=[av, bv, init], writes=[hv])

                    for h in horder:
                        conv(h)
                    for h in horder:
                        gates(h)
                    for h in horder:
                        act(hv_(R, h), hv_(R, h), AF.Exp, scale=c8)
                    for h in horder:
                        tt(hv_(tTMP, h), hv_(R, h), hv_(R, h), ALU.mult)
                    for h in horder:
                        act(hv_(tTMP, h), hv_(tTMP, h), AF.Ln, bias=1.0, scale=-1.0)
                    for h in horder:
                        act(hv_(tTMP, h), hv_(tTMP, h), AF.Exp, scale=0.5)
                    for h in horder:
                        tt(hv_(IG, h), hv_(IG, h), hv_(tTMP, h), ALU.mult)
                    for h in horder:
                        tt(hv_(IG, h), hv_(IG, h), hv_(XC, h), ALU.mult)
                    for h in horder:
                        scans(h)
                        if d == 1:
                            tt(hv_(tO, h), hv_(tO, h), hv_(HS, h), ALU.add)
                    oc = ((l * 2 + d) * 2 + cc) * 4
                    h4 = HSd.v().r("p (a j) -> p a j", j=256)
                    fin = h4.w(h4.ap[:, :, 255:256] if d == 0 else h4.ap[:, :, 0:1])
                    o4 = OSR.v(f0=oc, f1=oc + 4)
                    p.op("vector", lambda E, o4=o4, fin=fin: E.tensor_copy(o4.ap.rearrange("p (a o) -> p a o", o=1), fin.ap),
                         reads=[HSd.v()], writes=[o4])
                tt(OM.v(f0=cc * 1024, f1=cc * 1024 + 1024).bc(), tO.v(), GG[cc].v(), ALU.mult)
            wout_partial(l)

            slBv = next_slot()
            proj_vtok(slBv, 0)
            for i in range(4):
                cp(WA2T.v(f0=i * 128, f1=i * 128 + 128), sl_k(slBv, i, 256, 128).bc(F32))
            sc = 32 ** -0.5
            for pair in range(2):
                slB = next_slot()
                proj_chunk(slB, 0, lambda pv, half: act(tQ.v(f0=half * 512, f1=half * 512 + 512), pv, AF.Identity, scale=sc))
                proj_chunk(slB, 1, ev_copy(tKF))
                proj_chunk(slB, 2, lambda pv, half: act(tG.v(f0=half * 512, f1=half * 512 + 512), pv, AF.Silu))
                proj_chunk(slB, 3, ev_copy(AUXC))

                def gla_hook(d, pair=pair):
                    i = d * 2 + pair
                    th = []
                    for half in range(2):
                        def f(half=half, i=i):
                            mm(PMS.v(), WA2T.v(f0=i * 128, f1=i * 128 + 128), AUXC.v(f0=half * 512, f1=half * 512 + 512))
                            act(tLAF.v(f0=half * 512, f1=half * 512 + 512), PMS.v(), AF.Exp, bias=SM.v(f0=60 + i, f1=61 + i), scale=-1.0)
                        th.append(f)
                    for h in range(2):
                        th.append(lambda h=h: act(hv2(tLAF, h), hv2(tLAF, h), AF.Ln, bias=1.0))
                    for h in range(2):
                        th.append(lambda h=h: ts(hv2(tLAF, h), hv2(tLAF, h), -1.0 / 16.0, ALU.mult))
                    return th
                gla_core(1, l, pair, [gla_hook(0), gla_hook(1)], prm_v("ggain%d" % l, pair), osB, CS=128)
                if l == 0:
                    for g in (6 + 2 * pair, 7 + 2 * pair):
                        ada_slot(g, 0)
                if l + 1 < L:
                    for g in range(3 * pair, 3 * pair + 3):
                        ada_slot(g, l + 1)
            wout_partial(l)

            slAv = next_slot()
            proj_vtok(slAv, 0)
            for pair in range(2):
                slA = next_slot()
                proj_chunk(slA, 0, ev_copy(tQ))
                proj_chunk(slA, 1, lambda pv, half: act(tLAF.v(f0=half * 512, f1=half * 512 + 512), pv, AF.Sigmoid))
                proj_chunk(slA, 2, lambda pv, half: act(AUXC.v(f0=half * 512, f1=half * 512 + 512), pv, AF.Sigmoid))
                proj_chunk(slA, 3, lambda pv, half: act(tG.v(f0=half * 512, f1=half * 512 + 512), pv, AF.Silu))

                def hg_hook(d, pair=pair):
                    i = d * 2 + pair
                    src = tLAF if d == 0 else AUXC
                    th = []
                    for h in range(2):
                        th.append(lambda h=h: ts(hv2(tLAF, h), hv2(src, h), SM.v(f0=52 + i, f1=53 + i), ALU.mult, SM.v(f0=48 + i, f1=49 + i), ALU.add))
                    for h in range(2):
                        th.append(lambda h=h: ts(hv2(tKF, h), hv2(tLAF, h), -1.0, ALU.mult, 1.0, ALU.add))
                    for h in range(2):
                        th.append(lambda h=h: ts(hv2(tLAF, h), hv2(tLAF, h), 1e-20, ALU.max))
                    for h in range(2):
                        th.append(lambda h=h: act(hv2(tLAF, h), hv2(tLAF, h), AF.Ln))
                    return th
                gla_core(0, l, pair, [hg_hook(0), hg_hook(1)], prm_v("hgain%d" % l, pair), osA)
                if l == 0 and pair == 0:
                    for g in (10, 11):
                        ada_slot(g, 0)
                    mod_commit(0, 24, 48)
                    stt(GS.v(f0=8, f1=16), MOD.v(f0=32, f1=40), 1.0, prm_v("n2g%d" % l, 0, 8), ALU.add, ALU.mult)
                if l + 1 < L:
                    for g in range(6 + 3 * pair, 6 + 3 * pair + 3):
                        ada_slot(g, l + 1)
            wout_partial(l)

            rmsnorm_to_H(lambda c: GS.v(f0=8 + c, f1=9 + c), lambda c: MOD.v(f0=24 + c, f1=25 + c))
            W1B = [PJ[0], PJ[1], PTR, PMS]
            W2B = [PX, PSC, PAC, PDS]
            for g in range(8):
                sl = next_slot()
                for jj in range(4):
                    for half in range(2):
                        P = W1B[state["rb"] % 4]; state["rb"] += 1
                        for k in range(8):
                            mm(P.v(), sl_k(sl, k, jj * 128, 128), Hk(k, half * 512, 512), start=(k == 0), stop=(k == 7))
                        hv = HID.v(f0=jj * 1024 + half * 512, f1=jj * 1024 + half * 512 + 512)
                        sq = SQT.v(f0=(state["rb"] % 2) * 512, f1=(state["rb"] % 2) * 512 + 512)
                        act(sq, P.v(), AF.Square)
                        stt(hv, P.v(), 0.0, sq, ALU.is_gt, ALU.mult)
                sl = next_slot()
                for fo in range(8):
                    for half in range(2):
                        P = W2B[state["rb2"] % 4]; state["rb2"] += 1
                        for j in range(4):
                            mm(P.v(), sl.v(f0=j * 1024 + fo * 128, f1=j * 1024 + fo * 128 + 128),
                               HID.v(f0=j * 1024 + half * 512, f1=j * 1024 + half * 512 + 512), start=(j == 0), stop=(j == 3))
                        xv = X.v(f0=fo * 1024 + half * 512, f1=fo * 1024 + half * 512 + 512)
                        stt(xv, P.v(), MOD.v(f0=40 + fo, f1=41 + fo), xv, ALU.mult, ALU.add)

        rmsnorm_to_H(lambda c: prm_v("fng", c), None)
        p.dma("sync", osr, OSR.v().ap, reads=[OSR.v()])
        p.finish("sync")
        with nc.Block() as block:
            p.emit(block)
    return nc


def kernel(**inp):
    inp = {k: np.asarray(v) for k, v in inp.items()}
    wst = build_stream(inp)
    co = build_consts()
    cst = co.get()
    pos = [build_params(inp, c) for c in range(8)]
    nc = build_program(pos[0], co, wst.shape[0])
    ones = np.ones((128, 128), np.float32)
    in_maps = []
    for c in range(8):
        if c < 4:
            x = inp["x_sample"][c]
        else:
            x = inp["x_prompt"][4 * (c - 4):4 * (c - 4) + 4].reshape(1024, 1024)
        in_maps.append({"xT": np.ascontiguousarray(x.T), "wst": wst, "prm": pos[c].get(), "cst": cst,
                        "st0": build_st0(inp, c), "onesd": ones})
    res = run_bass_kernel_spmd(nc, in_maps, core_ids=list(range(8)))
    R = res.results
    y_sample = np.stack([R[c]["yT"].T for c in range(4)], 0).astype(np.float32)
    y_prompt = np.concatenate([R[c]["yT"].T.reshape(4, 256, 1024) for c in range(4, 8)], 0).astype(np.float32)
    nh = np.zeros((16, L, 2, 4, 64, 64), np.float32)
    ng = np.zeros((16, L, 2, 4, 32, 64), np.float32)
    nr = np.zeros((16, L, 2, 256), np.float32)
    for c in range(4, 8):
        oa, ob, orr = R[c]["osA"], R[c]["osB"], R[c]["osr"]
        for seg in range(4):
            bi = 4 * (c - 4) + seg
            for l in range(L):
                for d in range(2):
                    for ch in range(2):
                        oi = ((l * 2 + d) * 4 + seg) * 2 + ch
                        for hh in range(2):
                            nh[bi, l, d, 2 * ch + hh] = oa[oi, hh * 64:hh * 64 + 64, :]
                            ng[bi, l, d, 2 * ch + hh] = ob[oi, hh * 64:hh * 64 + 32, :]
                    for cc in range(2):
                        oc = ((l * 2 + d) * 2 + cc) * 4 + seg
                        nr[bi, l, d, cc * 128:(cc + 1) * 128] = orr[:, oc]
    return (y_prompt, y_sample, nh, ng, nr)
```
